# Optimizing a Trainium2 kernel written in Bass

```python
import jax, jax.numpy as jnp
from jax import lax
import numpy as np

D_MODEL = 1024
BATCH = 4
SEQ = 4096
DEPTH = 4
DEC_BATCH = 128
DEC_SEQ = 8
PAST_LEN = 8192
PAGE_SIZE = 128

N_MIXERS = 3
N_ATTN = (DEPTH + 2) // 3
N_RGLRU = (DEPTH + 1) // 3
N_SCONV = DEPTH // 3

HEAD_DIM = 64
N_HEADS = D_MODEL // HEAD_DIM
N_KV_HEADS = max(1, N_HEADS // 8)
GQA_GROUP = N_HEADS // N_KV_HEADS
Q_DIM = N_HEADS * HEAD_DIM
KV_DIM = N_KV_HEADS * HEAD_DIM
WINDOW = 128
ROPE_THETA = 10000.0
NEG_INF = -1e30

D_RNN = D_MODEL
RG_BLOCKS = 4
RG_BLOCK_W = D_RNN // RG_BLOCKS
RG_CONV_W = 4
RG_C = 8.0

D_SCONV = D_MODEL
SCONV_W = 3

D_FF = -(-8 * D_MODEL // (3 * 256)) * 256

EPS = 1e-6

kernel_name = "hybrid_swa_rglru_shortconv_decoder_step"


def rmsnorm(x, g):
    xf = x.astype(jnp.float32)
    y = xf * lax.rsqrt(jnp.mean(xf * xf, axis=-1, keepdims=True) + EPS)
    return (y * g.astype(jnp.float32)).astype(x.dtype)


def rope(x, pos):
    half = HEAD_DIM // 2
    inv = ROPE_THETA ** (-jnp.arange(half, dtype=jnp.float32) / half)
    ang = pos.astype(jnp.float32)[:, None] * inv[None, :]
    cos = jnp.cos(ang)[None, :, None, :]
    sin = jnp.sin(ang)[None, :, None, :]
    xf = x.astype(jnp.float32)
    x1, x2 = xf[..., :half], xf[..., half:]
    out = jnp.concatenate([x1 * cos - x2 * sin, x2 * cos + x1 * sin], axis=-1)
    return out.astype(x.dtype)


def swa_project(h, w_qkv, b_qkv, pos):
    B, T, _ = h.shape
    qkv = h @ w_qkv + b_qkv
    q = qkv[..., :Q_DIM].reshape(B, T, N_HEADS, HEAD_DIM)
    k = qkv[..., Q_DIM:Q_DIM + KV_DIM].reshape(B, T, N_KV_HEADS, HEAD_DIM)
    v = qkv[..., Q_DIM + KV_DIM:].reshape(B, T, N_KV_HEADS, HEAD_DIM)
    q = rope(q, pos).reshape(B, T, N_KV_HEADS, GQA_GROUP, HEAD_DIM)
    k = rope(k, pos)
    return q, k, v


def attn_core(q, k, v, mask, sinks):
    s = jnp.einsum('...qkgd,...skd->...kgqs', q.astype(jnp.float32), k.astype(jnp.float32)) * (HEAD_DIM ** -0.5)
    s = jnp.where(mask, s, NEG_INF)
    sink = jnp.broadcast_to(sinks.astype(jnp.float32).reshape(N_KV_HEADS, GQA_GROUP, 1, 1), s.shape[:-1] + (1,))
    p = jax.nn.softmax(jnp.concatenate([s, sink], axis=-1), axis=-1)[..., :-1]
    return jnp.einsum('...kgqs,...skd->...qkgd', p.astype(v.dtype), v)


def swa_prompt(h, w_qkv, b_qkv, w_o, b_o, sinks):
    B, T, _ = h.shape
    pos = jnp.arange(T, dtype=jnp.int32)
    q, k, v = swa_project(h, w_qkv, b_qkv, pos)
    nb = T // WINDOW
    qb = q.reshape(B, nb, WINDOW, N_KV_HEADS, GQA_GROUP, HEAD_DIM)
    kb = k.reshape(B, nb, WINDOW, N_KV_HEADS, HEAD_DIM)
    vb = v.reshape(B, nb, WINDOW, N_KV_HEADS, HEAD_DIM)
    kk = jnp.concatenate([jnp.concatenate([jnp.zeros_like(kb[:, :1]), kb[:, :-1]], axis=1), kb], axis=2)
    vv = jnp.concatenate([jnp.concatenate([jnp.zeros_like(vb[:, :1]), vb[:, :-1]], axis=1), vb], axis=2)
    i = jnp.arange(WINDOW)[:, None]
    c = jnp.arange(2 * WINDOW)[None, :]
    diff = i + WINDOW - c
    blk = jnp.arange(nb)[:, None, None]
    mask = (diff >= 0) & (diff < WINDOW) & (blk * WINDOW + c - WINDOW >= 0)
    o = attn_core(qb, kk, vv, mask[:, None, None], sinks)
    out = o.reshape(B, T, Q_DIM) @ w_o + b_o
    return out, k[:, T - WINDOW:], v[:, T - WINDOW:]


def swa_sample(h, ck, cv, w_qkv, b_qkv, w_o, b_o, sinks):
    B, T, _ = h.shape
    pos = PAST_LEN + jnp.arange(T, dtype=jnp.int32)
    q, k, v = swa_project(h, w_qkv, b_qkv, pos)
    kk = jnp.concatenate([ck.astype(k.dtype), k], axis=1)
    vv = jnp.concatenate([cv.astype(v.dtype), v], axis=1)
    i = jnp.arange(T)[:, None]
    c = jnp.arange(WINDOW + T)[None, :]
    diff = i + WINDOW - c
    mask = (diff >= 0) & (diff < WINDOW)
    o = attn_core(q, kk, vv, mask, sinks)
    out = o.reshape(B, T, Q_DIM) @ w_o + b_o
    return out, kk[:, T:], vv[:, T:]


def causal_dwconv(u, buf, w):
    T = u.shape[1]
    kw = w.shape[0]
    ext = jnp.concatenate([buf.astype(u.dtype), u], axis=1)
    y = ext[:, 0:T] * w[0]
    for j in range(1, kw):
        y = y + ext[:, j:j + T] * w[j]
    return y, ext[:, -(kw - 1):]


def linear_recurrence(a, b, h0):
    def combine(left, right):
        a1, b1 = left
        a2, b2 = right
        return a1 * a2, a2 * b1 + b2
    a_cum, b_cum = lax.associative_scan(combine, (a, b), axis=1)
    return a_cum * h0[:, None] + b_cum


def rglru_mixer(h, h0, conv_buf, w_gate, w_in, conv_w, conv_b, wa, ba, wx, bx, lam, w_out):
    B, T, _ = h.shape
    gate = jax.nn.gelu(h @ w_gate)
    u, new_buf = causal_dwconv(h @ w_in, conv_buf, conv_w)
    u = u + conv_b
    ub = u.reshape(B, T, RG_BLOCKS, RG_BLOCK_W)
    r = jax.nn.sigmoid(jnp.einsum('btnd,nde->btne', ub, wa).reshape(B, T, D_RNN) + ba)
    ig = jax.nn.sigmoid(jnp.einsum('btnd,nde->btne', ub, wx).reshape(B, T, D_RNN) + bx)
    log_a = RG_C * r.astype(jnp.float32) * jax.nn.log_sigmoid(lam.astype(jnp.float32))
    a = jnp.exp(log_a)
    mult = jnp.sqrt(-jnp.expm1(2.0 * log_a))
    hs = linear_recurrence(a, mult * (ig * u).astype(jnp.float32), h0.astype(jnp.float32))
    y = (hs.astype(h.dtype) * gate) @ w_out
    return y, hs[:, -1].astype(h0.dtype), new_buf


def sconv_mixer(h, buf, w_in, conv_w, w_out):
    bcx = h @ w_in
    bg = bcx[..., :D_SCONV]
    cg = bcx[..., D_SCONV:2 * D_SCONV]
    xv = bcx[..., 2 * D_SCONV:]
    y, new_buf = causal_dwconv(cg * xv, buf, conv_w)
    return (bg * y) @ w_out, new_buf


def swiglu(h, wg, wu, wd):
    return (jax.nn.silu(h @ wg) * (h @ wu)) @ wd


def setup_inputs(seed: int = 0) -> dict:
    key = jax.random.key(seed)
    keys = iter(jax.random.split(key, 64))

    def nrm(shape, scale):
        return scale * jax.random.normal(next(keys), shape, jnp.float32)

    x_prompt = nrm((BATCH, SEQ, D_MODEL), 1.0)
    x_sample = nrm((DEC_BATCH, DEC_SEQ, D_MODEL), 1.0)
    cache_k = nrm((N_ATTN, DEC_BATCH, WINDOW, N_KV_HEADS, HEAD_DIM), 1.0)
    cache_v = nrm((N_ATTN, DEC_BATCH, WINDOW, N_KV_HEADS, HEAD_DIM), 1.0)
    state_rglru_h = nrm((N_RGLRU, DEC_BATCH, D_RNN), 0.5)
    state_rglru_conv = nrm((N_RGLRU, DEC_BATCH, RG_CONV_W - 1, D_RNN), 1.0)
    state_shortconv = nrm((N_SCONV, DEC_BATCH, SCONV_W - 1, D_SCONV), 1.0)
    norm_mixer = 1.0 + nrm((DEPTH, D_MODEL), 0.02)
    norm_ffn = 1.0 + nrm((DEPTH, D_MODEL), 0.02)
    norm_final = 1.0 + nrm((D_MODEL,), 0.02)
    attn_w_qkv = nrm((N_ATTN, D_MODEL, Q_DIM + 2 * KV_DIM), D_MODEL ** -0.5)
    attn_b_qkv = nrm((N_ATTN, Q_DIM + 2 * KV_DIM), 0.02)
    attn_w_o = nrm((N_ATTN, Q_DIM, D_MODEL), Q_DIM ** -0.5)
    attn_b_o = nrm((N_ATTN, D_MODEL), 0.02)
    attn_sinks = nrm((N_ATTN, N_HEADS), 0.5)
    rglru_w_gate = nrm((N_RGLRU, D_MODEL, D_RNN), D_MODEL ** -0.5)
    rglru_w_in = nrm((N_RGLRU, D_MODEL, D_RNN), D_MODEL ** -0.5)
    rglru_conv_w = nrm((N_RGLRU, RG_CONV_W, D_RNN), RG_CONV_W ** -0.5)
    rglru_conv_b = nrm((N_RGLRU, D_RNN), 0.02)
    rglru_wa = nrm((N_RGLRU, RG_BLOCKS, RG_BLOCK_W, RG_BLOCK_W), RG_BLOCK_W ** -0.5)
    rglru_ba = nrm((N_RGLRU, D_RNN), 0.02)
    rglru_wx = nrm((N_RGLRU, RG_BLOCKS, RG_BLOCK_W, RG_BLOCK_W), RG_BLOCK_W ** -0.5)
    rglru_bx = nrm((N_RGLRU, D_RNN), 0.02)
    a_c = jax.random.uniform(next(keys), (N_RGLRU, D_RNN), jnp.float32, 0.9, 0.999)
    sig = a_c ** (1.0 / RG_C)
    rglru_lambda = jnp.log(sig) - jnp.log1p(-sig)
    rglru_w_out = nrm((N_RGLRU, D_RNN, D_MODEL), D_RNN ** -0.5)
    sconv_w_in = nrm((N_SCONV, D_MODEL, 3 * D_SCONV), D_MODEL ** -0.5)
    sconv_conv_w = nrm((N_SCONV, SCONV_W, D_SCONV), SCONV_W ** -0.5)
    sconv_w_out = nrm((N_SCONV, D_SCONV, D_MODEL), D_SCONV ** -0.5)
    ffn_w_gate = nrm((DEPTH, D_MODEL, D_FF), D_MODEL ** -0.5)
    ffn_w_up = nrm((DEPTH, D_MODEL, D_FF), D_MODEL ** -0.5)
    ffn_w_down = nrm((DEPTH, D_FF, D_MODEL), D_FF ** -0.5)
    return {
        "x_prompt": x_prompt, "x_sample": x_sample,
        "cache_k": cache_k, "cache_v": cache_v,
        "state_rglru_h": state_rglru_h, "state_rglru_conv": state_rglru_conv,
        "state_shortconv": state_shortconv,
        "norm_mixer": norm_mixer, "norm_ffn": norm_ffn, "norm_final": norm_final,
        "attn_w_qkv": attn_w_qkv, "attn_b_qkv": attn_b_qkv, "attn_w_o": attn_w_o,
        "attn_b_o": attn_b_o, "attn_sinks": attn_sinks,
        "rglru_w_gate": rglru_w_gate, "rglru_w_in": rglru_w_in, "rglru_conv_w": rglru_conv_w,
        "rglru_conv_b": rglru_conv_b, "rglru_wa": rglru_wa, "rglru_ba": rglru_ba,
        "rglru_wx": rglru_wx, "rglru_bx": rglru_bx, "rglru_lambda": rglru_lambda,
        "rglru_w_out": rglru_w_out,
        "sconv_w_in": sconv_w_in, "sconv_conv_w": sconv_conv_w, "sconv_w_out": sconv_w_out,
        "ffn_w_gate": ffn_w_gate, "ffn_w_up": ffn_w_up, "ffn_w_down": ffn_w_down,
    }


def reference(x_prompt, x_sample, cache_k, cache_v, state_rglru_h, state_rglru_conv, state_shortconv,
              norm_mixer, norm_ffn, norm_final,
              attn_w_qkv, attn_b_qkv, attn_w_o, attn_b_o, attn_sinks,
              rglru_w_gate, rglru_w_in, rglru_conv_w, rglru_conv_b, rglru_wa, rglru_ba,
              rglru_wx, rglru_bx, rglru_lambda, rglru_w_out,
              sconv_w_in, sconv_conv_w, sconv_w_out,
              ffn_w_gate, ffn_w_up, ffn_w_down):
    xp, xs = x_prompt, x_sample
    Bp = xp.shape[0]
    kp_l, vp_l, ks_l, vs_l = [], [], [], []
    hp_l, hs_l, rcp_l, rcs_l = [], [], [], []
    scp_l, scs_l = [], []
    for i in range(DEPTH):
        kind = i % N_MIXERS
        j = i // N_MIXERS
        hp = rmsnorm(xp, norm_mixer[i])
        hs = rmsnorm(xs, norm_mixer[i])
        if kind == 0:
            mp, kp, vp = swa_prompt(hp, attn_w_qkv[j], attn_b_qkv[j], attn_w_o[j], attn_b_o[j], attn_sinks[j])
            ms, ks, vs = swa_sample(hs, cache_k[j], cache_v[j], attn_w_qkv[j], attn_b_qkv[j],
                                    attn_w_o[j], attn_b_o[j], attn_sinks[j])
            kp_l.append(kp); vp_l.append(vp); ks_l.append(ks); vs_l.append(vs)
        elif kind == 1:
            rg = (rglru_w_gate[j], rglru_w_in[j], rglru_conv_w[j], rglru_conv_b[j], rglru_wa[j],
                  rglru_ba[j], rglru_wx[j], rglru_bx[j], rglru_lambda[j], rglru_w_out[j])
            h0p = jnp.zeros((Bp, D_RNN), xp.dtype)
            bufp = jnp.zeros((Bp, RG_CONV_W - 1, D_RNN), xp.dtype)
            mp, hfp, rcp = rglru_mixer(hp, h0p, bufp, *rg)
            ms, hfs, rcs = rglru_mixer(hs, state_rglru_h[j], state_rglru_conv[j], *rg)
            hp_l.append(hfp); hs_l.append(hfs); rcp_l.append(rcp); rcs_l.append(rcs)
        else:
            bufp = jnp.zeros((Bp, SCONV_W - 1, D_SCONV), xp.dtype)
            mp, scp = sconv_mixer(hp, bufp, sconv_w_in[j], sconv_conv_w[j], sconv_w_out[j])
            ms, scs = sconv_mixer(hs, state_shortconv[j], sconv_w_in[j], sconv_conv_w[j], sconv_w_out[j])
            scp_l.append(scp); scs_l.append(scs)
        xp = xp + mp
        xs = xs + ms
        xp = xp + swiglu(rmsnorm(xp, norm_ffn[i]), ffn_w_gate[i], ffn_w_up[i], ffn_w_down[i])
        xs = xs + swiglu(rmsnorm(xs, norm_ffn[i]), ffn_w_gate[i], ffn_w_up[i], ffn_w_down[i])
    y_prompt = rmsnorm(xp, norm_final)
    y_sample = rmsnorm(xs, norm_final)
    new_k_prompt = jnp.stack(kp_l)
    new_v_prompt = jnp.stack(vp_l)
    new_k_sample = jnp.stack(ks_l)
    new_v_sample = jnp.stack(vs_l)
    new_h_prompt = jnp.stack(hp_l)
    new_h_sample = jnp.stack(hs_l)
    new_rconv_prompt = jnp.stack(rcp_l)
    new_rconv_sample = jnp.stack(rcs_l)
    new_sconv_prompt = jnp.stack(scp_l)
    new_sconv_sample = jnp.stack(scs_l)
    return (y_prompt, y_sample, new_k_prompt, new_v_prompt, new_k_sample, new_v_sample,
            new_h_prompt, new_h_sample, new_rconv_prompt, new_rconv_sample,
            new_sconv_prompt, new_sconv_sample)
```

```python
import numpy as np
from contextlib import ExitStack
import concourse.bass as bass
import concourse.mybir as mybir
from concourse.bass_utils import run_bass_kernel_spmd

F32 = mybir.dt.float32
BF16 = mybir.dt.bfloat16
AF = mybir.ActivationFunctionType
ALU = mybir.AluOpType

D = 1024
KC = 8
DFF = 2816
GT = 1024
NSEQ = 16
TS = 8
NS = NSEQ * TS
EPS = 1e-6
PAST = 8192
import os
ATT_STAGE = int(os.environ.get('ATT_STAGE', '9'))
NWS = 4

VEC = {}
_nv = 0
def _v(name, n=8):
    global _nv
    VEC[name] = _nv
    _nv += n
for _i in range(4):
    _v("nm%d" % _i); _v("nf%d" % _i)
_v("nfin")
for _j in range(2):
    _v("bq%d" % _j); _v("bk%d" % _j, 1); _v("bo%d" % _j); _v("sink%d" % _j)
for _n in ["rcw0", "rcw1", "rcw2", "rcw3", "rcb", "rba", "rbx", "rlam", "scw0", "scw1", "scw2"]:
    _v(_n)
NV = _nv


def I(name, *args, **kw):
    return lambda e: getattr(e, name)(*args, **kw)


def MM(lst):
    lst = list(lst)

    def f(e):
        ins = None
        for (o_, l_, r_, st, sp) in lst:
            ins = e.matmul(o_, l_, r_, start=st, stop=sp)
        return ins
    return f


class Prog:
    ENG = ["pe", "act", "dve", "pool", "sp"]

    def __init__(self, nc, es):
        self.nc = nc
        self.es = es
        self.q = {e: [] for e in self.ENG}
        self.esem = {e: es.enter_context(nc.semaphore("s_" + e)) for e in self.ENG}
        self.ecnt = {e: 0 for e in self.ENG}
        self.seen = {e: {} for e in self.ENG}
        self.lastw = {}
        self.readers = {}
        self.dsem = {}
        self.dcnt = {}
        self.semobj = {}
        self.bar = {}
        self.msdma = {}

    def barrier(self):
        self.bar = {("e_" + e): self.ecnt[e] for e in ("pe", "act", "dve") if self.ecnt[e] > 0}
        self.bar.update(self.msdma)
        for e in ("pe", "act", "dve"):
            self.semobj["e_" + e] = self.esem[e]

    def _deps(self, eng, reads, writes, ms=True):
        toks = []
        if ms:
            toks += list(self.bar.items())
        for k in reads:
            if k in self.lastw:
                toks.append(self.lastw[k])
        for k in writes:
            if k in self.lastw:
                toks.append(self.lastw[k])
            toks += self.readers.get(k, [])
        need = {}
        for (sid, val) in toks:
            if val > need.get(sid, 0):
                need[sid] = val
        waits = []
        for sid, val in need.items():
            if self.seen[eng].get(sid, 0) >= val:
                continue
            self.seen[eng][sid] = val
            waits.append((self.semobj[sid], val))
        return waits

    def _commit(self, tok, reads, writes):
        for k in reads:
            self.readers.setdefault(k, []).append(tok)
        for k in writes:
            self.lastw[k] = tok
            self.readers[k] = []

    def op(self, eng, fn, reads=(), writes=()):
        waits = self._deps(eng, reads, writes)
        self.ecnt[eng] += 1
        sid = "e_" + eng
        self.semobj[sid] = self.esem[eng]
        tok = (sid, self.ecnt[eng])
        self.q[eng].append((waits, fn, self.esem[eng], 1))
        self._commit(tok, reads, writes)

    def dma(self, eng, out, in_, reads, writes, skey, ms=False, **kw):
        if skey not in self.dsem:
            self.dsem[skey] = self.es.enter_context(self.nc.semaphore("d_%d" % len(self.dsem)))
            self.dcnt[skey] = 0
        sid = "d_" + str(skey)
        self.semobj[sid] = self.dsem[skey]
        waits = self._deps(eng, reads, writes, ms)
        self.dcnt[skey] += 16
        tok = (sid, self.dcnt[skey])
        if ms:
            self.msdma[sid] = self.dcnt[skey]
        self.q[eng].append((waits, (lambda e: e.dma_start(out=out, in_=in_, **kw)), self.dsem[skey], 16))
        self._commit(tok, reads, writes)

    def cc(self, fn, reads, writes, skey):
        self.dsem[skey] = self.es.enter_context(self.nc.semaphore("d_%d" % len(self.dsem)))
        self.dcnt[skey] = 0
        sid = "d_" + str(skey)
        self.semobj[sid] = self.dsem[skey]
        waits = self._deps("pool", reads, writes, False)
        self.dcnt[skey] += 1
        self.q["pool"].append((waits, fn, self.dsem[skey], None))
        self._commit((sid, self.dcnt[skey]), reads, writes)

    def final_wait(self, eng):
        waits = []
        for skey, sem in self.dsem.items():
            if self.dcnt[skey] > 0:
                waits.append((sem, self.dcnt[skey]))
        self.q[eng].append((waits, None, None, 0))

    def emit(self, block):
        def mk(ename):
            def run(e):
                for (waits, fn, sem, inc) in self.q[ename]:
                    for (s, v) in waits:
                        e.wait_ge(s, v)
                    if fn is not None:
                        ins = fn(e)
                        if inc is None:
                            ins.then_inc(sem)
                        else:
                            ins.then_inc(sem, inc)
            return run
        block.tensor(mk("pe"))
        block.scalar(mk("act"))
        block.vector(mk("dve"))
        block.gpsimd(mk("pool"))
        block.sync(mk("sp"))


def build(TP=2048, layers=4, do_sample=True):
    assert TP % GT == 0
    NG = TP // GT
    assert NG == 2
    XS = GT
    XTOT = TP + NS
    HW_ = GT + NS
    nc = bass.Bass("TRN2", target_bir_lowering=False)

    def din(name, shape):
        return nc.dram_tensor(name, list(shape), F32, kind="ExternalInput").ap()

    def dout(name, shape):
        return nc.dram_tensor(name, list(shape), F32, kind="ExternalOutput").ap()

    xpT = din("xpT", [D, TP]); xsT = din("xsT", [D, NS])
    ckT = din("ckT", [2, 128, NSEQ, 128]); ckN = din("ckN", [2, NSEQ, 128, 128])
    cvN = din("cvN", [2, NSEQ, 128, 128])
    sh = din("sh", [128, KC, NSEQ]); src = din("src", [128, KC, NSEQ, 3]); ssc = din("ssc", [128, KC, NSEQ, 2])
    vec_d = din("vec", [128, NV])
    cosP = din("cosP", [128, TP]); sinP = din("sinP", [128, TP])
    cosS = din("cosS", [128, NS]); sinS = din("sinS", [128, NS])
    cst = din("cst", [128, 128 + 512 * 3 + 128])
    xhT = din("xhT", [D, 128]); cosH = din("cosH", [128, 128]); sinH = din("sinH", [128, 128])
    flags_d = din("flags", [128, 2])
    bvb = din("bvb", [2, 128, 128])
    wqkv = din("wqkv", [2, D, 1280]); wo = din("wo", [2, D, D])
    rwg = din("rwg", [D, D]); rwi = din("rwi", [D, D]); rwax = din("rwax", [4, 256, 512])
    rwo = din("rwo", [D, D])
    swi = din("swi", [D, 3 * D]); swo = din("swo", [D, D])
    fwg = din("fwg", [4, D, DFF]); fwu = din("fwu", [4, D, DFF]); fwd = din("fwd", [4, DFF, D])

    ypT = dout("ypT", [D, TP]); ysT = dout("ysT", [D, NS])
    o_kp = dout("o_kp", [2, 128, 128]); o_vp = dout("o_vp", [2, 128, 128])
    o_ks = dout("o_ks", [2, 128, 128]); o_vs = dout("o_vs", [2, 128, 128])
    o_ksh = dout("o_ksh", [2, NSEQ, 120, 128]); o_vsh = dout("o_vsh", [2, NSEQ, 120, 128])
    o_hp = dout("o_hp", [128, KC]); o_hs = dout("o_hs", [128, KC, NSEQ])
    o_rcp = dout("o_rcp", [128, KC, 3]); o_rcs = dout("o_rcs", [128, KC, NSEQ, 3])
    o_scp = dout("o_scp", [128, KC, 2]); o_scs = dout("o_scs", [128, KC, NSEQ, 2])

    es = ExitStack()
    with es:
        P = Prog(nc, es)

        def sb(name, shape, dt=F32):
            return es.enter_context(nc.sbuf_tensor("sb_" + name, list(shape), dt))

        xT = sb("xT", [128, KC, XTOT])
        hT = sb("hT", [128, KC, HW_], BF16)
        mixT = sb("mixT", [128, KC, HW_], BF16)
        ws = [sb("ws%d" % i, [128, KC, 512], BF16) for i in range(NWS)]
        vec = sb("vec", [128, NV])
        cvec = sb("cvec", [128, 40])
        flg = sb("flg", [128, 2])
        rot = sb("rot", [128, 128], BF16)
        maskP = sb("maskP", [128, 512], BF16); maskF = sb("maskF", [128, 512], BF16); maskS = sb("maskS", [128, 512], BF16)
        ones = sb("ones", [128, 128], BF16)
        ident = sb("ident", [128, 128], BF16)
        onesh = [sb("onesh%d" % g, [128, 128], BF16) for g in range(2)]
        bvt = sb("bvt", [128, 128])
        vf = sb("vf", [128, 128])
        kcar = [[sb("kcar%d_%d" % (j, g), [128, 128], BF16) for g in range(2)] for j in range(2)]
        vcar = [[sb("vcar%d_%d" % (j, g), [128, 128], BF16) for g in range(2)] for j in range(2)]
        NTMP = 3
        tmpA = [sb("tmpA%d" % i, [128, 512]) for i in range(NTMP)]
        tmpB = [sb("tmpB%d" % i, [128, 512]) for i in range(NTMP)]
        tmpC = [sb("tmpC%d" % i, [128, 512], BF16) for i in range(NTMP)]
        zcar = sb("zcar", [128, KC, 2])
        hcar = sb("hcar", [128, KC])
        ucar = sb("ucar", [128, KC, 3])
        sst = sb("sst", [128, KC, NSEQ, 3])
        shs = sb("shs", [128, KC, NSEQ])
        seq_ext = sb("seq_ext", [128, NSEQ, 3 + TS])
        xsend = sb("xsend", [128, 256]); xrecv = sb("xrecv", [128, 256])
        MSN = 10880
        MS = sb("MS", [128, MSN])

        def msv(lo, n, dt=F32):
            v = MS[:, lo:lo + n]
            return v.bitcast(dt) if dt != F32 else v
        qT = msv(0, 4096, BF16).rearrange("p (c t) -> p c t", c=KC)
        cosT = msv(4096, 1024); sinT = msv(5120, 1024)
        kf = msv(6144, 1024)
        kpad = [msv(7168 + g * 576, 576, BF16) for g in range(2)]
        vpad = [msv(8320 + g * 576, 576, BF16).rearrange("p (a b) -> p a b", a=9) for g in range(2)]
        AV_P = (cosT, sinT, kf, kpad, vpad)
        AV_S = (msv(4096, 128), msv(4224, 128), msv(4352, 128),
                [msv(4480 + g * 576, 576, BF16) for g in range(2)],
                [msv(5632 + g * 576, 576, BF16).rearrange("p (a b) -> p a b", a=9) for g in range(2)])
        pTb = [msv(9472 + i * 256, 256, BF16) for i in range(4)]
        dnb = [msv(10496 + i * 128, 128) for i in range(3)]
        kcp = [msv(6784 + g * 1024, 1024, BF16).rearrange("p (a b) -> p a b", a=NSEQ) for g in range(2)]
        vcp = [msv(8832 + g * 1024, 1024, BF16).rearrange("p (a b) -> p a b", a=NSEQ) for g in range(2)]
        uT = msv(0, 4096).rearrange("p (c t) -> p c t", c=KC)
        uext = msv(4096, 4120).rearrange("p (c t) -> p c t", c=KC)
        ubT = msv(8216, 2048, BF16).rearrange("p (c t) -> p c t", c=KC)
        zext = msv(0, 514)
        act_t = msv(0, 2 * HW_, BF16).rearrange("p (c t) -> p c t", c=4)
        xhv = msv(6144, 1024).rearrange("p (c t) -> p c t", c=KC)
        cur = {"xb": 0, "g": 0}
        epsc = sb("epsc", [128, 1]); onec = sb("onec", [128, 1])
        ps = [es.enter_context(nc.psum_tensor("ps%d" % i, [128, 512], F32)) for i in range(8)]
        psn = [0]

        def bank():
            i = psn[0] % 8
            psn[0] += 1
            return i

        tn = [0]
        pn = [0]
        dq = [0]

        def tmpi():
            i = tn[0] % NTMP
            tn[0] += 1
            return i

        def vcol(name, c=0):
            o = VEC[name] + c
            return vec[:, o:o + 1]

        def XV(c, o, n):
            if cur["g"] == "h":
                return xhv[:, c, o:o + n]
            return xT[:, c, cur["xb"] + o:cur["xb"] + o + n]

        def XKc(c, o=0):
            g = cur["g"]
            if g == 0 and o >= GT:
                g = "s"
            if g == "s1":
                g = "s" if o < NS else 1
            return ("x", g, c)

        def s3(ap):
            return ap.rearrange("p (b t) -> p b t", t=TS)

        HK = [("h", c) for c in range(KC)]

        P.dma("sp", vec[:], vec_d[:, :], [], ["vec"], "c0")
        P.dma("pool", rot[:], cst[:, 0:128], [], ["rot"], "c_rot")
        P.dma("pool", maskP[:], cst[:, 128:640], [], ["maskP"], "c_mp")
        P.dma("pool", maskF[:], cst[:, 640:1152], [], ["maskF"], "c_mf")
        P.dma("pool", maskS[:], cst[:, 1152:1664], [], ["maskS"], "c_ms")
        P.dma("pool", ident[:], cst[:, 1664:1792], [], ["ident"], "c_id")
        P.op("dve", I("memset", ones[:], 1.0), [], ["ones"])
        P.op("dve", I("memset", epsc[:], EPS), [], ["epsc"])
        P.op("dve", I("memset", onec[:], 1.0), [], ["onec"])
        for g in range(2):
            P.op("dve", I("memset", onesh[g][:], 0.0), [], [("onesh", g)])
            P.op("dve", I("memset", onesh[g][:, g * 64:(g + 1) * 64], 1.0), [], [("onesh", g)])
            for j in range(2):
                P.op("dve", I("memset", kcar[j][g][:], 0.0), [], [("kcar", j, g)])
                P.op("dve", I("memset", vcar[j][g][:], 0.0), [], [("vcar", j, g)])
        P.dma("sp", flg[:], flags_d[:, :], [], ["flg"], "c_flg")
        lam = vec[:, VEC["rlam"]:VEC["rlam"] + 8]
        P.op("act", I("activation", tmpA[0][:, 0:8], lam, AF.Exp, scale=-1.0), ["vec"], [("tA", 0)])
        P.op("act", I("activation", tmpA[0][:, 8:16], tmpA[0][:, 0:8], AF.Ln, bias=onec[:, 0:1], scale=1.0), [("tA", 0), "onec"], [("tA", 0)])
        P.op("dve", I("tensor_scalar", cvec[:, 0:8], tmpA[0][:, 8:16], -8.0, None, ALU.mult), [("tA", 0)], ["cvec"])
        P.op("dve", I("tensor_scalar", cvec[:, 8:16], tmpA[0][:, 8:16], -16.0, None, ALU.mult), [("tA", 0)], ["cvec"])
        P.op("dve", I("tensor_scalar", cvec[:, 16:24], tmpA[0][:, 8:16], -4.0, None, ALU.mult), [("tA", 0)], ["cvec"])
        P.op("dve", I("tensor_scalar", cvec[:, 24:32], vec[:, VEC["rba"]:VEC["rba"] + 8], 0.5, None, ALU.mult), ["vec"], ["cvec"])
        P.op("dve", I("tensor_scalar", cvec[:, 32:40], vec[:, VEC["rbx"]:VEC["rbx"] + 8], 0.5, None, ALU.mult), ["vec"], ["cvec"])
        for j in range(2):
            sk = vec[:, VEC["sink%d" % j]:VEC["sink%d" % j] + 8]
            P.op("act", I("activation", sk, sk, AF.Exp), ["vec", "cvec"], ["vec"])

        wn = [0]

        def wslot():
            i = wn[0] % NWS
            wn[0] += 1
            return i

        def wload(dram_ap, kc, ncols):
            i = wslot()
            P.dma("pool", ws[i][:, 0:kc, 0:ncols], dram_ap.rearrange("(kc p) n -> p kc n", p=128), [], [("ws", i)], ("ws", i))
            return i

        def proj_chunk(slot, kc, col0, rhs_t, o, n, b, extra_reads, kbase=0):
            P.op("pe", MM([(ps[b][:, 0:n], ws[slot][:, kbase + k, col0:col0 + 128], rhs_t[:, k, o:o + n], k == 0, k == kc - 1) for k in range(kc)]),
                 [("ws", slot)] + extra_reads, [("ps", b)])

        def rms_stats(o, n):
            b = bank()
            for c in range(KC):
                t = tmpi()
                P.op("act", I("activation", tmpC[t][:, 0:n], XV(c, o, n), AF.Square), [XKc(c, o)], [("tC", t)])
                P.op("pe", I("matmul", ps[b][:, 0:n], ones[:], tmpC[t][:, 0:n], start=(c == 0), stop=(c == KC - 1)), [("tC", t), "ones"], [("ps", b)])
            t = tmpi()
            P.op("act", I("activation", tmpA[t][:, 0:n], ps[b][:, 0:n], AF.Sqrt, bias=epsc[:, 0:1], scale=1.0 / D), [("ps", b), "epsc"], [("tA", t)])
            P.op("dve", I("reciprocal", tmpB[t][:, 0:n], tmpA[t][:, 0:n]), [("tA", t)], [("tB", t)])
            return t

        def rmsnorm(gname, tiles):
            for (o, n) in tiles:
                t = rms_stats(o, n)
                for c in range(KC):
                    P.op("dve", I("scalar_tensor_tensor", hT[:, c, o:o + n], XV(c, o, n), vcol(gname, c), tmpB[t][:, 0:n], ALU.mult, ALU.mult),
                         [XKc(c, o), ("tB", t), "vec"], [("h", c)])

        def rmsnorm_final(tiles, outd, tok0):
            for (o, n) in tiles:
                t = rms_stats(o, n)
                for c in range(KC):
                    t2 = tmpi()
                    P.op("dve", I("scalar_tensor_tensor", tmpA[t2][:, 0:n], XV(c, o, n), vcol("nfin", c), tmpB[t][:, 0:n], ALU.mult, ALU.mult),
                         [XKc(c), ("tB", t), "vec"], [("tA", t2)])
                    P.dma("sp", outd[c * 128:(c + 1) * 128, tok0 + o:tok0 + o + n], tmpA[t2][:, 0:n], [("tA", t2)], [], ("out", t2))

        def out_proj_add(w_dram, tiles, bias_name=None):
            for half in range(2):
                s = wload(w_dram[:, half * 512:(half + 1) * 512], KC, 512)
                for (o, n) in tiles:
                    for cc in range(4):
                        c = half * 4 + cc
                        b = bank()
                        proj_chunk(s, KC, cc * 128, mixT, o, n, b, [("mix", k) for k in range(KC)])
                        if bias_name is None:
                            P.op("dve", I("tensor_tensor", XV(c, o, n), XV(c, o, n), ps[b][:, 0:n], ALU.add), [("ps", b), XKc(c, o)], [XKc(c, o)])
                        else:
                            P.op("dve", I("scalar_tensor_tensor", XV(c, o, n), ps[b][:, 0:n], vcol(bias_name, c), XV(c, o, n), ALU.add, ALU.add),
                                 [("ps", b), XKc(c, o), "vec"], [XKc(c, o)])

        def ffn(li, tiles):
            P.barrier()
            rmsnorm("nf%d" % li, tiles)
            f0 = 0
            while f0 < DFF:
                fw = min(512, DFF - f0)
                nfc = fw // 128
                sg = wload(fwg[li][:, f0:f0 + fw], KC, fw)
                su = wload(fwu[li][:, f0:f0 + fw], KC, fw)
                for (o, n) in tiles:
                    for fc in range(nfc):
                        bg = bank(); bu = bank()
                        proj_chunk(sg, KC, fc * 128, hT, o, n, bg, HK)
                        proj_chunk(su, KC, fc * 128, hT, o, n, bu, HK)
                        t = tmpi()
                        P.op("act", I("activation", tmpA[t][:, 0:n], ps[bg][:, 0:n], AF.Silu), [("ps", bg)], [("tA", t)])
                        P.op("dve", I("tensor_tensor", act_t[:, fc, o:o + n], tmpA[t][:, 0:n], ps[bu][:, 0:n], ALU.mult), [("tA", t), ("ps", bu)], [("act", fc)])
                sd = []
                for half in range(2):
                    i = wslot()
                    P.dma("pool", ws[i][:, 0:nfc, 0:512], fwd[li][f0:f0 + fw, half * 512:(half + 1) * 512].rearrange("(kc p) n -> p kc n", p=128),
                          [], [("ws", i)], ("ws", i))
                    sd.append(i)
                for (o, n) in tiles:
                    for c in range(KC):
                        b = bank()
                        proj_chunk(sd[c // 4], nfc, (c % 4) * 128, act_t, o, n, b, [("act", k) for k in range(nfc)])
                        P.op("dve", I("tensor_tensor", XV(c, o, n), XV(c, o, n), ps[b][:, 0:n], ALU.add), [("ps", b), XKc(c, o)], [XKc(c, o)])
                f0 += fw

        def sconv(li, tiles, sample, last, stiles=()):
            P.barrier()
            rmsnorm("nm%d" % li, tiles)
            SK = [("sst", c) for c in range(KC)]
            has_s = sample or len(stiles) > 0
            has_p = (not sample)
            if has_s:
                P.dma("sp", sst[:, :, :, 0:2], ssc[:, :, :, :], [], SK, "sst")
            for c in range(KC):
                i = wslot()
                P.dma("pool", ws[i][:, :, 0:384], swi[:, c * 384:(c + 1) * 384].rearrange("(kc p) n -> p kc n", p=128), [], [("ws", i)], ("ws", i))
                for (o, n) in tiles:
                    bb = bank(); bc = bank(); bx = bank()
                    proj_chunk(i, KC, 0, hT, o, n, bb, HK)
                    proj_chunk(i, KC, 128, hT, o, n, bc, HK)
                    proj_chunk(i, KC, 256, hT, o, n, bx, HK)
                    t = tmpi()
                    P.op("act", I("activation", tmpA[t][:, 0:n], ps[bc][:, 0:n], AF.Copy), [("ps", bc)], [("tA", t)])
                    y = tmpB[t]
                    smp = sample or (o in stiles)
                    if not smp:
                        P.op("dve", I("tensor_copy", zext[:, 0:2], zcar[:, c, :]), [("zcar", c)], ["zext"])
                        P.op("dve", I("tensor_tensor", zext[:, 2:2 + n], tmpA[t][:, 0:n], ps[bx][:, 0:n], ALU.mult), [("tA", t), ("ps", bx)], ["zext"])
                        P.op("dve", I("tensor_copy", zcar[:, c, :], zext[:, n:n + 2]), ["zext"], [("zcar", c)])
                        e0, e1, e2, yv = zext[:, 0:n], zext[:, 1:1 + n], zext[:, 2:2 + n], y[:, 0:n]
                        ek = "zext"
                    else:
                        P.op("dve", I("tensor_copy", seq_ext[:, :, 0:2], sst[:, c, :, 0:2]), [("sst", c)], ["seq_ext"])
                        P.op("dve", I("tensor_tensor", seq_ext[:, :, 2:2 + TS], s3(tmpA[t][:, 0:NS]), s3(ps[bx][:, 0:NS]), ALU.mult), [("tA", t), ("ps", bx)], ["seq_ext"])
                        P.op("dve", I("tensor_copy", sst[:, c, :, 0:2], seq_ext[:, :, TS:TS + 2]), ["seq_ext"], [("sst", c)])
                        e0, e1, e2, yv = seq_ext[:, :, 0:TS], seq_ext[:, :, 1:1 + TS], seq_ext[:, :, 2:2 + TS], s3(y[:, 0:NS])
                        ek = "seq_ext"
                    P.op("dve", I("tensor_scalar", yv, e0, vcol("scw0", c), None, ALU.mult), [ek, "vec"], [("tB", t)])
                    P.op("dve", I("scalar_tensor_tensor", yv, e1, vcol("scw1", c), yv, ALU.mult, ALU.add), [ek, ("tB", t)], [("tB", t)])
                    P.op("dve", I("scalar_tensor_tensor", yv, e2, vcol("scw2", c), yv, ALU.mult, ALU.add), [ek, ("tB", t)], [("tB", t)])
                    P.op("dve", I("tensor_tensor", mixT[:, c, o:o + n], y[:, 0:n], ps[bb][:, 0:n], ALU.mult), [("tB", t), ("ps", bb)], [("mix", c)])
            if last and has_p:
                P.dma("sp", o_scp[:, :, :], zcar[:], [("zcar", c) for c in range(KC)], [], "o_sc")
            if has_s:
                P.dma("sp", o_scs[:, :, :, :], sst[:, :, :, 0:2], SK, [], "o_scs")
            out_proj_add(swo, tiles)

        mixF = mixT[:].rearrange("p c t -> p (c t)").bitcast(F32)
        P1R = [mixF[:, i * 512:(i + 1) * 512] for i in range(9)]
        p1n = [0]

        PRJ = [(tmpA[i], tmpB[i], tmpC[i], ("tA", i), ("tB", i), ("tC", i)) for i in range(NTMP)]
        PRJ += [(P1R[3 * i], P1R[3 * i + 1], P1R[3 * i + 2].bitcast(BF16)[:, 0:512], ("p1", 3 * i), ("p1", 3 * i + 1), ("p1", 3 * i + 2)) for i in range(3)]
        prn = [0]
        P1K = [("p1", i) for i in range(9)]

        def prj_ring():
            i = prn[0] % len(PRJ)
            prn[0] += 1
            return PRJ[i]

        def p1buf():
            i = p1n[0] % 9
            p1n[0] += 1
            return P1R[i], ("p1", i)

        def rglru(li, tiles, sample, last, phase1=False, scan=True):
            P.barrier()
            rmsnorm("nm%d" % li, tiles)
            SK = [("sst", c) for c in range(KC)]
            if sample:
                P.dma("sp", shs[:], sh[:, :, :], [], [("shs", c) for c in range(KC)], "shs_in")
                P.dma("sp", sst[:], src[:, :, :, :], [], SK, "sst")
            else:
                for c in range(KC):
                    P.op("dve", I("tensor_copy", uext[:, c, 0:3], ucar[:, c, :]), [("ucar", c)], [("uext", c)])
            for (o, n) in tiles:
                for half in range(2):
                    s = wload(rwi[:, half * 512:(half + 1) * 512], KC, 512)
                    for cc in range(4):
                        c = half * 4 + cc
                        b = bank()
                        proj_chunk(s, KC, cc * 128, hT, o, n, b, HK)
                        if not sample:
                            P.op("act", I("activation", uext[:, c, 3:3 + n], ps[b][:, 0:n], AF.Copy), [("ps", b)], [("uext", c)])
                            ex = [uext[:, c, j:j + n] for j in range(4)]
                            uo = uT[:, c, 0:n]
                            rk = [("uext", c), "vec"]
                        else:
                            P.op("dve", I("tensor_copy", seq_ext[:, :, 0:3], sst[:, c, :, :]), [("sst", c)], ["seq_ext"])
                            P.op("act", I("activation", seq_ext[:, :, 3:3 + TS], s3(ps[b][:, 0:NS]), AF.Copy), [("ps", b)], ["seq_ext"])
                            P.op("dve", I("tensor_copy", sst[:, c, :, :], seq_ext[:, :, TS:TS + 3]), ["seq_ext"], [("sst", c)])
                            ex = [seq_ext[:, :, j:j + TS] for j in range(4)]
                            uo = s3(uT[:, c, 0:NS])
                            rk = ["seq_ext", "vec"]
                        P.op("dve", I("tensor_scalar", uo, ex[0], vcol("rcw0", c), vcol("rcb", c), ALU.mult, ALU.add), rk, [("u", c)])
                        for j in range(1, 4):
                            P.op("dve", I("scalar_tensor_tensor", uo, ex[j], vcol("rcw%d" % j, c), uo, ALU.mult, ALU.add), rk + [("u", c)], [("u", c)])
                        P.op("dve", I("tensor_copy", ubT[:, c, 0:n], uT[:, c, 0:n]), [("u", c)], [("ub", c)])
                        if not sample:
                            P.op("dve", I("tensor_copy", uext[:, c, 0:3], uext[:, c, n:n + 3]), [("uext", c), ("u", c)], [("uext", c)])
                for nb in range(4):
                    ia = wslot()
                    P.dma("pool", ws[ia][:, 0:2, 0:512], rwax[nb].rearrange("(kc p) n -> p kc n", p=128), [], [("ws", ia)], ("ws", ia))
                    sgt = None if phase1 else wload(rwg[:, nb * 256:(nb + 1) * 256], KC, 256)
                    if phase1:
                        ti = cur["g"] * 2 + o // 512
                        ur = [("ub", nb * 2), ("ub", nb * 2 + 1), ("ws", ia)]
                        st = []
                        for sub in range(2):
                            c = nb * 2 + sub
                            ba_ = bank(); bx_ = bank()
                            P.op("pe", MM([(ps[ba_][:, 0:n], ws[ia][:, k, sub * 128:(sub + 1) * 128], ubT[:, nb * 2 + k, 0:n], k == 0, k == 1) for k in range(2)]), ur, [("ps", ba_)])
                            P.op("pe", MM([(ps[bx_][:, 0:n], ws[ia][:, k, 256 + sub * 128:256 + (sub + 1) * 128], ubT[:, nb * 2 + k, 0:n], k == 0, k == 1) for k in range(2)]), ur, [("ps", bx_)])
                            (t1, k1), (t2, k2), (a_, ka), (m_, km) = p1buf(), p1buf(), p1buf(), p1buf()
                            P.op("act", I("activation", t1[:, 0:n], ps[ba_][:, 0:n], AF.Tanh, bias=cvec[:, 24 + c:25 + c], scale=0.5), [("ps", ba_), "cvec"], [k1])
                            P.op("act", I("activation", t2[:, 0:n], ps[bx_][:, 0:n], AF.Tanh, bias=cvec[:, 32 + c:33 + c], scale=0.5), [("ps", bx_), "cvec"], [k2])
                            P.op("act", I("activation", a_[:, 0:n], t1[:, 0:n], AF.Exp, bias=cvec[:, 16 + c:17 + c], scale=cvec[:, 16 + c:17 + c]), [k1, "cvec"], [ka])
                            P.op("act", I("activation", m_[:, 0:n], t1[:, 0:n], AF.Exp, bias=cvec[:, c:c + 1], scale=cvec[:, c:c + 1]), [k1, "cvec"], [km])
                            P.op("dve", I("scalar_tensor_tensor", t2[:, 0:n], t2[:, 0:n], 1.0, uT[:, c, 0:n], ALU.add, ALU.mult), [k2, ("u", c)], [k2])
                            st.append((c, t1, k1, t2, k2, a_, ka, m_, km))
                        for (c, t1, k1, t2, k2, a_, ka, m_, km) in st:
                            P.op("act", I("activation", m_[:, 0:n], m_[:, 0:n], AF.Sqrt, bias=onec[:, 0:1], scale=-1.0), [km, "onec"], [km])
                        for (c, t1, k1, t2, k2, a_, ka, m_, km) in st:
                            P.op("dve", I("scalar_tensor_tensor", t2[:, 0:n], t2[:, 0:n], 0.5, m_[:, 0:n], ALU.mult, ALU.mult), [k2, km], [k2])
                            P.dma("sp", abv(0, ti)[:, c, 0:n], a_[:, 0:n], [ka], [("abd", ti, c, 0)], ("st", ka), ms=True)
                            P.dma("sp", abv(1, ti)[:, c, 0:n], t2[:, 0:n], [k2], [("abd", ti, c, 1)], ("st", k2), ms=True)
                            if scan:
                                P.op("dve", I("tensor_tensor_scan", t1[:, 0:n], a_[:, 0:n], t2[:, 0:n], hcar[:, c:c + 1], ALU.mult, ALU.add), [ka, k2, ("hcar", c), k1], [k1])
                                P.op("dve", I("tensor_copy", hcar[:, c:c + 1], t1[:, n - 1:n]), [k1], [("hcar", c)])
                        continue
                    for sub in range(2):
                        c = nb * 2 + sub
                        ba_ = bank(); bx_ = bank(); bg_ = bank()
                        ur = [("ub", nb * 2), ("ub", nb * 2 + 1), ("ws", ia)]
                        P.op("pe", MM([(ps[ba_][:, 0:n], ws[ia][:, k, sub * 128:(sub + 1) * 128], ubT[:, nb * 2 + k, 0:n], k == 0, k == 1) for k in range(2)]), ur, [("ps", ba_)])
                        P.op("pe", MM([(ps[bx_][:, 0:n], ws[ia][:, k, 256 + sub * 128:256 + (sub + 1) * 128], ubT[:, nb * 2 + k, 0:n], k == 0, k == 1) for k in range(2)]), ur, [("ps", bx_)])
                        if not phase1:
                            proj_chunk(sgt, KC, sub * 128, hT, o, n, bg_, HK)
                        if phase1:
                            (r_, kr), (ig_, ki), (a_, ka), (m_, km) = p1buf(), p1buf(), p1buf(), p1buf()
                        else:
                            t = tmpi(); t2 = tmpi()
                            r_ = tmpA[t]; ig_ = tmpB[t]; a_ = tmpA[t2]; m_ = tmpB[t2]
                            kr, ki, ka, km = ("tA", t), ("tB", t), ("tA", t2), ("tB", t2)
                        P.op("act", I("activation", r_[:, 0:n], ps[ba_][:, 0:n], AF.Sigmoid, bias=vcol("rba", c), scale=1.0), [("ps", ba_), "vec"], [kr])
                        P.op("act", I("activation", ig_[:, 0:n], ps[bx_][:, 0:n], AF.Sigmoid, bias=vcol("rbx", c), scale=1.0), [("ps", bx_), "vec"], [ki])
                        P.op("act", I("activation", a_[:, 0:n], r_[:, 0:n], AF.Exp, scale=cvec[:, c:c + 1]), [kr, "cvec"], [ka])
                        P.op("act", I("activation", m_[:, 0:n], r_[:, 0:n], AF.Exp, scale=cvec[:, 8 + c:9 + c]), [kr, "cvec"], [km])
                        P.op("act", I("activation", m_[:, 0:n], m_[:, 0:n], AF.Sqrt, bias=onec[:, 0:1], scale=-1.0), [km, "onec"], [km])
                        P.op("dve", I("tensor_tensor", ig_[:, 0:n], ig_[:, 0:n], m_[:, 0:n], ALU.mult), [ki, km], [ki])
                        P.op("dve", I("tensor_tensor", ig_[:, 0:n], ig_[:, 0:n], uT[:, c, 0:n], ALU.mult), [ki, ("u", c)], [ki])
                        hs_ = r_
                        if phase1:
                            ti = cur["g"] * 2 + o // 512
                            P.dma("sp", abv(0, ti)[:, c, 0:n], a_[:, 0:n], [ka], [("abd", ti, c, 0)], ("st", ka), ms=True)
                            P.dma("sp", abv(1, ti)[:, c, 0:n], ig_[:, 0:n], [ki], [("abd", ti, c, 1)], ("st", ki), ms=True)
                        if not sample and scan:
                            P.op("dve", I("tensor_tensor_scan", hs_[:, 0:n], a_[:, 0:n], ig_[:, 0:n], hcar[:, c:c + 1], ALU.mult, ALU.add),
                                 [ka, ki, ("hcar", c), kr], [kr])
                            P.op("dve", I("tensor_copy", hcar[:, c:c + 1], hs_[:, n - 1:n]), [kr], [("hcar", c)])
                        elif sample:
                            a3 = s3(a_[:, 0:NS]); b3 = s3(ig_[:, 0:NS])
                            P.op("dve", I("tensor_tensor", seq_ext[:, :, 0:1], a3[:, :, 0:1], shs[:, c, :].unsqueeze(2), ALU.mult), [ka, ("shs", c)], ["seq_ext"])
                            P.op("dve", I("tensor_tensor", b3[:, :, 0:1], b3[:, :, 0:1], seq_ext[:, :, 0:1], ALU.add), ["seq_ext", ki], [ki])
                            P.op("dve", I("memset", a3[:, :, 0:1], 0.0), [ka], [ka])
                            P.op("dve", I("tensor_tensor_scan", hs_[:, 0:NS], a_[:, 0:NS], ig_[:, 0:NS], 0.0, ALU.mult, ALU.add),
                                 [ka, ki, kr], [kr])
                            P.op("dve", I("tensor_copy", shs[:, c, :].unsqueeze(2), s3(hs_[:, 0:NS])[:, :, TS - 1:TS]), [kr], [("shs", c)])
                        if not phase1:
                            P.op("act", I("activation", a_[:, 0:n], ps[bg_][:, 0:n], AF.Gelu), [("ps", bg_), ka], [ka])
                            P.op("dve", I("tensor_tensor", mixT[:, c, o:o + n], hs_[:, 0:n], a_[:, 0:n], ALU.mult), [kr, ka], [("mix", c)])
            if not sample:
                for c in range(KC):
                    P.op("dve", I("tensor_copy", ucar[:, c, :], uext[:, c, 0:3]), [("uext", c)], [("ucar", c)])
            if phase1:
                return
            if last:
                if not sample:
                    P.dma("sp", o_hp[:, :], hcar[:], [("hcar", c) for c in range(KC)], [], "o_h")
                    P.dma("sp", o_rcp[:, :, :], ucar[:], [("ucar", c) for c in range(KC)], [], "o_rcp")
                else:
                    P.dma("sp", o_hs[:, :, :], shs[:], [("shs", c) for c in range(KC)], [], "o_hs")
                    P.dma("sp", o_rcs[:, :, :, :], sst[:], SK, [], "o_rcs")
            out_proj_add(rwo, tiles)

        ab_scr = nc.dram_tensor("ab_scr", [2, 2 * NG, 128, KC * 512], F32)

        def abv(which, ti):
            return ab_scr[which, ti].rearrange("p (c t) -> p c t", c=KC)
        NPB = 8
        pa = [msv(i * 512, 512) for i in range(NPB)]
        pb = [msv((NPB + i) * 512, 512) for i in range(NPB)]
        pq = [0]

        def rglru_p2(li, tiles, last):
            P.barrier()
            rmsnorm("nm%d" % li, tiles)
            for (o, n) in tiles:
                ti = cur["g"] * 2 + o // 512
                for nb in range(4):
                    sgt = wload(rwg[:, nb * 256:(nb + 1) * 256], KC, 256)
                    for sub in range(2):
                        c = nb * 2 + sub
                        bg_ = bank()
                        proj_chunk(sgt, KC, sub * 128, hT, o, n, bg_, HK)
                        i = pq[0] % NPB
                        pq[0] += 1
                        P.dma("sp", pa[i][:, 0:n], abv(0, ti)[:, c, 0:n], [("abd", ti, c, 0)], [("pa", i)], ("ld_a", i), ms=True)
                        P.dma("sp", pb[i][:, 0:n], abv(1, ti)[:, c, 0:n], [("abd", ti, c, 1)], [("pb", i)], ("ld_b", i), ms=True)
                        t = tmpi()
                        hs_ = tmpA[t]; g_ = tmpB[t]
                        P.op("act", I("activation", g_[:, 0:n], ps[bg_][:, 0:n], AF.Gelu), [("ps", bg_)], [("tB", t)])
                        P.op("dve", I("tensor_tensor_scan", hs_[:, 0:n], pa[i][:, 0:n], pb[i][:, 0:n], hcar[:, c:c + 1], ALU.mult, ALU.add),
                             [("pa", i), ("pb", i), ("hcar", c)], [("tA", t)])
                        P.op("dve", I("tensor_copy", hcar[:, c:c + 1], hs_[:, n - 1:n]), [("tA", t)], [("hcar", c)])
                        P.op("dve", I("tensor_tensor", mixT[:, c, o:o + n], hs_[:, 0:n], g_[:, 0:n], ALU.mult), [("tA", t), ("tB", t)], [("mix", c)])
            if last:
                P.dma("sp", o_hp[:, :], hcar[:], [("hcar", c) for c in range(KC)], [], "o_h")
            out_proj_add(rwo, tiles)

        def attn(li, j, tiles, sample, first, last, tok0):
            P.barrier()
            cosT, sinT, kf, kpad, vpad = AV_S if sample else AV_P
            rmsnorm("nm%d" % li, tiles)
            ntok = sum(n for _, n in tiles)
            if not sample:
                P.dma("sp", cosT[:, 0:ntok], cosP[:, tok0:tok0 + ntok], [], ["cos"], "cs_cos", ms=True)
                P.dma("sp", sinT[:, 0:ntok], sinP[:, tok0:tok0 + ntok], [], ["sin"], "cs_sin", ms=True)
            else:
                P.dma("sp", cosT[:, 0:ntok], cosS[:, :], [], ["cos"], "cs_cos", ms=True)
                P.dma("sp", sinT[:, 0:ntok], sinS[:, :], [], ["sin"], "cs_sin", ms=True)
            P.dma("sp", bvt[:], bvb[j], [], ["bvt"], "cs_bvt")
            for g in range(2):
                P.op("dve", I("memset", kpad[g][:], 0.0), [], ["kbuf"])
                P.op("dve", I("memset", vpad[g][:].rearrange("p a b -> p (a b)"), 0.0), [], [("vpad", g)])
                if sample:
                    P.op("dve", I("memset", kcp[g][:].rearrange("p a b -> p (a b)"), 0.0), [], ["kcT"])
                    P.op("dve", I("memset", vcp[g][:].rearrange("p a b -> p (a b)"), 0.0), [], [("vcp", g)])
            if not sample:
                for g in range(2):
                    P.op("dve", I("tensor_copy", kpad[g][:, 0:128], kcar[j][g][:]), [("kcar", j, g)], ["kbuf"])
                    P.op("dve", I("tensor_copy", vpad[g][:, 0, :], vcar[j][g][:]), [("vcar", j, g)], [("vpad", g)])
            for (c0, ncol) in [(0, 512), (512, 512), (1024, 128)]:
                s = wload(wqkv[j][:, c0:c0 + ncol], KC, ncol)
                for (o, n) in tiles:
                    for cc in range(ncol // 128):
                        c = c0 // 128 + cc
                        b = bank()
                        proj_chunk(s, KC, cc * 128, hT, o, n, b, HK)
                        A_, B_, C_, kA, kB, kC = prj_ring()
                        bias = vcol("bq%d" % j, c) if c < 8 else vcol("bk%d" % j, 0)
                        P.op("act", I("activation", A_[:, 0:n], ps[b][:, 0:n], AF.Identity, bias=bias, scale=1.0), [("ps", b), "vec"], [kA])
                        P.op("act", I("activation", C_[:, 0:n], A_[:, 0:n], AF.Copy), [kA], [kC])
                        b2 = bank()
                        P.op("pe", I("matmul", ps[b2][:, 0:n], rot[:], C_[:, 0:n], start=True, stop=True), [kC, "rot"], [("ps", b2)])
                        P.op("dve", I("tensor_tensor", A_[:, 0:n], A_[:, 0:n], cosT[:, o:o + n], ALU.mult), [kA, "cos"], [kA])
                        P.op("dve", I("tensor_tensor", B_[:, 0:n], ps[b2][:, 0:n], sinT[:, o:o + n], ALU.mult), [("ps", b2), "sin"], [kB])
                        if c < 8:
                            P.op("dve", I("tensor_tensor", qT[:, c, o:o + n], A_[:, 0:n], B_[:, 0:n], ALU.add), [kA, kB], [("q", c)])
                        else:
                            P.op("dve", I("tensor_tensor", kf[:, o:o + n], A_[:, 0:n], B_[:, 0:n], ALU.add), [kA, kB], ["kf"])
                            for g in range(2):
                                P.op("act", I("activation", kpad[g][g * 64:(g + 1) * 64, 128 + o:128 + o + n], kf[g * 64:(g + 1) * 64, o:o + n], AF.Copy), ["kf"], ["kbuf"])
            sv = wload(wqkv[j][:, 1152:1280], KC, 128)
            nblk = ntok // 128
            for bi in range(nblk):
                b = bank()
                P.op("pe", MM([(ps[b][:, 0:128], hT[:, k, bi * 128:(bi + 1) * 128], ws[sv][:, k, 0:128], k == 0, k == KC - 1) for k in range(KC)]),
                     [("ws", sv)] + HK, [("ps", b)])
                P.op("dve", I("tensor_tensor", vf[:], ps[b][:, 0:128], bvt[:], ALU.add), [("ps", b), "bvt"], ["vf"])
                for g in range(2):
                    P.op("act", I("activation", vpad[g][:, 1 + bi, g * 64:(g + 1) * 64], vf[:, g * 64:(g + 1) * 64], AF.Copy), ["vf"], [("vpad", g)])
                if last and bi == nblk - 1:
                    P.dma("sp", (o_vs if sample else o_vp)[j], vf[:], ["vf"], [], ("o_v", sample, j))
            if last:
                P.dma("sp", (o_ks if sample else o_kp)[j], kf[:, ntok - 128:ntok], ["kf"], [], ("o_k", sample, j), ms=True)
            if sample:
                for g in range(2):
                    P.dma("pool", kcp[g][g * 64:(g + 1) * 64, :, :], ckT[j][g * 64:(g + 1) * 64, :, :], [], ["kcT"], "kc", ms=True)
                for g in range(2):
                    P.dma("pool", vcp[g][:, :, g * 64:(g + 1) * 64], cvN[j].rearrange("b k f -> k b f")[:, :, g * 64:(g + 1) * 64], [], [("vcp", g)], ("vc", g), ms=True)
                P.dma("sp", o_ksh[j], ckN[j][:, 8:128, :], [], [], "o_shk")
                P.dma("sp", o_vsh[j], cvN[j][:, 8:128, :], [], [], "o_sh")
            sinkn = "sink%d" % j
            if not sample:
                def v4(ap):
                    return ap.rearrange("p (a b) -> p a b", a=4)
                for bi in range(nblk):
                    msk, mr = (maskF, "maskF") if (first and bi == 0) else (maskP, "maskP")
                    for hh in range(2):
                        cs = [4 * hh + i for i in range(4)]
                        qv = qT[:, 4 * hh:4 * hh + 4, bi * 128:(bi + 1) * 128]
                        pts = []
                        for kb in range(2):
                            for g in range(2):
                                i4 = kb * 2 + g
                                bs = bank()
                                mb = msk[:, i4 * 128:(i4 + 1) * 128].unsqueeze(1).broadcast_to([128, 4, 128])
                                P.op("pe", MM([(v4(ps[bs][:, :]), kpad[g][:, (bi + kb) * 128:(bi + kb + 1) * 128], qv, True, False),
                                               (v4(ps[bs][:, :]), ident[:], mb, False, True)]),
                                     [("q", c) for c in cs] + ["kbuf", mr, "ident"], [("ps", bs)])
                                pi = pn[0] % 4
                                pn[0] += 1
                                P.op("act", I("activation", pTb[pi][:, :], ps[bs][:, :], AF.Exp, scale=0.125), [("ps", bs)], [("pT", pi)])
                                pts.append((pi, kb, g))
                        bo = bank(); bd = bank()
                        lo = []; ld = []
                        for n_, (pi, kb, g) in enumerate(pts):
                            lo.append((ps[bo][:, :], vpad[g][:, bi + kb, :], pTb[pi][:, :], n_ == 0, n_ == 3))
                            ld.append((ps[bd][:, :], onesh[g][:], pTb[pi][:, :], n_ == 0, n_ == 3))
                        rk = [("pT", pi) for (pi, _, _) in pts]
                        P.op("pe", MM(lo), rk + [("vpad", 0), ("vpad", 1)], [("ps", bo)])
                        P.op("pe", MM(ld), rk + [("onesh", 0), ("onesh", 1)], [("ps", bd)])
                        t2 = tmpi()
                        dn = tmpA[t2]
                        for i4, c_ in enumerate(cs):
                            P.op("act", I("activation", dn[:, i4 * 128:(i4 + 1) * 128], ps[bd][:, i4 * 128:(i4 + 1) * 128], AF.Identity, bias=vcol(sinkn, c_), scale=1.0),
                                 [("ps", bd), "vec"], [("tA", t2)])
                        P.op("dve", I("reciprocal", dn[:, :], dn[:, :]), [("tA", t2)], [("tA", t2)])
                        P.op("dve", I("tensor_tensor", mixT[:, 4 * hh:4 * hh + 4, bi * 128:(bi + 1) * 128], v4(ps[bo][:, :]), v4(dn[:, :]), ALU.mult),
                             [("tA", t2), ("ps", bo)], [("mix", c) for c in cs] + P1K)
            for bi in range(nblk if sample else 0):
                for c in range(KC):
                    bs = bank()
                    qr = [("q", c), "kbuf"]
                    lst = []
                    for g in range(2):
                        lst.append((ps[bs][:, g * 128:(g + 1) * 128], kpad[g][:, 128:256], qT[:, c, 0:128], True, True))
                    for g in range(2):
                        for sq in range(NSEQ):
                            c0_ = 256 + g * 128 + sq * TS
                            lst.append((ps[bs][:, c0_:c0_ + TS], kcp[g][:, sq, :], qT[:, c, sq * TS:(sq + 1) * TS], True, True))
                    P.op("pe", MM(lst), qr + ["kcT"], [("ps", bs)])
                    msk, mr = maskS, "maskS"
                    t = tmpi()
                    pT = tmpC[t]
                    P.op("act", I("activation", pT[:, :], ps[bs][:, :], AF.Exp, scale=0.125), [("ps", bs)], [("tC", t)])
                    P.op("dve", I("tensor_tensor", pT[:, :], pT[:, :], msk[:, :], ALU.mult), [("tC", t), mr], [("tC", t)])
                    bo = bank(); bd = bank()
                    lo = []; ld = []
                    for g in range(2):
                        lo.append((ps[bo][:, 0:128], vpad[g][:, 1, :], pT[:, g * 128:(g + 1) * 128], g == 0, g == 1))
                        ld.append((ps[bd][:, 0:128], onesh[g][:], pT[:, g * 128:(g + 1) * 128], g == 0, g == 1))
                    for sq in range(NSEQ):
                        for g in range(2):
                            c0_ = 256 + g * 128 + sq * TS
                            lo.append((ps[bo][:, 128 + sq * TS:128 + (sq + 1) * TS], vcp[g][:, sq, :], pT[:, c0_:c0_ + TS], g == 0, g == 1))
                            ld.append((ps[bd][:, 128 + sq * TS:128 + (sq + 1) * TS], onesh[g][:], pT[:, c0_:c0_ + TS], g == 0, g == 1))
                    P.op("pe", MM(lo), [("tC", t), ("vpad", 0), ("vpad", 1), ("vcp", 0), ("vcp", 1)], [("ps", bo)])
                    P.op("pe", MM(ld), [("tC", t), ("onesh", 0), ("onesh", 1)], [("ps", bd)])
                    t2 = tmpi()
                    dn = tmpA[t2]; on = tmpB[t2]
                    P.op("dve", I("tensor_scalar", dn[:, 0:128], ps[bd][:, 0:128], vcol(sinkn, c), None, ALU.add), [("ps", bd), "vec"], [("tA", t2)])
                    P.op("dve", I("tensor_tensor", dn[:, 0:128], dn[:, 0:128], ps[bd][:, 128:256], ALU.add), [("ps", bd), ("tA", t2)], [("tA", t2)])
                    P.op("act", I("activation", on[:, 0:128], ps[bo][:, 0:128], AF.Copy), [("ps", bo)], [("tB", t2)])
                    P.op("dve", I("tensor_tensor", on[:, 0:128], on[:, 0:128], ps[bo][:, 128:256], ALU.add), [("ps", bo), ("tB", t2)], [("tB", t2)])
                    P.op("dve", I("reciprocal", dn[:, 0:128], dn[:, 0:128]), [("tA", t2)], [("tA", t2)])
                    P.op("dve", I("tensor_tensor", mixT[:, c, 0:128], on[:, 0:128], dn[:, 0:128], ALU.mult), [("tA", t2), ("tB", t2)], [("mix", c)] + P1K)
            if not sample:
                for g in range(2):
                    P.op("dve", I("tensor_copy", kcar[j][g][:], kpad[g][:, 8 * 128:9 * 128]), ["kbuf"], [("kcar", j, g)])
                    P.op("dve", I("tensor_copy", vcar[j][g][:], vpad[g][:, 8, :]), [("vpad", g)], [("vcar", j, g)])
            out_proj_add(wo[j], tiles, "bo%d" % j)

        PAIRS = [[0, 1], [2, 3], [4, 5], [6, 7]]
        xch = {}

        def exchange(idx, F):
            cin = nc.dram_tensor("cc_in%d" % idx, [128, F], F32)
            cout = nc.dram_tensor("cc_out%d" % idx, [128, F], F32)
            P.op("dve", I("tensor_scalar", xsend[:, 0:F], xsend[:, 0:F], flg[:, 0:1], None, ALU.mult), ["xsend", "flg"], ["xsend"])
            P.dma("sp", cin[:, :], xsend[:, 0:F], ["xsend"], [("cin", idx)], ("cin", idx))
            P.cc(lambda e: e.collective_compute("AllReduce", ALU.add, replica_groups=PAIRS, ins=[cin.ap().opt()], outs=[cout.ap().opt()]),
                 [("cin", idx)], [("cout", idx)], ("cc", idx))
            xch[idx] = (cout, F)

        def receive(idx):
            cout, F = xch[idx]
            P.dma("sp", xrecv[:, 0:F], cout[:, :], [("cout", idx)], ["xrecv"], ("rcv", idx))

        def kv_block(li, j, o):
            P.barrier()
            rmsnorm("nm%d" % li, [(o, 128)])
            n = 128
            s_ = wload(wqkv[j][:, 1024:1152], KC, 128)
            b = bank()
            proj_chunk(s_, KC, 0, hT, o, n, b, HK)
            t = tmpi()
            P.op("act", I("activation", tmpA[t][:, 0:n], ps[b][:, 0:n], AF.Identity, bias=vcol("bk%d" % j, 0), scale=1.0), [("ps", b), "vec"], [("tA", t)])
            P.op("act", I("activation", tmpC[t][:, 0:n], tmpA[t][:, 0:n], AF.Copy), [("tA", t)], [("tC", t)])
            b2 = bank()
            P.op("pe", I("matmul", ps[b2][:, 0:n], rot[:], tmpC[t][:, 0:n], start=True, stop=True), [("tC", t), "rot"], [("ps", b2)])
            P.op("dve", I("tensor_tensor", tmpA[t][:, 0:n], tmpA[t][:, 0:n], cosT[:, o:o + n], ALU.mult), [("tA", t), "cos"], [("tA", t)])
            P.op("dve", I("tensor_tensor", tmpB[t][:, 0:n], ps[b2][:, 0:n], sinT[:, o:o + n], ALU.mult), [("ps", b2), "sin"], [("tB", t)])
            P.op("dve", I("tensor_tensor", xsend[:, 0:128], tmpA[t][:, 0:n], tmpB[t][:, 0:n], ALU.add), [("tA", t), ("tB", t)], ["xsend"])
            sv = wload(wqkv[j][:, 1152:1280], KC, 128)
            b = bank()
            P.op("pe", MM([(ps[b][:, 0:128], hT[:, k, o:o + 128], ws[sv][:, k, 0:128], k == 0, k == KC - 1) for k in range(KC)]), [("ws", sv)] + HK, [("ps", b)])
            P.op("dve", I("tensor_tensor", xsend[:, 128:256], ps[b][:, 0:128], bvt[:], ALU.add), [("ps", b), "bvt"], ["xsend"])

        def set_carry(j, src, skey):
            for g in range(2):
                P.op("act", I("activation", kcar[j][g][g * 64:(g + 1) * 64, :], src[g * 64:(g + 1) * 64, 0:128], AF.Copy), [skey], [("kcar", j, g)])
                P.op("act", I("activation", vcar[j][g][:, g * 64:(g + 1) * 64], src[:, 128 + g * 64:128 + (g + 1) * 64], AF.Copy), [skey], [("vcar", j, g)])

        def pre_sconv(li, o, dest, dkeys):
            P.barrier()
            rmsnorm("nm%d" % li, [(o, 128)])
            n = 128
            for c in range(KC):
                i = wslot()
                P.dma("pool", ws[i][:, :, 0:256], swi[:, c * 384 + 128:(c + 1) * 384].rearrange("(kc p) n -> p kc n", p=128), [], [("ws", i)], ("ws", i))
                bc = bank(); bx = bank()
                proj_chunk(i, KC, 0, hT, o, n, bc, HK)
                proj_chunk(i, KC, 128, hT, o, n, bx, HK)
                t = tmpi()
                P.op("act", I("activation", tmpA[t][:, 0:n], ps[bc][:, 0:n], AF.Copy), [("ps", bc)], [("tA", t)])
                P.op("dve", I("tensor_tensor", dest[:, 2 * c:2 * c + 2], tmpA[t][:, n - 2:n], ps[bx][:, n - 2:n], ALU.mult), [("tA", t), ("ps", bx)], dkeys(c))

        PT = [(0, 512), (512, 512)]
        ST = [(0, NS)]
        PST = PT + [(GT, NS)]
        def setg(g):
            cur["g"] = g
            cur["xb"] = {0: 0, "s": GT, "s1": GT, 1: GT + NS, "h": 0}[g]

        def load_x(src, tok0, tiles):
            o0 = tiles[0][0]
            n_all = sum(n for _, n in tiles)
            if cur["g"] == "h":
                dst = xhv[:, :, o0:o0 + n_all]
            else:
                dst = xT[:, :, cur["xb"] + o0:cur["xb"] + o0 + n_all]
            P.dma("sp", dst, src[:, tok0 + o0:tok0 + o0 + n_all].rearrange("(c p) t -> p c t", p=128), [], [XKc(c) for c in range(KC)], ("xin", cur["g"]), ms=(cur["g"] == "h"))

        setg("h"); load_x(xhT, 0, [(0, 128)])
        for g in range(NG):
            setg(g); load_x(xpT, g * GT, PT)
        setg("s"); load_x(xsT, 0, ST)
        CK = lambda nm: [(nm, c) for c in range(KC)]
        P.op("dve", I("memset", zcar[:].rearrange("p a b -> p (a b)"), 0.0), [], CK("zcar"))
        P.op("dve", I("memset", hcar[:], 0.0), [], CK("hcar"))
        P.op("dve", I("memset", ucar[:].rearrange("p a b -> p (a b)"), 0.0), [], CK("ucar"))
        LG = NG - 1
        setg("h")
        P.barrier()
        P.dma("sp", cosT[:, 0:128], cosH[:, :], [], ["cos"], "cs_cos", ms=True)
        P.dma("sp", sinT[:, 0:128], sinH[:, :], [], ["sin"], "cs_sin", ms=True)
        P.dma("sp", bvt[:], bvb[0], [], ["bvt"], "cs_bvt")
        kv_block(0, 0, 0)
        set_carry(0, xsend, "xsend")
        setg("s"); attn(0, 0, ST, True, False, True, 0)
        setg(0); attn(0, 0, PT, False, True, False, 0)
        ffn(0, PST)
        setg(1); attn(0, 0, PT, False, False, True, GT); ffn(0, PT)
        for g in range(NG):
            setg(g); rglru(1, PT, False, False, phase1=True)
        P.dma("sp", o_rcp[:, :, :], ucar[:], CK("ucar"), [], "o_rcp")
        P.op("dve", I("tensor_copy", xsend[:, 0:8], hcar[:]), CK("hcar"), ["xsend"])
        P.op("dve", I("tensor_copy", xsend[:, 8:32], ucar[:].rearrange("p a b -> p (a b)")), CK("ucar"), ["xsend"])
        exchange(0, 32)
        setg("s"); rglru(1, ST, True, True)
        receive(0)
        P.op("dve", I("tensor_scalar", hcar[:], xrecv[:, 0:8], flg[:, 1:2], None, ALU.mult), ["xrecv", "flg"], CK("hcar"))
        P.op("dve", I("tensor_scalar", ucar[:].rearrange("p a b -> p (a b)"), xrecv[:, 8:32], flg[:, 1:2], None, ALU.mult), ["xrecv", "flg"], CK("ucar"))
        setg(0); rglru(1, [(0, 128)], False, False, phase1=True, scan=False)
        setg(0); rglru_p2(1, PT, False); ffn(1, PST)
        setg(1); rglru_p2(1, PT, True); ffn(1, PT)
        S1T = [(0, NS), (NS, 512), (NS + 512, 512)]
        setg(1); pre_sconv(2, GT - 128, xsend, lambda c: ["xsend"])
        exchange(1, 16)
        setg(0); pre_sconv(2, GT - 128, zcar[:].rearrange("p a b -> p (a b)"), lambda c: [("zcar", c)])
        setg("s1"); sconv(2, S1T, False, True, stiles=(0,)); ffn(2, S1T)
        setg(LG)
        P.barrier()
        P.dma("sp", cosT[:, GT - 128:GT], cosP[:, TP - 128:TP], [], ["cos"], "cs_cos", ms=True)
        P.dma("sp", sinT[:, GT - 128:GT], sinP[:, TP - 128:TP], [], ["sin"], "cs_sin", ms=True)
        P.dma("sp", bvt[:], bvb[1], [], ["bvt"], "cs_bvt")
        kv_block(3, 1, GT - 128)
        exchange(2, 256)
        receive(1)
        P.op("dve", I("tensor_scalar", zcar[:].rearrange("p a b -> p (a b)"), xrecv[:, 0:16], flg[:, 1:2], None, ALU.mult), ["xrecv", "flg"], CK("zcar"))
        setg(0); sconv(2, PT, False, False); ffn(2, PT)
        setg("s"); attn(3, 1, ST, True, False, True, 0)
        receive(2)
        set_carry(1, xrecv, "xrecv")
        setg(0); attn(3, 1, PT, False, True, False, 0); ffn(3, PST)
        rmsnorm_final(PT, ypT, 0)
        setg("s"); rmsnorm_final(ST, ysT, 0)
        setg(1); attn(3, 1, PT, False, False, True, GT); ffn(3, PT); rmsnorm_final(PT, ypT, GT)
        P.final_wait("sp")
        block = es.enter_context(nc.Block())
        P.emit(block)
    return nc


def _fm(v):
    return np.ascontiguousarray(np.asarray(v, np.float32).reshape(8, 128).T)


def _host_consts(TP, start, half):
    hd = 32
    inv = (np.float32(10000.0) ** (-(np.arange(hd, dtype=np.float32)) / np.float32(hd))).astype(np.float32)

    def tables(pos):
        ang = (pos.astype(np.float32)[:, None] * inv[None, :]).astype(np.float32)
        cos = np.cos(ang.astype(np.float64)).astype(np.float32).T
        sin = np.sin(ang.astype(np.float64)).astype(np.float32).T
        return np.ascontiguousarray(np.tile(cos, (4, 1))), np.ascontiguousarray(np.tile(sin, (4, 1)))
    cosP, sinP = tables(start + np.arange(TP))
    cosH, sinH = tables(np.maximum(start - 128 + np.arange(128), 0))
    cosS, sinS = tables(PAST + (np.arange(NS) % TS))
    rot = np.zeros((128, 128), np.float32)
    for blk in range(2):
        for d in range(64):
            m = blk * 64 + d
            if d < 32:
                rot[blk * 64 + d + 32, m] = -1.0
            else:
                rot[blk * 64 + d - 32, m] = 1.0
    k = np.arange(128)[:, None]
    q = np.arange(128)[None, :]
    prev = (k > q).astype(np.float32)
    cur = (k <= q).astype(np.float32)
    zero = np.zeros_like(prev)
    NEG = np.float32(-30000.0)
    maskP = (np.concatenate([prev, prev, cur, cur], axis=1) - 1.0) * (-NEG)
    maskF = maskP if half else (np.concatenate([zero, zero, cur, cur], axis=1) - 1.0) * (-NEG)
    mnew = ((k // TS == q // TS) & (k % TS <= q % TS)).astype(np.float32)
    mc = (k > (q % TS)).astype(np.float32)
    maskS = np.concatenate([mnew, mnew, mc, mc], axis=1)
    cst = np.ascontiguousarray(np.concatenate([rot, maskP, maskF, maskS, np.eye(128, dtype=np.float32)], axis=1).astype(np.float32))
    flags = np.zeros((128, 2), np.float32)
    flags[:, 0] = 1.0 - half
    flags[:, 1] = float(half)
    return dict(cosP=cosP, sinP=sinP, cosH=cosH, sinH=sinH, cosS=cosS, sinS=sinS, cst=cst, flags=flags)


def _prep_shared(inp):
    f = lambda a: np.ascontiguousarray(np.asarray(a, np.float32))
    vec = np.zeros((128, NV), np.float32)

    def put(name, v):
        vec[:, VEC[name]:VEC[name] + 8] = _fm(v)
    for i in range(4):
        put("nm%d" % i, inp["norm_mixer"][i]); put("nf%d" % i, inp["norm_ffn"][i])
    put("nfin", inp["norm_final"])
    perm = np.concatenate([np.arange((g * 8 + c) * 64, (g * 8 + c) * 64 + 64) for c in range(8) for g in range(2)])
    wq = np.asarray(inp["attn_w_qkv"], np.float32)
    bq = np.asarray(inp["attn_b_qkv"], np.float32)
    wqkv = f(np.concatenate([wq[:, :, perm], wq[:, :, 1024:]], axis=2))
    wo = f(np.asarray(inp["attn_w_o"], np.float32)[:, perm, :])
    bvb = np.zeros((2, 128, 128), np.float32)
    for j in range(2):
        put("bq%d" % j, bq[j][perm])
        vec[:, VEC["bk%d" % j]] = bq[j][1024:1152]
        put("bo%d" % j, inp["attn_b_o"][j])
        sk = np.asarray(inp["attn_sinks"], np.float32)[j]
        for c in range(8):
            vec[0:64, VEC["sink%d" % j] + c] = sk[c]
            vec[64:128, VEC["sink%d" % j] + c] = sk[c + 8]
        bvb[j] = np.broadcast_to(bq[j][1152:1280][None, :], (128, 128))
    for jj in range(4):
        put("rcw%d" % jj, inp["rglru_conv_w"][0][jj])
    put("rcb", inp["rglru_conv_b"][0]); put("rba", inp["rglru_ba"][0]); put("rbx", inp["rglru_bx"][0]); put("rlam", inp["rglru_lambda"][0])
    for jj in range(3):
        put("scw%d" % jj, inp["sconv_conv_w"][0][jj])
    sh = dict(vec=vec, bvb=bvb, wqkv=wqkv, wo=wo,
              rwg=f(inp["rglru_w_gate"][0]), rwi=f(inp["rglru_w_in"][0]), rwax=f(np.concatenate([np.asarray(inp["rglru_wa"][0], np.float32), np.asarray(inp["rglru_wx"][0], np.float32)], axis=2)),
              rwo=f(inp["rglru_w_out"][0]), swi=f(np.asarray(inp["sconv_w_in"][0], np.float32).reshape(D, 3, 8, 128).transpose(0, 2, 1, 3).reshape(D, 3 * D)), swo=f(inp["sconv_w_out"][0]),
              fwg=f(inp["ffn_w_gate"]), fwu=f(inp["ffn_w_up"]), fwd=f(inp["ffn_w_down"]))
    return sh


def run(inp, SEQ=4096, n_cores=8):
    TP = SEQ // 2
    nc = build(TP)
    shared = _prep_shared(inp)
    f = lambda a: np.ascontiguousarray(np.asarray(a, np.float32))
    xp = np.asarray(inp["x_prompt"], np.float32); xs = np.asarray(inp["x_sample"], np.float32)
    ck = np.asarray(inp["cache_k"], np.float32).reshape(2, 128, 128, 128)
    cv = np.asarray(inp["cache_v"], np.float32).reshape(2, 128, 128, 128)
    srh = np.asarray(inp["state_rglru_h"], np.float32)[0]
    src = np.asarray(inp["state_rglru_conv"], np.float32)[0]
    ssc = np.asarray(inp["state_shortconv"], np.float32)[0]
    in_maps = []
    for c in range(n_cores):
        seq, half = c // 2, c % 2
        start = half * TP
        b0 = c * NSEQ
        m = dict(shared)
        m.update(_host_consts(TP, start, half))
        m["xpT"] = f(xp[seq, start:start + TP].T)
        m["xhT"] = f(xp[seq, start - 128:start].T) if half else np.zeros((D, 128), np.float32)
        m["xsT"] = f(xs[b0:b0 + NSEQ].reshape(NS, D).T)
        m["ckN"] = f(ck[:, b0:b0 + NSEQ]); m["cvN"] = f(cv[:, b0:b0 + NSEQ])
        m["ckT"] = f(ck[:, b0:b0 + NSEQ].transpose(0, 3, 1, 2))
        m["sh"] = f(srh[b0:b0 + NSEQ].reshape(NSEQ, 8, 128).transpose(2, 1, 0))
        m["src"] = f(src[b0:b0 + NSEQ].reshape(NSEQ, 3, 8, 128).transpose(3, 2, 0, 1))
        m["ssc"] = f(ssc[b0:b0 + NSEQ].reshape(NSEQ, 2, 8, 128).transpose(3, 2, 0, 1))
        in_maps.append(m)
    res = run_bass_kernel_spmd(nc, in_maps, core_ids=list(range(n_cores)))
    R = res.results
    NB = n_cores * NSEQ
    nP = n_cores // 2
    y_p = np.stack([np.concatenate([R[2 * s]["ypT"].T, R[2 * s + 1]["ypT"].T], axis=0) for s in range(nP)])
    y_s = np.concatenate([R[c]["ysT"].T.reshape(NSEQ, TS, D) for c in range(n_cores)])
    L = lambda s: R[2 * s + 1]
    kp = np.stack([np.stack([L(s)["o_kp"][j].T.reshape(128, 2, 64) for s in range(nP)]) for j in range(2)])
    vp = np.stack([np.stack([L(s)["o_vp"][j].reshape(128, 2, 64) for s in range(nP)]) for j in range(2)])

    def samp(sh_name, new_name, transpose):
        out = np.zeros((2, NB, 128, 128), np.float32)
        for j in range(2):
            for c in range(n_cores):
                out[j, c * NSEQ:(c + 1) * NSEQ, 0:120] = R[c][sh_name][j]
                nw = R[c][new_name][j]
                nw = nw.T if transpose else nw
                out[j, c * NSEQ:(c + 1) * NSEQ, 120:128] = nw.reshape(NSEQ, TS, 128)
        return out.reshape(2, NB, 128, 2, 64)
    ks = samp("o_ksh", "o_ks", True)
    vs = samp("o_vsh", "o_vs", False)
    hp = np.stack([L(s)["o_hp"].T.reshape(D) for s in range(nP)])[None]
    hs = np.concatenate([R[c]["o_hs"].transpose(2, 1, 0).reshape(NSEQ, D) for c in range(n_cores)])[None]
    rcp = np.stack([L(s)["o_rcp"].transpose(2, 1, 0).reshape(3, D) for s in range(nP)])[None]
    rcs = np.concatenate([R[c]["o_rcs"].transpose(2, 3, 1, 0).reshape(NSEQ, 3, D) for c in range(n_cores)])[None]
    scp = np.stack([L(s)["o_scp"].transpose(2, 1, 0).reshape(2, D) for s in range(nP)])[None]
    scs = np.concatenate([R[c]["o_scs"].transpose(2, 3, 1, 0).reshape(NSEQ, 2, D) for c in range(n_cores)])[None]
    outs = (y_p, y_s, kp, vp, ks, vs, hp, hs, rcp, rcs, scp, scs)
    return tuple(np.ascontiguousarray(o.astype(np.float32)) for o in outs)


def kernel(**inputs):
    return run(inputs, 4096, 8)
```

```python
import numpy as np
from contextlib import ExitStack
import concourse.bass as bass
import concourse.mybir as mybir
from concourse.bass_utils import run_bass_kernel_spmd

F32 = mybir.dt.float32
BF16 = mybir.dt.bfloat16
AF = mybir.ActivationFunctionType
ALU = mybir.AluOpType

D = 1024
KC = 8
DFF = 2816
GT = 1024
NSEQ = 16
TS = 8
NS = NSEQ * TS
EPS = 1e-6
PAST = 8192
import os
ATT_STAGE = int(os.environ.get('ATT_STAGE', '9'))
NWS = 4

VEC = {}
_nv = 0
def _v(name, n=8):
    global _nv
    VEC[name] = _nv
    _nv += n
for _i in range(4):
    _v("nm%d" % _i); _v("nf%d" % _i)
_v("nfin")
for _j in range(2):
    _v("bq%d" % _j); _v("bk%d" % _j, 1); _v("bo%d" % _j); _v("sink%d" % _j)
for _n in ["rcw0", "rcw1", "rcw2", "rcw3", "rcb", "rba", "rbx", "rlam", "scw0", "scw1", "scw2"]:
    _v(_n)
NV = _nv


def I(name, *args, **kw):
    return lambda e: getattr(e, name)(*args, **kw)


def MM(lst):
    lst = list(lst)

    def f(e):
        ins = None
        for (o_, l_, r_, st, sp) in lst:
            ins = e.matmul(o_, l_, r_, start=st, stop=sp)
        return ins
    return f


class Prog:
    ENG = ["pe", "act", "dve", "pool", "sp"]

    def __init__(self, nc, es):
        self.nc = nc
        self.es = es
        self.q = {e: [] for e in self.ENG}
        self.esem = {e: es.enter_context(nc.semaphore("s_" + e)) for e in self.ENG}
        self.ecnt = {e: 0 for e in self.ENG}
        self.seen = {e: {} for e in self.ENG}
        self.lastw = {}
        self.readers = {}
        self.dsem = {}
        self.dcnt = {}
        self.semobj = {}
        self.bar = {}
        self.msdma = {}

    def barrier(self):
        self.bar = {("e_" + e): self.ecnt[e] for e in ("pe", "act", "dve") if self.ecnt[e] > 0}
        self.bar.update(self.msdma)
        for e in ("pe", "act", "dve"):
            self.semobj["e_" + e] = self.esem[e]

    def _deps(self, eng, reads, writes, ms=True):
        toks = []
        if ms:
            toks += list(self.bar.items())
        for k in reads:
            if k in self.lastw:
                toks.append(self.lastw[k])
        for k in writes:
            if k in self.lastw:
                toks.append(self.lastw[k])
            toks += self.readers.get(k, [])
        need = {}
        for (sid, val) in toks:
            if val > need.get(sid, 0):
                need[sid] = val
        waits = []
        for sid, val in need.items():
            if self.seen[eng].get(sid, 0) >= val:
                continue
            self.seen[eng][sid] = val
            waits.append((self.semobj[sid], val))
        return waits

    def _commit(self, tok, reads, writes):
        for k in reads:
            self.readers.setdefault(k, []).append(tok)
        for k in writes:
            self.lastw[k] = tok
            self.readers[k] = []

    def op(self, eng, fn, reads=(), writes=(), ms=True):
        waits = self._deps(eng, reads, writes, ms)
        self.ecnt[eng] += 1
        sid = "e_" + eng
        self.semobj[sid] = self.esem[eng]
        tok = (sid, self.ecnt[eng])
        self.q[eng].append((waits, fn, self.esem[eng], 1))
        self._commit(tok, reads, writes)

    def dma(self, eng, out, in_, reads, writes, skey, ms=False, **kw):
        if skey not in self.dsem:
            self.dsem[skey] = self.es.enter_context(self.nc.semaphore("d_%d" % len(self.dsem)))
            self.dcnt[skey] = 0
        sid = "d_" + str(skey)
        self.semobj[sid] = self.dsem[skey]
        waits = self._deps(eng, reads, writes, ms)
        self.dcnt[skey] += 16
        tok = (sid, self.dcnt[skey])
        if ms:
            self.msdma[sid] = self.dcnt[skey]
        self.q[eng].append((waits, (lambda e: e.dma_start(out=out, in_=in_, **kw)), self.dsem[skey], 16))
        self._commit(tok, reads, writes)

    def cc(self, fn, reads, writes, skey):
        self.dsem[skey] = self.es.enter_context(self.nc.semaphore("d_%d" % len(self.dsem)))
        self.dcnt[skey] = 0
        sid = "d_" + str(skey)
        self.semobj[sid] = self.dsem[skey]
        waits = self._deps("pool", reads, writes, False)
        self.dcnt[skey] += 1
        self.q["pool"].append((waits, fn, self.dsem[skey], None))
        self._commit((sid, self.dcnt[skey]), reads, writes)

    def final_wait(self, eng):
        waits = []
        for skey, sem in self.dsem.items():
            if self.dcnt[skey] > 0:
                waits.append((sem, self.dcnt[skey]))
        self.q[eng].append((waits, None, None, 0))

    def emit(self, block):
        def mk(ename):
            def run(e):
                for (waits, fn, sem, inc) in self.q[ename]:
                    for (s, v) in waits:
                        e.wait_ge(s, v)
                    if fn is not None:
                        ins = fn(e)
                        if inc is None:
                            ins.then_inc(sem)
                        else:
                            ins.then_inc(sem, inc)
            return run
        block.tensor(mk("pe"))
        block.scalar(mk("act"))
        block.vector(mk("dve"))
        block.gpsimd(mk("pool"))
        block.sync(mk("sp"))


def build(TP=2048, layers=4, do_sample=True):
    assert TP % GT == 0
    NG = TP // GT
    assert NG == 2
    XS = GT
    XTOT = TP + NS
    HW_ = GT + NS
    nc = bass.Bass("TRN2", target_bir_lowering=False)

    def din(name, shape):
        return nc.dram_tensor(name, list(shape), F32, kind="ExternalInput").ap()

    def dout(name, shape):
        return nc.dram_tensor(name, list(shape), F32, kind="ExternalOutput").ap()

    xpT = din("xpT", [D, TP]); xsT = din("xsT", [D, NS])
    ckT = din("ckT", [2, 128, NSEQ, 128]); ckN = din("ckN", [2, NSEQ, 128, 128])
    cvN = din("cvN", [2, NSEQ, 128, 128])
    sh = din("sh", [128, KC, NSEQ]); src = din("src", [128, KC, NSEQ, 3]); ssc = din("ssc", [128, KC, NSEQ, 2])
    vec_d = din("vec", [128, NV])
    cosP = din("cosP", [128, TP]); sinP = din("sinP", [128, TP])
    cosS = din("cosS", [128, NS]); sinS = din("sinS", [128, NS])
    cst = din("cst", [128, 128 + 512 * 3 + 128])
    xhT = din("xhT", [D, 128]); cosH = din("cosH", [128, 128]); sinH = din("sinH", [128, 128])
    flags_d = din("flags", [128, 2])
    bvb = din("bvb", [2, 128, 128])
    wqkv = din("wqkv", [2, D, 1280]); wo = din("wo", [2, D, D])
    rwg = din("rwg", [D, D]); rwi = din("rwi", [D, D]); rwax = din("rwax", [4, 256, 512])
    rwo = din("rwo", [D, D])
    swi = din("swi", [D, 3 * D]); swo = din("swo", [D, D])
    fwg = din("fwg", [4, D, DFF]); fwu = din("fwu", [4, D, DFF]); fwd = din("fwd", [4, DFF, D])

    ypT = dout("ypT", [D, TP]); ysT = dout("ysT", [D, NS])
    o_kp = dout("o_kp", [2, 128, 128]); o_vp = dout("o_vp", [2, 128, 128])
    o_ks = dout("o_ks", [2, 128, 128]); o_vs = dout("o_vs", [2, 128, 128])
    o_ksh = dout("o_ksh", [2, NSEQ, 120, 128]); o_vsh = dout("o_vsh", [2, NSEQ, 120, 128])
    o_hp = dout("o_hp", [128, KC]); o_hs = dout("o_hs", [128, KC, NSEQ])
    o_rcp = dout("o_rcp", [128, KC, 3]); o_rcs = dout("o_rcs", [128, KC, NSEQ, 3])
    o_scp = dout("o_scp", [128, KC, 2]); o_scs = dout("o_scs", [128, KC, NSEQ, 2])

    es = ExitStack()
    with es:
        P = Prog(nc, es)

        def sb(name, shape, dt=F32):
            return es.enter_context(nc.sbuf_tensor("sb_" + name, list(shape), dt))

        xT = sb("xT", [128, KC, XTOT])
        hT = sb("hT", [128, KC, HW_], BF16)
        mixT = sb("mixT", [128, KC, HW_], BF16)
        ws = [sb("ws%d" % i, [128, KC, 512], BF16) for i in range(NWS)]
        vec = sb("vec", [128, NV])
        cvec = sb("cvec", [128, 40])
        flg = sb("flg", [128, 2])
        rot = sb("rot", [128, 128], BF16)
        maskP = sb("maskP", [128, 512], BF16); maskF = sb("maskF", [128, 512], BF16); maskS = sb("maskS", [128, 512], BF16)
        ones = sb("ones", [128, 128], BF16)
        ident = sb("ident", [128, 128], BF16)
        onesh = [sb("onesh%d" % g, [128, 128], BF16) for g in range(2)]
        bvt = sb("bvt", [128, 128])
        vf = sb("vf", [128, 128])
        kcar = [[sb("kcar%d_%d" % (j, g), [128, 128], BF16) for g in range(2)] for j in range(2)]
        vcar = [[sb("vcar%d_%d" % (j, g), [128, 128], BF16) for g in range(2)] for j in range(2)]
        NTMP = 3
        tmpA = [sb("tmpA%d" % i, [128, 512]) for i in range(NTMP)]
        tmpB = [sb("tmpB%d" % i, [128, 512]) for i in range(NTMP)]
        tmpC = [sb("tmpC%d" % i, [128, 512], BF16) for i in range(NTMP)]
        zcar = sb("zcar", [128, KC, 2])
        hcar = sb("hcar", [128, KC])
        ucar = sb("ucar", [128, KC, 3])
        sst = sb("sst", [128, KC, NSEQ, 3])
        shs = sb("shs", [128, KC, NSEQ])
        seq_ext = sb("seq_ext", [128, NSEQ, 3 + TS])
        xsend = sb("xsend", [128, 256]); xrecv = sb("xrecv", [128, 256])
        MSN = 10880
        MS = sb("MS", [128, MSN])

        def msv(lo, n, dt=F32):
            v = MS[:, lo:lo + n]
            return v.bitcast(dt) if dt != F32 else v
        qT = msv(0, 4096, BF16).rearrange("p (c t) -> p c t", c=KC)
        cosT = msv(4096, 1024); sinT = msv(5120, 1024)
        kf = msv(6144, 1024)
        kpad = [msv(7168 + g * 576, 576, BF16) for g in range(2)]
        vpad = [msv(8320 + g * 576, 576, BF16).rearrange("p (a b) -> p a b", a=9) for g in range(2)]
        AV_P = (cosT, sinT, kf, kpad, vpad)
        AV_S = (msv(4096, 128), msv(4224, 128), msv(4352, 128),
                [msv(4480 + g * 576, 576, BF16) for g in range(2)],
                [msv(5632 + g * 576, 576, BF16).rearrange("p (a b) -> p a b", a=9) for g in range(2)])
        pTb = [msv(9472 + i * 256, 256, BF16) for i in range(4)]
        dnb = [msv(10496 + i * 128, 128) for i in range(3)]
        kcp = [msv(6784 + g * 1024, 1024, BF16).rearrange("p (a b) -> p a b", a=NSEQ) for g in range(2)]
        vcp = [msv(8832 + g * 1024, 1024, BF16).rearrange("p (a b) -> p a b", a=NSEQ) for g in range(2)]
        uT = msv(0, 4096).rearrange("p (c t) -> p c t", c=KC)
        uext = msv(4096, 4120).rearrange("p (c t) -> p c t", c=KC)
        ubT = msv(8216, 2048, BF16).rearrange("p (c t) -> p c t", c=KC)
        zext = msv(0, 514)
        act_t = msv(0, 2 * HW_, BF16).rearrange("p (c t) -> p c t", c=4)
        xhv = msv(6144, 1024).rearrange("p (c t) -> p c t", c=KC)
        cur = {"xb": 0, "g": 0}
        epsc = sb("epsc", [128, 1]); onec = sb("onec", [128, 1])
        ps = [es.enter_context(nc.psum_tensor("ps%d" % i, [128, 512], F32)) for i in range(8)]
        psn = [0]

        def bank():
            i = psn[0] % 8
            psn[0] += 1
            return i

        tn = [0]
        pn = [0]
        dq = [0]

        def tmpi():
            i = tn[0] % NTMP
            tn[0] += 1
            return i

        def vcol(name, c=0):
            o = VEC[name] + c
            return vec[:, o:o + 1]

        def XV(c, o, n):
            if cur["g"] == "h":
                return xhv[:, c, o:o + n]
            return xT[:, c, cur["xb"] + o:cur["xb"] + o + n]

        def XKc(c, o=0):
            g = cur["g"]
            if g == 0 and o >= GT:
                g = "s"
            if g == "s1":
                g = "s" if o < NS else 1
            return ("x", g, c)

        def s3(ap):
            return ap.rearrange("p (b t) -> p b t", t=TS)

        HK = [("h", c) for c in range(KC)]

        P.dma("sp", vec[:], vec_d[:, :], [], ["vec"], "c0")
        P.dma("pool", rot[:], cst[:, 0:128], [], ["rot"], "c_rot")
        P.dma("pool", maskP[:], cst[:, 128:640], [], ["maskP"], "c_mp")
        P.dma("pool", maskF[:], cst[:, 640:1152], [], ["maskF"], "c_mf")
        P.dma("pool", maskS[:], cst[:, 1152:1664], [], ["maskS"], "c_ms")
        P.dma("pool", ident[:], cst[:, 1664:1792], [], ["ident"], "c_id")
        P.op("dve", I("memset", ones[:], 1.0), [], ["ones"])
        P.op("dve", I("memset", epsc[:], EPS), [], ["epsc"])
        P.op("dve", I("memset", onec[:], 1.0), [], ["onec"])
        for g in range(2):
            P.op("dve", I("memset", onesh[g][:], 0.0), [], [("onesh", g)])
            P.op("dve", I("memset", onesh[g][:, g * 64:(g + 1) * 64], 1.0), [], [("onesh", g)])
            for j in range(2):
                P.op("dve", I("memset", kcar[j][g][:], 0.0), [], [("kcar", j, g)])
                P.op("dve", I("memset", vcar[j][g][:], 0.0), [], [("vcar", j, g)])
        P.dma("sp", flg[:], flags_d[:, :], [], ["flg"], "c_flg")
        lam = vec[:, VEC["rlam"]:VEC["rlam"] + 8]
        P.op("act", I("activation", tmpA[0][:, 0:8], lam, AF.Exp, scale=-1.0), ["vec"], [("tA", 0)])
        P.op("act", I("activation", tmpA[0][:, 8:16], tmpA[0][:, 0:8], AF.Ln, bias=onec[:, 0:1], scale=1.0), [("tA", 0), "onec"], [("tA", 0)])
        P.op("dve", I("tensor_scalar", cvec[:, 0:8], tmpA[0][:, 8:16], -8.0, None, ALU.mult), [("tA", 0)], ["cvec"])
        P.op("dve", I("tensor_scalar", cvec[:, 8:16], tmpA[0][:, 8:16], -16.0, None, ALU.mult), [("tA", 0)], ["cvec"])
        P.op("dve", I("tensor_scalar", cvec[:, 16:24], tmpA[0][:, 8:16], -4.0, None, ALU.mult), [("tA", 0)], ["cvec"])
        P.op("dve", I("tensor_scalar", cvec[:, 24:32], vec[:, VEC["rba"]:VEC["rba"] + 8], 0.5, None, ALU.mult), ["vec"], ["cvec"])
        P.op("dve", I("tensor_scalar", cvec[:, 32:40], vec[:, VEC["rbx"]:VEC["rbx"] + 8], 0.5, None, ALU.mult), ["vec"], ["cvec"])
        for j in range(2):
            sk = vec[:, VEC["sink%d" % j]:VEC["sink%d" % j] + 8]
            P.op("act", I("activation", sk, sk, AF.Exp), ["vec", "cvec"], ["vec"])

        wn = [0]

        def wslot():
            i = wn[0] % NWS
            wn[0] += 1
            return i

        def wload(dram_ap, kc, ncols):
            i = wslot()
            P.dma("pool", ws[i][:, 0:kc, 0:ncols], dram_ap.rearrange("(kc p) n -> p kc n", p=128), [], [("ws", i)], ("ws", i))
            return i

        def proj_chunk(slot, kc, col0, rhs_t, o, n, b, extra_reads, kbase=0):
            P.op("pe", MM([(ps[b][:, 0:n], ws[slot][:, kbase + k, col0:col0 + 128], rhs_t[:, k, o:o + n], k == 0, k == kc - 1) for k in range(kc)]),
                 [("ws", slot)] + extra_reads, [("ps", b)])

        def rms_stats(o, n):
            b = bank()
            for c in range(KC):
                t = tmpi()
                P.op("act", I("activation", tmpC[t][:, 0:n], XV(c, o, n), AF.Square), [XKc(c, o)], [("tC", t)], ms=False)
                P.op("pe", I("matmul", ps[b][:, 0:n], ones[:], tmpC[t][:, 0:n], start=(c == 0), stop=(c == KC - 1)), [("tC", t), "ones"], [("ps", b)], ms=False)
            t = tmpi()
            P.op("act", I("activation", tmpA[t][:, 0:n], ps[b][:, 0:n], AF.Sqrt, bias=epsc[:, 0:1], scale=1.0 / D), [("ps", b), "epsc"], [("tA", t)], ms=False)
            P.op("dve", I("reciprocal", tmpB[t][:, 0:n], tmpA[t][:, 0:n]), [("tA", t)], [("tB", t)], ms=False)
            return t

        def rmsnorm(gname, tiles):
            for (o, n) in tiles:
                t = rms_stats(o, n)
                for c in range(KC):
                    P.op("dve", I("scalar_tensor_tensor", hT[:, c, o:o + n], XV(c, o, n), vcol(gname, c), tmpB[t][:, 0:n], ALU.mult, ALU.mult),
                         [XKc(c, o), ("tB", t), "vec"], [("h", c)], ms=False)

        def rmsnorm_final(tiles, outd, tok0):
            for (o, n) in tiles:
                t = rms_stats(o, n)
                for c in range(KC):
                    t2 = tmpi()
                    P.op("dve", I("scalar_tensor_tensor", tmpA[t2][:, 0:n], XV(c, o, n), vcol("nfin", c), tmpB[t][:, 0:n], ALU.mult, ALU.mult),
                         [XKc(c), ("tB", t), "vec"], [("tA", t2)])
                    P.dma("sp", outd[c * 128:(c + 1) * 128, tok0 + o:tok0 + o + n], tmpA[t2][:, 0:n], [("tA", t2)], [], ("out", t2))

        def out_proj_add(w_dram, tiles, bias_name=None):
            sl = [wload(w_dram[:, half * 512:(half + 1) * 512], KC, 512) for half in range(2)]
            for (o, n) in tiles:
                for c in range(KC):
                    b = bank()
                    proj_chunk(sl[c // 4], KC, (c % 4) * 128, mixT, o, n, b, [("mix", k) for k in range(KC)])
                    if bias_name is None:
                        P.op("dve", I("tensor_tensor", XV(c, o, n), XV(c, o, n), ps[b][:, 0:n], ALU.add), [("ps", b), XKc(c, o)], [XKc(c, o)])
                    else:
                        P.op("dve", I("scalar_tensor_tensor", XV(c, o, n), ps[b][:, 0:n], vcol(bias_name, c), XV(c, o, n), ALU.add, ALU.add),
                             [("ps", b), XKc(c, o), "vec"], [XKc(c, o)])

        def ffn(li, tiles):
            rmsnorm("nf%d" % li, tiles)
            f0 = 0
            while f0 < DFF:
                fw = min(512, DFF - f0)
                nfc = fw // 128
                sg = wload(fwg[li][:, f0:f0 + fw], KC, fw)
                su = wload(fwu[li][:, f0:f0 + fw], KC, fw)
                for (o, n) in tiles:
                    for fc in range(nfc):
                        bg = bank(); bu = bank()
                        proj_chunk(sg, KC, fc * 128, hT, o, n, bg, HK)
                        proj_chunk(su, KC, fc * 128, hT, o, n, bu, HK)
                        t = tmpi()
                        P.op("act", I("activation", tmpA[t][:, 0:n], ps[bg][:, 0:n], AF.Silu), [("ps", bg)], [("tA", t)])
                        P.op("dve", I("tensor_tensor", act_t[:, fc, o:o + n], tmpA[t][:, 0:n], ps[bu][:, 0:n], ALU.mult), [("tA", t), ("ps", bu)], [("act", fc)])
                sd = []
                for half in range(2):
                    i = wslot()
                    P.dma("pool", ws[i][:, 0:nfc, 0:512], fwd[li][f0:f0 + fw, half * 512:(half + 1) * 512].rearrange("(kc p) n -> p kc n", p=128),
                          [], [("ws", i)], ("ws", i))
                    sd.append(i)
                for (o, n) in tiles:
                    for c in range(KC):
                        b = bank()
                        proj_chunk(sd[c // 4], nfc, (c % 4) * 128, act_t, o, n, b, [("act", k) for k in range(nfc)])
                        P.op("dve", I("tensor_tensor", XV(c, o, n), XV(c, o, n), ps[b][:, 0:n], ALU.add), [("ps", b), XKc(c, o)], [XKc(c, o)])
                f0 += fw

        def sconv(li, tiles, sample, last, stiles=()):
            P.barrier()
            rmsnorm("nm%d" % li, tiles)
            SK = [("sst", c) for c in range(KC)]
            has_s = sample or len(stiles) > 0
            has_p = (not sample)
            if has_s:
                P.dma("sp", sst[:, :, :, 0:2], ssc[:, :, :, :], [], SK, "sst")
            for c in range(KC):
                i = wslot()
                P.dma("pool", ws[i][:, :, 0:384], swi[:, c * 384:(c + 1) * 384].rearrange("(kc p) n -> p kc n", p=128), [], [("ws", i)], ("ws", i))
                for (o, n) in tiles:
                    bb = bank(); bc = bank(); bx = bank()
                    proj_chunk(i, KC, 0, hT, o, n, bb, HK)
                    proj_chunk(i, KC, 128, hT, o, n, bc, HK)
                    proj_chunk(i, KC, 256, hT, o, n, bx, HK)
                    t = tmpi()
                    P.op("act", I("activation", tmpA[t][:, 0:n], ps[bc][:, 0:n], AF.Copy), [("ps", bc)], [("tA", t)])
                    y = tmpB[t]
                    smp = sample or (o in stiles)
                    if not smp:
                        P.op("dve", I("tensor_copy", zext[:, 0:2], zcar[:, c, :]), [("zcar", c)], ["zext"])
                        P.op("dve", I("tensor_tensor", zext[:, 2:2 + n], tmpA[t][:, 0:n], ps[bx][:, 0:n], ALU.mult), [("tA", t), ("ps", bx)], ["zext"])
                        P.op("dve", I("tensor_copy", zcar[:, c, :], zext[:, n:n + 2]), ["zext"], [("zcar", c)])
                        e0, e1, e2, yv = zext[:, 0:n], zext[:, 1:1 + n], zext[:, 2:2 + n], y[:, 0:n]
                        ek = "zext"
                    else:
                        P.op("dve", I("tensor_copy", seq_ext[:, :, 0:2], sst[:, c, :, 0:2]), [("sst", c)], ["seq_ext"])
                        P.op("dve", I("tensor_tensor", seq_ext[:, :, 2:2 + TS], s3(tmpA[t][:, 0:NS]), s3(ps[bx][:, 0:NS]), ALU.mult), [("tA", t), ("ps", bx)], ["seq_ext"])
                        P.op("dve", I("tensor_copy", sst[:, c, :, 0:2], seq_ext[:, :, TS:TS + 2]), ["seq_ext"], [("sst", c)])
                        e0, e1, e2, yv = seq_ext[:, :, 0:TS], seq_ext[:, :, 1:1 + TS], seq_ext[:, :, 2:2 + TS], s3(y[:, 0:NS])
                        ek = "seq_ext"
                    P.op("dve", I("tensor_scalar", yv, e0, vcol("scw0", c), None, ALU.mult), [ek, "vec"], [("tB", t)])
                    P.op("dve", I("scalar_tensor_tensor", yv, e1, vcol("scw1", c), yv, ALU.mult, ALU.add), [ek, ("tB", t)], [("tB", t)])
                    P.op("dve", I("scalar_tensor_tensor", yv, e2, vcol("scw2", c), yv, ALU.mult, ALU.add), [ek, ("tB", t)], [("tB", t)])
                    P.op("dve", I("tensor_tensor", mixT[:, c, o:o + n], y[:, 0:n], ps[bb][:, 0:n], ALU.mult), [("tB", t), ("ps", bb)], [("mix", c)])
            if last and has_p:
                P.dma("sp", o_scp[:, :, :], zcar[:], [("zcar", c) for c in range(KC)], [], "o_sc")
            if has_s:
                P.dma("sp", o_scs[:, :, :, :], sst[:, :, :, 0:2], SK, [], "o_scs")
            out_proj_add(swo, tiles)

        mixF = mixT[:].rearrange("p c t -> p (c t)").bitcast(F32)
        P1R = [mixF[:, i * 512:(i + 1) * 512] for i in range(9)]
        p1n = [0]

        PRJ = [(tmpA[i], tmpB[i], tmpC[i], ("tA", i), ("tB", i), ("tC", i)) for i in range(NTMP)]
        PRJ += [(P1R[3 * i], P1R[3 * i + 1], P1R[3 * i + 2].bitcast(BF16)[:, 0:512], ("p1", 3 * i), ("p1", 3 * i + 1), ("p1", 3 * i + 2)) for i in range(3)]
        prn = [0]
        P1K = [("p1", i) for i in range(9)]

        def prj_ring():
            i = prn[0] % len(PRJ)
            prn[0] += 1
            return PRJ[i]

        def p1buf():
            i = p1n[0] % 9
            p1n[0] += 1
            return P1R[i], ("p1", i)

        def rglru(li, tiles, sample, last, phase1=False, scan=True):
            P.barrier()
            rmsnorm("nm%d" % li, tiles)
            SK = [("sst", c) for c in range(KC)]
            if sample:
                P.dma("sp", shs[:], sh[:, :, :], [], [("shs", c) for c in range(KC)], "shs_in")
                P.dma("sp", sst[:], src[:, :, :, :], [], SK, "sst")
            else:
                for c in range(KC):
                    P.op("dve", I("tensor_copy", uext[:, c, 0:3], ucar[:, c, :]), [("ucar", c)], [("uext", c)])
            for (o, n) in tiles:
                for half in range(2):
                    s = wload(rwi[:, half * 512:(half + 1) * 512], KC, 512)
                    for cc in range(4):
                        c = half * 4 + cc
                        b = bank()
                        proj_chunk(s, KC, cc * 128, hT, o, n, b, HK)
                        if not sample:
                            P.op("act", I("activation", uext[:, c, 3:3 + n], ps[b][:, 0:n], AF.Copy), [("ps", b)], [("uext", c)])
                            ex = [uext[:, c, j:j + n] for j in range(4)]
                            uo = uT[:, c, 0:n]
                            rk = [("uext", c), "vec"]
                        else:
                            P.op("dve", I("tensor_copy", seq_ext[:, :, 0:3], sst[:, c, :, :]), [("sst", c)], ["seq_ext"])
                            P.op("act", I("activation", seq_ext[:, :, 3:3 + TS], s3(ps[b][:, 0:NS]), AF.Copy), [("ps", b)], ["seq_ext"])
                            P.op("dve", I("tensor_copy", sst[:, c, :, :], seq_ext[:, :, TS:TS + 3]), ["seq_ext"], [("sst", c)])
                            ex = [seq_ext[:, :, j:j + TS] for j in range(4)]
                            uo = s3(uT[:, c, 0:NS])
                            rk = ["seq_ext", "vec"]
                        P.op("dve", I("tensor_scalar", uo, ex[0], vcol("rcw0", c), vcol("rcb", c), ALU.mult, ALU.add), rk, [("u", c)])
                        for j in range(1, 4):
                            P.op("dve", I("scalar_tensor_tensor", uo, ex[j], vcol("rcw%d" % j, c), uo, ALU.mult, ALU.add), rk + [("u", c)], [("u", c)])
                        P.op("dve", I("tensor_copy", ubT[:, c, 0:n], uT[:, c, 0:n]), [("u", c)], [("ub", c)])
                        if not sample:
                            P.op("dve", I("tensor_copy", uext[:, c, 0:3], uext[:, c, n:n + 3]), [("uext", c), ("u", c)], [("uext", c)])
                for nb in range(4):
                    ia = wslot()
                    P.dma("pool", ws[ia][:, 0:2, 0:512], rwax[nb].rearrange("(kc p) n -> p kc n", p=128), [], [("ws", ia)], ("ws", ia))
                    sgt = None if phase1 else wload(rwg[:, nb * 256:(nb + 1) * 256], KC, 256)
                    if phase1:
                        ti = cur["g"] * 2 + o // 512
                        ur = [("ub", nb * 2), ("ub", nb * 2 + 1), ("ws", ia)]
                        st = []
                        for sub in range(2):
                            c = nb * 2 + sub
                            ba_ = bank(); bx_ = bank()
                            P.op("pe", MM([(ps[ba_][:, 0:n], ws[ia][:, k, sub * 128:(sub + 1) * 128], ubT[:, nb * 2 + k, 0:n], k == 0, k == 1) for k in range(2)]), ur, [("ps", ba_)])
                            P.op("pe", MM([(ps[bx_][:, 0:n], ws[ia][:, k, 256 + sub * 128:256 + (sub + 1) * 128], ubT[:, nb * 2 + k, 0:n], k == 0, k == 1) for k in range(2)]), ur, [("ps", bx_)])
                            (t1, k1), (t2, k2), (a_, ka), (m_, km) = p1buf(), p1buf(), p1buf(), p1buf()
                            P.op("act", I("activation", t1[:, 0:n], ps[ba_][:, 0:n], AF.Tanh, bias=cvec[:, 24 + c:25 + c], scale=0.5), [("ps", ba_), "cvec"], [k1])
                            P.op("act", I("activation", t2[:, 0:n], ps[bx_][:, 0:n], AF.Tanh, bias=cvec[:, 32 + c:33 + c], scale=0.5), [("ps", bx_), "cvec"], [k2])
                            P.op("act", I("activation", a_[:, 0:n], t1[:, 0:n], AF.Exp, bias=cvec[:, 16 + c:17 + c], scale=cvec[:, 16 + c:17 + c]), [k1, "cvec"], [ka])
                            P.op("act", I("activation", m_[:, 0:n], t1[:, 0:n], AF.Exp, bias=cvec[:, c:c + 1], scale=cvec[:, c:c + 1]), [k1, "cvec"], [km])
                            P.op("dve", I("scalar_tensor_tensor", t2[:, 0:n], t2[:, 0:n], 1.0, uT[:, c, 0:n], ALU.add, ALU.mult), [k2, ("u", c)], [k2])
                            st.append((c, t1, k1, t2, k2, a_, ka, m_, km))
                        for (c, t1, k1, t2, k2, a_, ka, m_, km) in st:
                            P.op("act", I("activation", m_[:, 0:n], m_[:, 0:n], AF.Sqrt, bias=onec[:, 0:1], scale=-1.0), [km, "onec"], [km])
                        for (c, t1, k1, t2, k2, a_, ka, m_, km) in st:
                            P.op("dve", I("scalar_tensor_tensor", t2[:, 0:n], t2[:, 0:n], 0.5, m_[:, 0:n], ALU.mult, ALU.mult), [k2, km], [k2])
                            P.dma("sp", abv(0, ti)[:, c, 0:n], a_[:, 0:n], [ka], [("abd", ti, c, 0)], ("st", ka), ms=True)
                            P.dma("sp", abv(1, ti)[:, c, 0:n], t2[:, 0:n], [k2], [("abd", ti, c, 1)], ("st", k2), ms=True)
                            if scan:
                                P.op("dve", I("tensor_tensor_scan", t1[:, 0:n], a_[:, 0:n], t2[:, 0:n], hcar[:, c:c + 1], ALU.mult, ALU.add), [ka, k2, ("hcar", c), k1], [k1])
                                P.op("dve", I("tensor_copy", hcar[:, c:c + 1], t1[:, n - 1:n]), [k1], [("hcar", c)])
                        continue
                    for sub in range(2):
                        c = nb * 2 + sub
                        ba_ = bank(); bx_ = bank(); bg_ = bank()
                        ur = [("ub", nb * 2), ("ub", nb * 2 + 1), ("ws", ia)]
                        P.op("pe", MM([(ps[ba_][:, 0:n], ws[ia][:, k, sub * 128:(sub + 1) * 128], ubT[:, nb * 2 + k, 0:n], k == 0, k == 1) for k in range(2)]), ur, [("ps", ba_)])
                        P.op("pe", MM([(ps[bx_][:, 0:n], ws[ia][:, k, 256 + sub * 128:256 + (sub + 1) * 128], ubT[:, nb * 2 + k, 0:n], k == 0, k == 1) for k in range(2)]), ur, [("ps", bx_)])
                        if not phase1:
                            proj_chunk(sgt, KC, sub * 128, hT, o, n, bg_, HK)
                        if phase1:
                            (r_, kr), (ig_, ki), (a_, ka), (m_, km) = p1buf(), p1buf(), p1buf(), p1buf()
                        else:
                            t = tmpi(); t2 = tmpi()
                            r_ = tmpA[t]; ig_ = tmpB[t]; a_ = tmpA[t2]; m_ = tmpB[t2]
                            kr, ki, ka, km = ("tA", t), ("tB", t), ("tA", t2), ("tB", t2)
                        P.op("act", I("activation", r_[:, 0:n], ps[ba_][:, 0:n], AF.Sigmoid, bias=vcol("rba", c), scale=1.0), [("ps", ba_), "vec"], [kr])
                        P.op("act", I("activation", ig_[:, 0:n], ps[bx_][:, 0:n], AF.Sigmoid, bias=vcol("rbx", c), scale=1.0), [("ps", bx_), "vec"], [ki])
                        P.op("act", I("activation", a_[:, 0:n], r_[:, 0:n], AF.Exp, scale=cvec[:, c:c + 1]), [kr, "cvec"], [ka])
                        P.op("act", I("activation", m_[:, 0:n], r_[:, 0:n], AF.Exp, scale=cvec[:, 8 + c:9 + c]), [kr, "cvec"], [km])
                        P.op("act", I("activation", m_[:, 0:n], m_[:, 0:n], AF.Sqrt, bias=onec[:, 0:1], scale=-1.0), [km, "onec"], [km])
                        P.op("dve", I("tensor_tensor", ig_[:, 0:n], ig_[:, 0:n], m_[:, 0:n], ALU.mult), [ki, km], [ki])
                        P.op("dve", I("tensor_tensor", ig_[:, 0:n], ig_[:, 0:n], uT[:, c, 0:n], ALU.mult), [ki, ("u", c)], [ki])
                        hs_ = r_
                        if phase1:
                            ti = cur["g"] * 2 + o // 512
                            P.dma("sp", abv(0, ti)[:, c, 0:n], a_[:, 0:n], [ka], [("abd", ti, c, 0)], ("st", ka), ms=True)
                            P.dma("sp", abv(1, ti)[:, c, 0:n], ig_[:, 0:n], [ki], [("abd", ti, c, 1)], ("st", ki), ms=True)
                        if not sample and scan:
                            P.op("dve", I("tensor_tensor_scan", hs_[:, 0:n], a_[:, 0:n], ig_[:, 0:n], hcar[:, c:c + 1], ALU.mult, ALU.add),
                                 [ka, ki, ("hcar", c), kr], [kr])
                            P.op("dve", I("tensor_copy", hcar[:, c:c + 1], hs_[:, n - 1:n]), [kr], [("hcar", c)])
                        elif sample:
                            a3 = s3(a_[:, 0:NS]); b3 = s3(ig_[:, 0:NS])
                            P.op("dve", I("tensor_tensor", seq_ext[:, :, 0:1], a3[:, :, 0:1], shs[:, c, :].unsqueeze(2), ALU.mult), [ka, ("shs", c)], ["seq_ext"])
                            P.op("dve", I("tensor_tensor", b3[:, :, 0:1], b3[:, :, 0:1], seq_ext[:, :, 0:1], ALU.add), ["seq_ext", ki], [ki])
                            P.op("dve", I("memset", a3[:, :, 0:1], 0.0), [ka], [ka])
                            P.op("dve", I("tensor_tensor_scan", hs_[:, 0:NS], a_[:, 0:NS], ig_[:, 0:NS], 0.0, ALU.mult, ALU.add),
                                 [ka, ki, kr], [kr])
                            P.op("dve", I("tensor_copy", shs[:, c, :].unsqueeze(2), s3(hs_[:, 0:NS])[:, :, TS - 1:TS]), [kr], [("shs", c)])
                        if not phase1:
                            P.op("act", I("activation", a_[:, 0:n], ps[bg_][:, 0:n], AF.Gelu), [("ps", bg_), ka], [ka])
                            P.op("dve", I("tensor_tensor", mixT[:, c, o:o + n], hs_[:, 0:n], a_[:, 0:n], ALU.mult), [kr, ka], [("mix", c)])
            if not sample:
                for c in range(KC):
                    P.op("dve", I("tensor_copy", ucar[:, c, :], uext[:, c, 0:3]), [("uext", c)], [("ucar", c)])
            if phase1:
                return
            if last:
                if not sample:
                    P.dma("sp", o_hp[:, :], hcar[:], [("hcar", c) for c in range(KC)], [], "o_h")
                    P.dma("sp", o_rcp[:, :, :], ucar[:], [("ucar", c) for c in range(KC)], [], "o_rcp")
                else:
                    P.dma("sp", o_hs[:, :, :], shs[:], [("shs", c) for c in range(KC)], [], "o_hs")
                    P.dma("sp", o_rcs[:, :, :, :], sst[:], SK, [], "o_rcs")
            out_proj_add(rwo, tiles)

        ab_scr = nc.dram_tensor("ab_scr", [2, 2 * NG, 128, KC * 512], F32)

        def abv(which, ti):
            return ab_scr[which, ti].rearrange("p (c t) -> p c t", c=KC)
        NPB = 8
        pa = [msv(i * 512, 512) for i in range(NPB)]
        pb = [msv((NPB + i) * 512, 512) for i in range(NPB)]
        pq = [0]

        def rglru_p2(li, tiles, last):
            P.barrier()
            rmsnorm("nm%d" % li, tiles)
            for (o, n) in tiles:
                ti = cur["g"] * 2 + o // 512
                for nb in range(4):
                    sgt = wload(rwg[:, nb * 256:(nb + 1) * 256], KC, 256)
                    for sub in range(2):
                        c = nb * 2 + sub
                        bg_ = bank()
                        proj_chunk(sgt, KC, sub * 128, hT, o, n, bg_, HK)
                        i = pq[0] % NPB
                        pq[0] += 1
                        P.dma("sp", pa[i][:, 0:n], abv(0, ti)[:, c, 0:n], [("abd", ti, c, 0)], [("pa", i)], ("ld_a", i), ms=True)
                        P.dma("sp", pb[i][:, 0:n], abv(1, ti)[:, c, 0:n], [("abd", ti, c, 1)], [("pb", i)], ("ld_b", i), ms=True)
                        t = tmpi()
                        hs_ = tmpA[t]; g_ = tmpB[t]
                        P.op("act", I("activation", g_[:, 0:n], ps[bg_][:, 0:n], AF.Gelu), [("ps", bg_)], [("tB", t)])
                        P.op("dve", I("tensor_tensor_scan", hs_[:, 0:n], pa[i][:, 0:n], pb[i][:, 0:n], hcar[:, c:c + 1], ALU.mult, ALU.add),
                             [("pa", i), ("pb", i), ("hcar", c)], [("tA", t)])
                        P.op("dve", I("tensor_copy", hcar[:, c:c + 1], hs_[:, n - 1:n]), [("tA", t)], [("hcar", c)])
                        P.op("dve", I("tensor_tensor", mixT[:, c, o:o + n], hs_[:, 0:n], g_[:, 0:n], ALU.mult), [("tA", t), ("tB", t)], [("mix", c)])
            if last:
                P.dma("sp", o_hp[:, :], hcar[:], [("hcar", c) for c in range(KC)], [], "o_h")
            out_proj_add(rwo, tiles)

        def attn(li, j, tiles, sample, first, last, tok0):
            P.barrier()
            cosT, sinT, kf, kpad, vpad = AV_S if sample else AV_P
            rmsnorm("nm%d" % li, tiles)
            ntok = sum(n for _, n in tiles)
            if not sample:
                P.dma("sp", cosT[:, 0:ntok], cosP[:, tok0:tok0 + ntok], [], ["cos"], "cs_cos", ms=True)
                P.dma("sp", sinT[:, 0:ntok], sinP[:, tok0:tok0 + ntok], [], ["sin"], "cs_sin", ms=True)
            else:
                P.dma("sp", cosT[:, 0:ntok], cosS[:, :], [], ["cos"], "cs_cos", ms=True)
                P.dma("sp", sinT[:, 0:ntok], sinS[:, :], [], ["sin"], "cs_sin", ms=True)
            P.dma("sp", bvt[:], bvb[j], [], ["bvt"], "cs_bvt")
            for g in range(2):
                P.op("dve", I("memset", kpad[g][:], 0.0), [], ["kbuf"])
                P.op("dve", I("memset", vpad[g][:].rearrange("p a b -> p (a b)"), 0.0), [], [("vpad", g)])
                if sample:
                    P.op("dve", I("memset", kcp[g][:].rearrange("p a b -> p (a b)"), 0.0), [], ["kcT"])
                    P.op("dve", I("memset", vcp[g][:].rearrange("p a b -> p (a b)"), 0.0), [], [("vcp", g)])
            if not sample:
                for g in range(2):
                    P.op("dve", I("tensor_copy", kpad[g][:, 0:128], kcar[j][g][:]), [("kcar", j, g)], ["kbuf"])
                    P.op("dve", I("tensor_copy", vpad[g][:, 0, :], vcar[j][g][:]), [("vcar", j, g)], [("vpad", g)])
            for (c0, ncol) in [(0, 512), (512, 512), (1024, 128)]:
                s = wload(wqkv[j][:, c0:c0 + ncol], KC, ncol)
                for (o, n) in tiles:
                    for cc in range(ncol // 128):
                        c = c0 // 128 + cc
                        b = bank()
                        proj_chunk(s, KC, cc * 128, hT, o, n, b, HK)
                        A_, B_, C_, kA, kB, kC = prj_ring()
                        bias = vcol("bq%d" % j, c) if c < 8 else vcol("bk%d" % j, 0)
                        P.op("act", I("activation", A_[:, 0:n], ps[b][:, 0:n], AF.Identity, bias=bias, scale=1.0), [("ps", b), "vec"], [kA])
                        P.op("act", I("activation", C_[:, 0:n], A_[:, 0:n], AF.Copy), [kA], [kC])
                        b2 = bank()
                        P.op("pe", I("matmul", ps[b2][:, 0:n], rot[:], C_[:, 0:n], start=True, stop=True), [kC, "rot"], [("ps", b2)])
                        P.op("dve", I("tensor_tensor", A_[:, 0:n], A_[:, 0:n], cosT[:, o:o + n], ALU.mult), [kA, "cos"], [kA])
                        P.op("dve", I("tensor_tensor", B_[:, 0:n], ps[b2][:, 0:n], sinT[:, o:o + n], ALU.mult), [("ps", b2), "sin"], [kB])
                        if c < 8:
                            P.op("dve", I("tensor_tensor", qT[:, c, o:o + n], A_[:, 0:n], B_[:, 0:n], ALU.add), [kA, kB], [("q", c)])
                        else:
                            P.op("dve", I("tensor_tensor", kf[:, o:o + n], A_[:, 0:n], B_[:, 0:n], ALU.add), [kA, kB], ["kf"])
                            for g in range(2):
                                P.op("act", I("activation", kpad[g][g * 64:(g + 1) * 64, 128 + o:128 + o + n], kf[g * 64:(g + 1) * 64, o:o + n], AF.Copy), ["kf"], ["kbuf"])
            sv = wload(wqkv[j][:, 1152:1280], KC, 128)
            nblk = ntok // 128
            for bi in range(nblk):
                b = bank()
                P.op("pe", MM([(ps[b][:, 0:128], hT[:, k, bi * 128:(bi + 1) * 128], ws[sv][:, k, 0:128], k == 0, k == KC - 1) for k in range(KC)]),
                     [("ws", sv)] + HK, [("ps", b)])
                P.op("dve", I("tensor_tensor", vf[:], ps[b][:, 0:128], bvt[:], ALU.add), [("ps", b), "bvt"], ["vf"])
                for g in range(2):
                    P.op("act", I("activation", vpad[g][:, 1 + bi, g * 64:(g + 1) * 64], vf[:, g * 64:(g + 1) * 64], AF.Copy), ["vf"], [("vpad", g)])
                if last and bi == nblk - 1:
                    P.dma("sp", (o_vs if sample else o_vp)[j], vf[:], ["vf"], [], ("o_v", sample, j))
            if last:
                P.dma("sp", (o_ks if sample else o_kp)[j], kf[:, ntok - 128:ntok], ["kf"], [], ("o_k", sample, j), ms=True)
            if sample:
                for g in range(2):
                    P.dma("pool", kcp[g][g * 64:(g + 1) * 64, :, :], ckT[j][g * 64:(g + 1) * 64, :, :], [], ["kcT"], "kc", ms=True)
                for g in range(2):
                    P.dma("pool", vcp[g][:, :, g * 64:(g + 1) * 64], cvN[j].rearrange("b k f -> k b f")[:, :, g * 64:(g + 1) * 64], [], [("vcp", g)], ("vc", g), ms=True)
                P.dma("sp", o_ksh[j], ckN[j][:, 8:128, :], [], [], "o_shk")
                P.dma("sp", o_vsh[j], cvN[j][:, 8:128, :], [], [], "o_sh")
            sinkn = "sink%d" % j
            if not sample:
                def v4(ap):
                    return ap.rearrange("p (a b) -> p a b", a=4)
                for bi in range(nblk):
                    msk, mr = (maskF, "maskF") if (first and bi == 0) else (maskP, "maskP")
                    for hh in range(2):
                        cs = [4 * hh + i for i in range(4)]
                        qv = qT[:, 4 * hh:4 * hh + 4, bi * 128:(bi + 1) * 128]
                        pts = []
                        for kb in range(2):
                            for g in range(2):
                                i4 = kb * 2 + g
                                bs = bank()
                                mb = msk[:, i4 * 128:(i4 + 1) * 128].unsqueeze(1).broadcast_to([128, 4, 128])
                                P.op("pe", MM([(v4(ps[bs][:, :]), kpad[g][:, (bi + kb) * 128:(bi + kb + 1) * 128], qv, True, False),
                                               (v4(ps[bs][:, :]), ident[:], mb, False, True)]),
                                     [("q", c) for c in cs] + ["kbuf", mr, "ident"], [("ps", bs)])
                                pi = pn[0] % 4
                                pn[0] += 1
                                P.op("act", I("activation", pTb[pi][:, :], ps[bs][:, :], AF.Exp, scale=0.125), [("ps", bs)], [("pT", pi)])
                                pts.append((pi, kb, g))
                        bo = bank(); bd = bank()
                        lo = []; ld = []
                        for n_, (pi, kb, g) in enumerate(pts):
                            lo.append((ps[bo][:, :], vpad[g][:, bi + kb, :], pTb[pi][:, :], n_ == 0, n_ == 3))
                            ld.append((ps[bd][:, :], onesh[g][:], pTb[pi][:, :], n_ == 0, n_ == 3))
                        rk = [("pT", pi) for (pi, _, _) in pts]
                        P.op("pe", MM(lo), rk + [("vpad", 0), ("vpad", 1)], [("ps", bo)])
                        P.op("pe", MM(ld), rk + [("onesh", 0), ("onesh", 1)], [("ps", bd)])
                        t2 = tmpi()
                        dn = tmpA[t2]
                        sk = vec[:, VEC[sinkn] + 4 * hh:VEC[sinkn] + 4 * hh + 4].unsqueeze(2).broadcast_to([128, 4, 128])
                        P.op("dve", I("tensor_tensor", v4(dn[:, :]), v4(ps[bd][:, :]), sk, ALU.add), [("ps", bd), "vec"], [("tA", t2)])
                        P.op("dve", I("reciprocal", dn[:, :], dn[:, :]), [("tA", t2)], [("tA", t2)])
                        P.op("dve", I("tensor_tensor", mixT[:, 4 * hh:4 * hh + 4, bi * 128:(bi + 1) * 128], v4(ps[bo][:, :]), v4(dn[:, :]), ALU.mult),
                             [("tA", t2), ("ps", bo)], [("mix", c) for c in cs] + P1K)
            for bi in range(nblk if sample else 0):
                for c in range(KC):
                    bs = bank()
                    qr = [("q", c), "kbuf"]
                    lst = []
                    for g in range(2):
                        lst.append((ps[bs][:, g * 128:(g + 1) * 128], kpad[g][:, 128:256], qT[:, c, 0:128], True, True))
                    for g in range(2):
                        for sq in range(NSEQ):
                            c0_ = 256 + g * 128 + sq * TS
                            lst.append((ps[bs][:, c0_:c0_ + TS], kcp[g][:, sq, :], qT[:, c, sq * TS:(sq + 1) * TS], True, True))
                    P.op("pe", MM(lst), qr + ["kcT"], [("ps", bs)])
                    msk, mr = maskS, "maskS"
                    t = tmpi()
                    pT = tmpC[t]
                    P.op("act", I("activation", pT[:, :], ps[bs][:, :], AF.Exp, scale=0.125), [("ps", bs)], [("tC", t)])
                    P.op("dve", I("tensor_tensor", pT[:, :], pT[:, :], msk[:, :], ALU.mult), [("tC", t), mr], [("tC", t)])
                    bo = bank(); bd = bank()
                    lo = []; ld = []
                    for g in range(2):
                        lo.append((ps[bo][:, 0:128], vpad[g][:, 1, :], pT[:, g * 128:(g + 1) * 128], g == 0, g == 1))
                        ld.append((ps[bd][:, 0:128], onesh[g][:], pT[:, g * 128:(g + 1) * 128], g == 0, g == 1))
                    for sq in range(NSEQ):
                        for g in range(2):
                            c0_ = 256 + g * 128 + sq * TS
                            lo.append((ps[bo][:, 128 + sq * TS:128 + (sq + 1) * TS], vcp[g][:, sq, :], pT[:, c0_:c0_ + TS], g == 0, g == 1))
                            ld.append((ps[bd][:, 128 + sq * TS:128 + (sq + 1) * TS], onesh[g][:], pT[:, c0_:c0_ + TS], g == 0, g == 1))
                    P.op("pe", MM(lo), [("tC", t), ("vpad", 0), ("vpad", 1), ("vcp", 0), ("vcp", 1)], [("ps", bo)])
                    P.op("pe", MM(ld), [("tC", t), ("onesh", 0), ("onesh", 1)], [("ps", bd)])
                    t2 = tmpi()
                    dn = tmpA[t2]; on = tmpB[t2]
                    P.op("dve", I("tensor_scalar", dn[:, 0:128], ps[bd][:, 0:128], vcol(sinkn, c), None, ALU.add), [("ps", bd), "vec"], [("tA", t2)])
                    P.op("dve", I("tensor_tensor", dn[:, 0:128], dn[:, 0:128], ps[bd][:, 128:256], ALU.add), [("ps", bd), ("tA", t2)], [("tA", t2)])
                    P.op("act", I("activation", on[:, 0:128], ps[bo][:, 0:128], AF.Copy), [("ps", bo)], [("tB", t2)])
                    P.op("dve", I("tensor_tensor", on[:, 0:128], on[:, 0:128], ps[bo][:, 128:256], ALU.add), [("ps", bo), ("tB", t2)], [("tB", t2)])
                    P.op("dve", I("reciprocal", dn[:, 0:128], dn[:, 0:128]), [("tA", t2)], [("tA", t2)])
                    P.op("dve", I("tensor_tensor", mixT[:, c, 0:128], on[:, 0:128], dn[:, 0:128], ALU.mult), [("tA", t2), ("tB", t2)], [("mix", c)] + P1K)
            if not sample:
                for g in range(2):
                    P.op("dve", I("tensor_copy", kcar[j][g][:], kpad[g][:, 8 * 128:9 * 128]), ["kbuf"], [("kcar", j, g)])
                    P.op("dve", I("tensor_copy", vcar[j][g][:], vpad[g][:, 8, :]), [("vpad", g)], [("vcar", j, g)])
            out_proj_add(wo[j], tiles, "bo%d" % j)

        PAIRS = [[0, 1], [2, 3], [4, 5], [6, 7]]
        xch = {}

        def exchange(idx, F):
            cin = nc.dram_tensor("cc_in%d" % idx, [128, F], F32)
            cout = nc.dram_tensor("cc_out%d" % idx, [128, F], F32)
            P.op("dve", I("tensor_scalar", xsend[:, 0:F], xsend[:, 0:F], flg[:, 0:1], None, ALU.mult), ["xsend", "flg"], ["xsend"])
            P.dma("sp", cin[:, :], xsend[:, 0:F], ["xsend"], [("cin", idx)], ("cin", idx))
            P.cc(lambda e: e.collective_compute("AllReduce", ALU.add, replica_groups=PAIRS, ins=[cin.ap().opt()], outs=[cout.ap().opt()]),
                 [("cin", idx)], [("cout", idx)], ("cc", idx))
            xch[idx] = (cout, F)

        def receive(idx):
            cout, F = xch[idx]
            P.dma("sp", xrecv[:, 0:F], cout[:, :], [("cout", idx)], ["xrecv"], ("rcv", idx))

        def kv_block(li, j, o):
            P.barrier()
            rmsnorm("nm%d" % li, [(o, 128)])
            n = 128
            s_ = wload(wqkv[j][:, 1024:1152], KC, 128)
            b = bank()
            proj_chunk(s_, KC, 0, hT, o, n, b, HK)
            t = tmpi()
            P.op("act", I("activation", tmpA[t][:, 0:n], ps[b][:, 0:n], AF.Identity, bias=vcol("bk%d" % j, 0), scale=1.0), [("ps", b), "vec"], [("tA", t)])
            P.op("act", I("activation", tmpC[t][:, 0:n], tmpA[t][:, 0:n], AF.Copy), [("tA", t)], [("tC", t)])
            b2 = bank()
            P.op("pe", I("matmul", ps[b2][:, 0:n], rot[:], tmpC[t][:, 0:n], start=True, stop=True), [("tC", t), "rot"], [("ps", b2)])
            P.op("dve", I("tensor_tensor", tmpA[t][:, 0:n], tmpA[t][:, 0:n], cosT[:, o:o + n], ALU.mult), [("tA", t), "cos"], [("tA", t)])
            P.op("dve", I("tensor_tensor", tmpB[t][:, 0:n], ps[b2][:, 0:n], sinT[:, o:o + n], ALU.mult), [("ps", b2), "sin"], [("tB", t)])
            P.op("dve", I("tensor_tensor", xsend[:, 0:128], tmpA[t][:, 0:n], tmpB[t][:, 0:n], ALU.add), [("tA", t), ("tB", t)], ["xsend"])
            sv = wload(wqkv[j][:, 1152:1280], KC, 128)
            b = bank()
            P.op("pe", MM([(ps[b][:, 0:128], hT[:, k, o:o + 128], ws[sv][:, k, 0:128], k == 0, k == KC - 1) for k in range(KC)]), [("ws", sv)] + HK, [("ps", b)])
            P.op("dve", I("tensor_tensor", xsend[:, 128:256], ps[b][:, 0:128], bvt[:], ALU.add), [("ps", b), "bvt"], ["xsend"])

        def set_carry(j, src, skey):
            for g in range(2):
                P.op("act", I("activation", kcar[j][g][g * 64:(g + 1) * 64, :], src[g * 64:(g + 1) * 64, 0:128], AF.Copy), [skey], [("kcar", j, g)])
                P.op("act", I("activation", vcar[j][g][:, g * 64:(g + 1) * 64], src[:, 128 + g * 64:128 + (g + 1) * 64], AF.Copy), [skey], [("vcar", j, g)])

        def pre_sconv(li, o, dest, dkeys):
            P.barrier()
            rmsnorm("nm%d" % li, [(o, 128)])
            n = 128
            for c in range(KC):
                i = wslot()
                P.dma("pool", ws[i][:, :, 0:256], swi[:, c * 384 + 128:(c + 1) * 384].rearrange("(kc p) n -> p kc n", p=128), [], [("ws", i)], ("ws", i))
                bc = bank(); bx = bank()
                proj_chunk(i, KC, 0, hT, o, n, bc, HK)
                proj_chunk(i, KC, 128, hT, o, n, bx, HK)
                t = tmpi()
                P.op("act", I("activation", tmpA[t][:, 0:n], ps[bc][:, 0:n], AF.Copy), [("ps", bc)], [("tA", t)])
                P.op("dve", I("tensor_tensor", dest[:, 2 * c:2 * c + 2], tmpA[t][:, n - 2:n], ps[bx][:, n - 2:n], ALU.mult), [("tA", t), ("ps", bx)], dkeys(c))

        PT = [(0, 512), (512, 512)]
        ST = [(0, NS)]
        PST = PT + [(GT, NS)]
        def setg(g):
            cur["g"] = g
            cur["xb"] = {0: 0, "s": GT, "s1": GT, 1: GT + NS, "h": 0}[g]

        def load_x(src, tok0, tiles):
            o0 = tiles[0][0]
            n_all = sum(n for _, n in tiles)
            if cur["g"] == "h":
                dst = xhv[:, :, o0:o0 + n_all]
            else:
                dst = xT[:, :, cur["xb"] + o0:cur["xb"] + o0 + n_all]
            P.dma("sp", dst, src[:, tok0 + o0:tok0 + o0 + n_all].rearrange("(c p) t -> p c t", p=128), [], [XKc(c) for c in range(KC)], ("xin", cur["g"]), ms=(cur["g"] == "h"))

        setg("h"); load_x(xhT, 0, [(0, 128)])
        for g in range(NG):
            setg(g); load_x(xpT, g * GT, PT)
        setg("s"); load_x(xsT, 0, ST)
        CK = lambda nm: [(nm, c) for c in range(KC)]
        P.op("dve", I("memset", zcar[:].rearrange("p a b -> p (a b)"), 0.0), [], CK("zcar"))
        P.op("dve", I("memset", hcar[:], 0.0), [], CK("hcar"))
        P.op("dve", I("memset", ucar[:].rearrange("p a b -> p (a b)"), 0.0), [], CK("ucar"))
        LG = NG - 1
        setg("h")
        P.barrier()
        P.dma("sp", cosT[:, 0:128], cosH[:, :], [], ["cos"], "cs_cos", ms=True)
        P.dma("sp", sinT[:, 0:128], sinH[:, :], [], ["sin"], "cs_sin", ms=True)
        P.dma("sp", bvt[:], bvb[0], [], ["bvt"], "cs_bvt")
        kv_block(0, 0, 0)
        set_carry(0, xsend, "xsend")
        setg("s"); attn(0, 0, ST, True, False, True, 0)
        setg(0); attn(0, 0, PT, False, True, False, 0)
        ffn(0, PST)
        setg(1); attn(0, 0, PT, False, False, True, GT); ffn(0, PT)
        for g in range(NG):
            setg(g); rglru(1, PT, False, False, phase1=True)
        P.dma("sp", o_rcp[:, :, :], ucar[:], CK("ucar"), [], "o_rcp")
        P.op("dve", I("tensor_copy", xsend[:, 0:8], hcar[:]), CK("hcar"), ["xsend"])
        P.op("dve", I("tensor_copy", xsend[:, 8:32], ucar[:].rearrange("p a b -> p (a b)")), CK("ucar"), ["xsend"])
        exchange(0, 32)
        setg("s"); rglru(1, ST, True, True)
        receive(0)
        P.op("dve", I("tensor_scalar", hcar[:], xrecv[:, 0:8], flg[:, 1:2], None, ALU.mult), ["xrecv", "flg"], CK("hcar"))
        P.op("dve", I("tensor_scalar", ucar[:].rearrange("p a b -> p (a b)"), xrecv[:, 8:32], flg[:, 1:2], None, ALU.mult), ["xrecv", "flg"], CK("ucar"))
        setg(0); rglru(1, [(0, 128)], False, False, phase1=True, scan=False)
        setg(0); rglru_p2(1, PT, False); ffn(1, PST)
        setg(1); rglru_p2(1, PT, True); ffn(1, PT)
        S1T = [(0, NS), (NS, 512), (NS + 512, 512)]
        setg(1); pre_sconv(2, GT - 128, xsend, lambda c: ["xsend"])
        exchange(1, 16)
        setg(0); pre_sconv(2, GT - 128, zcar[:].rearrange("p a b -> p (a b)"), lambda c: [("zcar", c)])
        setg("s1"); sconv(2, S1T, False, True, stiles=(0,)); ffn(2, S1T)
        setg(LG)
        P.barrier()
        P.dma("sp", cosT[:, GT - 128:GT], cosP[:, TP - 128:TP], [], ["cos"], "cs_cos", ms=True)
        P.dma("sp", sinT[:, GT - 128:GT], sinP[:, TP - 128:TP], [], ["sin"], "cs_sin", ms=True)
        P.dma("sp", bvt[:], bvb[1], [], ["bvt"], "cs_bvt")
        kv_block(3, 1, GT - 128)
        exchange(2, 256)
        receive(1)
        P.op("dve", I("tensor_scalar", zcar[:].rearrange("p a b -> p (a b)"), xrecv[:, 0:16], flg[:, 1:2], None, ALU.mult), ["xrecv", "flg"], CK("zcar"))
        setg(0); sconv(2, PT, False, False); ffn(2, PT)
        setg("s"); attn(3, 1, ST, True, False, True, 0)
        receive(2)
        set_carry(1, xrecv, "xrecv")
        setg(0); attn(3, 1, PT, False, True, False, 0); ffn(3, PST)
        rmsnorm_final(PT, ypT, 0)
        setg("s"); rmsnorm_final(ST, ysT, 0)
        setg(1); attn(3, 1, PT, False, False, True, GT); ffn(3, PT); rmsnorm_final(PT, ypT, GT)
        P.final_wait("sp")
        block = es.enter_context(nc.Block())
        P.emit(block)
    return nc


def _fm(v):
    return np.ascontiguousarray(np.asarray(v, np.float32).reshape(8, 128).T)


def _host_consts(TP, start, half):
    hd = 32
    inv = (np.float32(10000.0) ** (-(np.arange(hd, dtype=np.float32)) / np.float32(hd))).astype(np.float32)

    def tables(pos):
        ang = (pos.astype(np.float32)[:, None] * inv[None, :]).astype(np.float32)
        cos = np.cos(ang.astype(np.float64)).astype(np.float32).T
        sin = np.sin(ang.astype(np.float64)).astype(np.float32).T
        return np.ascontiguousarray(np.tile(cos, (4, 1))), np.ascontiguousarray(np.tile(sin, (4, 1)))
    cosP, sinP = tables(start + np.arange(TP))
    cosH, sinH = tables(np.maximum(start - 128 + np.arange(128), 0))
    cosS, sinS = tables(PAST + (np.arange(NS) % TS))
    rot = np.zeros((128, 128), np.float32)
    for blk in range(2):
        for d in range(64):
            m = blk * 64 + d
            if d < 32:
                rot[blk * 64 + d + 32, m] = -1.0
            else:
                rot[blk * 64 + d - 32, m] = 1.0
    k = np.arange(128)[:, None]
    q = np.arange(128)[None, :]
    prev = (k > q).astype(np.float32)
    cur = (k <= q).astype(np.float32)
    zero = np.zeros_like(prev)
    NEG = np.float32(-30000.0)
    maskP = (np.concatenate([prev, prev, cur, cur], axis=1) - 1.0) * (-NEG)
    maskF = maskP if half else (np.concatenate([zero, zero, cur, cur], axis=1) - 1.0) * (-NEG)
    mnew = ((k // TS == q // TS) & (k % TS <= q % TS)).astype(np.float32)
    mc = (k > (q % TS)).astype(np.float32)
    maskS = np.concatenate([mnew, mnew, mc, mc], axis=1)
    cst = np.ascontiguousarray(np.concatenate([rot, maskP, maskF, maskS, np.eye(128, dtype=np.float32)], axis=1).astype(np.float32))
    flags = np.zeros((128, 2), np.float32)
    flags[:, 0] = 1.0 - half
    flags[:, 1] = float(half)
    return dict(cosP=cosP, sinP=sinP, cosH=cosH, sinH=sinH, cosS=cosS, sinS=sinS, cst=cst, flags=flags)


def _prep_shared(inp):
    f = lambda a: np.ascontiguousarray(np.asarray(a, np.float32))
    vec = np.zeros((128, NV), np.float32)

    def put(name, v):
        vec[:, VEC[name]:VEC[name] + 8] = _fm(v)
    for i in range(4):
        put("nm%d" % i, inp["norm_mixer"][i]); put("nf%d" % i, inp["norm_ffn"][i])
    put("nfin", inp["norm_final"])
    perm = np.concatenate([np.arange((g * 8 + c) * 64, (g * 8 + c) * 64 + 64) for c in range(8) for g in range(2)])
    wq = np.asarray(inp["attn_w_qkv"], np.float32)
    bq = np.asarray(inp["attn_b_qkv"], np.float32)
    wqkv = f(np.concatenate([wq[:, :, perm], wq[:, :, 1024:]], axis=2))
    wo = f(np.asarray(inp["attn_w_o"], np.float32)[:, perm, :])
    bvb = np.zeros((2, 128, 128), np.float32)
    for j in range(2):
        put("bq%d" % j, bq[j][perm])
        vec[:, VEC["bk%d" % j]] = bq[j][1024:1152]
        put("bo%d" % j, inp["attn_b_o"][j])
        sk = np.asarray(inp["attn_sinks"], np.float32)[j]
        for c in range(8):
            vec[0:64, VEC["sink%d" % j] + c] = sk[c]
            vec[64:128, VEC["sink%d" % j] + c] = sk[c + 8]
        bvb[j] = np.broadcast_to(bq[j][1152:1280][None, :], (128, 128))
    for jj in range(4):
        put("rcw%d" % jj, inp["rglru_conv_w"][0][jj])
    put("rcb", inp["rglru_conv_b"][0]); put("rba", inp["rglru_ba"][0]); put("rbx", inp["rglru_bx"][0]); put("rlam", inp["rglru_lambda"][0])
    for jj in range(3):
        put("scw%d" % jj, inp["sconv_conv_w"][0][jj])
    sh = dict(vec=vec, bvb=bvb, wqkv=wqkv, wo=wo,
              rwg=f(inp["rglru_w_gate"][0]), rwi=f(inp["rglru_w_in"][0]), rwax=f(np.concatenate([np.asarray(inp["rglru_wa"][0], np.float32), np.asarray(inp["rglru_wx"][0], np.float32)], axis=2)),
              rwo=f(inp["rglru_w_out"][0]), swi=f(np.asarray(inp["sconv_w_in"][0], np.float32).reshape(D, 3, 8, 128).transpose(0, 2, 1, 3).reshape(D, 3 * D)), swo=f(inp["sconv_w_out"][0]),
              fwg=f(inp["ffn_w_gate"]), fwu=f(inp["ffn_w_up"]), fwd=f(inp["ffn_w_down"]))
    return sh


def run(inp, SEQ=4096, n_cores=8):
    TP = SEQ // 2
    nc = build(TP)
    shared = _prep_shared(inp)
    f = lambda a: np.ascontiguousarray(np.asarray(a, np.float32))
    xp = np.asarray(inp["x_prompt"], np.float32); xs = np.asarray(inp["x_sample"], np.float32)
    ck = np.asarray(inp["cache_k"], np.float32).reshape(2, 128, 128, 128)
    cv = np.asarray(inp["cache_v"], np.float32).reshape(2, 128, 128, 128)
    srh = np.asarray(inp["state_rglru_h"], np.float32)[0]
    src = np.asarray(inp["state_rglru_conv"], np.float32)[0]
    ssc = np.asarray(inp["state_shortconv"], np.float32)[0]
    in_maps = []
    for c in range(n_cores):
        seq, half = c // 2, c % 2
        start = half * TP
        b0 = c * NSEQ
        m = dict(shared)
        m.update(_host_consts(TP, start, half))
        m["xpT"] = f(xp[seq, start:start + TP].T)
        m["xhT"] = f(xp[seq, start - 128:start].T) if half else np.zeros((D, 128), np.float32)
        m["xsT"] = f(xs[b0:b0 + NSEQ].reshape(NS, D).T)
        m["ckN"] = f(ck[:, b0:b0 + NSEQ]); m["cvN"] = f(cv[:, b0:b0 + NSEQ])
        m["ckT"] = f(ck[:, b0:b0 + NSEQ].transpose(0, 3, 1, 2))
        m["sh"] = f(srh[b0:b0 + NSEQ].reshape(NSEQ, 8, 128).transpose(2, 1, 0))
        m["src"] = f(src[b0:b0 + NSEQ].reshape(NSEQ, 3, 8, 128).transpose(3, 2, 0, 1))
        m["ssc"] = f(ssc[b0:b0 + NSEQ].reshape(NSEQ, 2, 8, 128).transpose(3, 2, 0, 1))
        in_maps.append(m)
    res = run_bass_kernel_spmd(nc, in_maps, core_ids=list(range(n_cores)))
    R = res.results
    NB = n_cores * NSEQ
    nP = n_cores // 2
    y_p = np.stack([np.concatenate([R[2 * s]["ypT"].T, R[2 * s + 1]["ypT"].T], axis=0) for s in range(nP)])
    y_s = np.concatenate([R[c]["ysT"].T.reshape(NSEQ, TS, D) for c in range(n_cores)])
    L = lambda s: R[2 * s + 1]
    kp = np.stack([np.stack([L(s)["o_kp"][j].T.reshape(128, 2, 64) for s in range(nP)]) for j in range(2)])
    vp = np.stack([np.stack([L(s)["o_vp"][j].reshape(128, 2, 64) for s in range(nP)]) for j in range(2)])

    def samp(sh_name, new_name, transpose):
        out = np.zeros((2, NB, 128, 128), np.float32)
        for j in range(2):
            for c in range(n_cores):
                out[j, c * NSEQ:(c + 1) * NSEQ, 0:120] = R[c][sh_name][j]
                nw = R[c][new_name][j]
                nw = nw.T if transpose else nw
                out[j, c * NSEQ:(c + 1) * NSEQ, 120:128] = nw.reshape(NSEQ, TS, 128)
        return out.reshape(2, NB, 128, 2, 64)
    ks = samp("o_ksh", "o_ks", True)
    vs = samp("o_vsh", "o_vs", False)
    hp = np.stack([L(s)["o_hp"].T.reshape(D) for s in range(nP)])[None]
    hs = np.concatenate([R[c]["o_hs"].transpose(2, 1, 0).reshape(NSEQ, D) for c in range(n_cores)])[None]
    rcp = np.stack([L(s)["o_rcp"].transpose(2, 1, 0).reshape(3, D) for s in range(nP)])[None]
    rcs = np.concatenate([R[c]["o_rcs"].transpose(2, 3, 1, 0).reshape(NSEQ, 3, D) for c in range(n_cores)])[None]
    scp = np.stack([L(s)["o_scp"].transpose(2, 1, 0).reshape(2, D) for s in range(nP)])[None]
    scs = np.concatenate([R[c]["o_scs"].transpose(2, 3, 1, 0).reshape(NSEQ, 2, D) for c in range(n_cores)])[None]
    outs = (y_p, y_s, kp, vp, ks, vs, hp, hs, rcp, rcs, scp, scs)
    return tuple(np.ascontiguousarray(o.astype(np.float32)) for o in outs)


def kernel(**inputs):
    return run(inputs, 4096, 8)
```

```python
import numpy as np
from contextlib import ExitStack
import concourse.bass as bass
import concourse.mybir as mybir
from concourse.bass_utils import run_bass_kernel_spmd

F32 = mybir.dt.float32
BF16 = mybir.dt.bfloat16
AF = mybir.ActivationFunctionType
ALU = mybir.AluOpType

D = 1024
KC = 8
DFF = 2816
GT = 1024
NSEQ = 16
TS = 8
NS = NSEQ * TS
EPS = 1e-6
PAST = 8192
import os
ATT_STAGE = int(os.environ.get('ATT_STAGE', '9'))
NWS = 4

VEC = {}
_nv = 0
def _v(name, n=8):
    global _nv
    VEC[name] = _nv
    _nv += n
for _i in range(4):
    _v("nm%d" % _i); _v("nf%d" % _i)
_v("nfin")
for _j in range(2):
    _v("bq%d" % _j); _v("bk%d" % _j, 1); _v("bo%d" % _j); _v("sink%d" % _j)
for _n in ["rcw0", "rcw1", "rcw2", "rcw3", "rcb", "rba", "rbx", "rlam", "scw0", "scw1", "scw2"]:
    _v(_n)
NV = _nv


def I(name, *args, **kw):
    return lambda e: getattr(e, name)(*args, **kw)


def MM(lst):
    lst = list(lst)

    def f(e):
        ins = None
        for (o_, l_, r_, st, sp) in lst:
            ins = e.matmul(o_, l_, r_, start=st, stop=sp)
        return ins
    return f


class Prog:
    ENG = ["pe", "act", "dve", "pool", "sp"]

    def __init__(self, nc, es):
        self.nc = nc
        self.es = es
        self.q = {e: [] for e in self.ENG}
        self.esem = {e: es.enter_context(nc.semaphore("s_" + e)) for e in self.ENG}
        self.ecnt = {e: 0 for e in self.ENG}
        self.seen = {e: {} for e in self.ENG}
        self.lastw = {}
        self.readers = {}
        self.dsem = {}
        self.dcnt = {}
        self.semobj = {}
        self.bar = {}
        self.msdma = {}

    def barrier(self):
        self.bar = {("e_" + e): self.ecnt[e] for e in ("pe", "act", "dve") if self.ecnt[e] > 0}
        self.bar.update(self.msdma)
        for e in ("pe", "act", "dve"):
            self.semobj["e_" + e] = self.esem[e]

    def _deps(self, eng, reads, writes, ms=True):
        toks = []
        if ms:
            toks += list(self.bar.items())
        for k in reads:
            if k in self.lastw:
                toks.append(self.lastw[k])
        for k in writes:
            if k in self.lastw:
                toks.append(self.lastw[k])
            toks += self.readers.get(k, [])
        need = {}
        for (sid, val) in toks:
            if val > need.get(sid, 0):
                need[sid] = val
        waits = []
        for sid, val in need.items():
            if self.seen[eng].get(sid, 0) >= val:
                continue
            self.seen[eng][sid] = val
            waits.append((self.semobj[sid], val))
        return waits

    def _commit(self, tok, reads, writes):
        for k in reads:
            self.readers.setdefault(k, []).append(tok)
        for k in writes:
            self.lastw[k] = tok
            self.readers[k] = []

    def op(self, eng, fn, reads=(), writes=(), ms=True):
        waits = self._deps(eng, reads, writes, ms)
        self.ecnt[eng] += 1
        sid = "e_" + eng
        self.semobj[sid] = self.esem[eng]
        tok = (sid, self.ecnt[eng])
        self.q[eng].append((waits, fn, self.esem[eng], 1))
        self._commit(tok, reads, writes)

    def dma(self, eng, out, in_, reads, writes, skey, ms=False, **kw):
        if skey not in self.dsem:
            self.dsem[skey] = self.es.enter_context(self.nc.semaphore("d_%d" % len(self.dsem)))
            self.dcnt[skey] = 0
        sid = "d_" + str(skey)
        self.semobj[sid] = self.dsem[skey]
        waits = self._deps(eng, reads, writes, ms)
        self.dcnt[skey] += 16
        tok = (sid, self.dcnt[skey])
        if ms:
            self.msdma[sid] = self.dcnt[skey]
        self.q[eng].append((waits, (lambda e: e.dma_start(out=out, in_=in_, **kw)), self.dsem[skey], 16))
        self._commit(tok, reads, writes)

    def cc(self, fn, reads, writes, skey):
        self.dsem[skey] = self.es.enter_context(self.nc.semaphore("d_%d" % len(self.dsem)))
        self.dcnt[skey] = 0
        sid = "d_" + str(skey)
        self.semobj[sid] = self.dsem[skey]
        waits = self._deps("pool", reads, writes, False)
        self.dcnt[skey] += 1
        self.q["pool"].append((waits, fn, self.dsem[skey], None))
        self._commit((sid, self.dcnt[skey]), reads, writes)

    def final_wait(self, eng):
        waits = []
        for skey, sem in self.dsem.items():
            if self.dcnt[skey] > 0:
                waits.append((sem, self.dcnt[skey]))
        self.q[eng].append((waits, None, None, 0))

    def emit(self, block):
        def mk(ename):
            def run(e):
                for (waits, fn, sem, inc) in self.q[ename]:
                    for (s, v) in waits:
                        e.wait_ge(s, v)
                    if fn is not None:
                        ins = fn(e)
                        if inc is None:
                            ins.then_inc(sem)
                        else:
                            ins.then_inc(sem, inc)
            return run
        block.tensor(mk("pe"))
        block.scalar(mk("act"))
        block.vector(mk("dve"))
        block.gpsimd(mk("pool"))
        block.sync(mk("sp"))


def build(TP=2048, layers=4, do_sample=True):
    assert TP % GT == 0
    NG = TP // GT
    assert NG == 2
    XS = GT
    XTOT = TP + NS
    HW_ = GT + NS
    nc = bass.Bass("TRN2", target_bir_lowering=False)

    def din(name, shape):
        return nc.dram_tensor(name, list(shape), F32, kind="ExternalInput").ap()

    def dout(name, shape):
        return nc.dram_tensor(name, list(shape), F32, kind="ExternalOutput").ap()

    xpT = din("xpT", [D, TP]); xsT = din("xsT", [D, NS])
    ckT = din("ckT", [2, 128, NSEQ, 128]); ckN = din("ckN", [2, NSEQ, 128, 128])
    cvN = din("cvN", [2, NSEQ, 128, 128])
    sh = din("sh", [128, KC, NSEQ]); src = din("src", [128, KC, NSEQ, 3]); ssc = din("ssc", [128, KC, NSEQ, 2])
    vec_d = din("vec", [128, NV])
    cosP = din("cosP", [128, TP]); sinP = din("sinP", [128, TP])
    cosS = din("cosS", [128, NS]); sinS = din("sinS", [128, NS])
    cst = din("cst", [128, 128 + 512 * 3 + 128])
    xhT = din("xhT", [D, 128]); cosH = din("cosH", [128, 128]); sinH = din("sinH", [128, 128])
    flags_d = din("flags", [128, 2])
    bvb = din("bvb", [2, 128, 128])
    wqkv = din("wqkv", [2, D, 1280]); wo = din("wo", [2, D, D])
    rwg = din("rwg", [D, D]); rwi = din("rwi", [D, D]); rwax = din("rwax", [4, 256, 512])
    rwo = din("rwo", [D, D])
    swi = din("swi", [D, 3 * D]); swo = din("swo", [D, D])
    fwg = din("fwg", [4, D, DFF]); fwu = din("fwu", [4, D, DFF]); fwd = din("fwd", [4, DFF, D])

    ypT = dout("ypT", [D, TP]); ysT = dout("ysT", [D, NS])
    o_kp = dout("o_kp", [2, 128, 128]); o_vp = dout("o_vp", [2, 128, 128])
    o_ks = dout("o_ks", [2, 128, 128]); o_vs = dout("o_vs", [2, 128, 128])
    o_ksh = dout("o_ksh", [2, NSEQ, 120, 128]); o_vsh = dout("o_vsh", [2, NSEQ, 120, 128])
    o_hp = dout("o_hp", [128, KC]); o_hs = dout("o_hs", [128, KC, NSEQ])
    o_rcp = dout("o_rcp", [128, KC, 3]); o_rcs = dout("o_rcs", [128, KC, NSEQ, 3])
    o_scp = dout("o_scp", [128, KC, 2]); o_scs = dout("o_scs", [128, KC, NSEQ, 2])

    es = ExitStack()
    with es:
        P = Prog(nc, es)

        def sb(name, shape, dt=F32):
            return es.enter_context(nc.sbuf_tensor("sb_" + name, list(shape), dt))

        xT = sb("xT", [128, KC, XTOT])
        hT = sb("hT", [128, KC, HW_], BF16)
        mixT = sb("mixT", [128, KC, HW_], BF16)
        ws = [sb("ws%d" % i, [128, KC, 512], BF16) for i in range(NWS)]
        vec = sb("vec", [128, NV])
        cvec = sb("cvec", [128, 40])
        flg = sb("flg", [128, 2])
        rot = sb("rot", [128, 128], BF16)
        maskP = sb("maskP", [128, 512], BF16); maskF = sb("maskF", [128, 512], BF16); maskS = sb("maskS", [128, 512], BF16)
        ones = sb("ones", [128, 128], BF16)
        ident = sb("ident", [128, 128], BF16)
        onesh = [sb("onesh%d" % g, [128, 128], BF16) for g in range(2)]
        bvt = sb("bvt", [128, 128])
        vf = sb("vf", [128, 128])
        kcar = [[sb("kcar%d_%d" % (j, g), [128, 128], BF16) for g in range(2)] for j in range(2)]
        vcar = [[sb("vcar%d_%d" % (j, g), [128, 128], BF16) for g in range(2)] for j in range(2)]
        NTMP = 3
        tmpA = [sb("tmpA%d" % i, [128, 512]) for i in range(NTMP)]
        tmpB = [sb("tmpB%d" % i, [128, 512]) for i in range(NTMP)]
        tmpC = [sb("tmpC%d" % i, [128, 512], BF16) for i in range(NTMP)]
        zcar = sb("zcar", [128, KC, 2])
        hcar = sb("hcar", [128, KC])
        ucar = sb("ucar", [128, KC, 3])
        sst = sb("sst", [128, KC, NSEQ, 3])
        shs = sb("shs", [128, KC, NSEQ])
        seq_ext = sb("seq_ext", [128, NSEQ, 3 + TS])
        xsend = sb("xsend", [128, 256]); xrecv = sb("xrecv", [128, 256])
        MSN = 10880
        MS = sb("MS", [128, MSN])

        def msv(lo, n, dt=F32):
            v = MS[:, lo:lo + n]
            return v.bitcast(dt) if dt != F32 else v
        qT = msv(0, 4096, BF16).rearrange("p (c t) -> p c t", c=KC)
        cosT = msv(4096, 1024); sinT = msv(5120, 1024)
        kf = msv(6144, 1024)
        kpad = [msv(7168 + g * 576, 576, BF16) for g in range(2)]
        vpad = [msv(8320 + g * 576, 576, BF16).rearrange("p (a b) -> p a b", a=9) for g in range(2)]
        AV_P = (cosT, sinT, kf, kpad, vpad)
        AV_S = (msv(4096, 128), msv(4224, 128), msv(4352, 128),
                [msv(4480 + g * 576, 576, BF16) for g in range(2)],
                [msv(5632 + g * 576, 576, BF16).rearrange("p (a b) -> p a b", a=9) for g in range(2)])
        pTb = [msv(9472 + i * 256, 256, BF16) for i in range(4)]
        dnb = [msv(10496 + i * 128, 128) for i in range(3)]
        kcp = [msv(6784 + g * 1024, 1024, BF16).rearrange("p (a b) -> p a b", a=NSEQ) for g in range(2)]
        vcp = [msv(8832 + g * 1024, 1024, BF16).rearrange("p (a b) -> p a b", a=NSEQ) for g in range(2)]
        uT = msv(0, 4096).rearrange("p (c t) -> p c t", c=KC)
        uext = msv(4096, 4120).rearrange("p (c t) -> p c t", c=KC)
        ubT = msv(8216, 2048, BF16).rearrange("p (c t) -> p c t", c=KC)
        zext = msv(0, 514)
        act_t = msv(0, 2 * HW_, BF16).rearrange("p (c t) -> p c t", c=4)
        xhv = msv(6144, 1024).rearrange("p (c t) -> p c t", c=KC)
        cur = {"xb": 0, "g": 0}
        epsc = sb("epsc", [128, 1]); onec = sb("onec", [128, 1])
        ps = [es.enter_context(nc.psum_tensor("ps%d" % i, [128, 512], F32)) for i in range(8)]
        psn = [0]

        def bank():
            i = psn[0] % 8
            psn[0] += 1
            return i

        tn = [0]
        pn = [0]
        dq = [0]

        def tmpi():
            i = tn[0] % NTMP
            tn[0] += 1
            return i

        def vcol(name, c=0):
            o = VEC[name] + c
            return vec[:, o:o + 1]

        def XV(c, o, n):
            if cur["g"] == "h":
                return xhv[:, c, o:o + n]
            return xT[:, c, cur["xb"] + o:cur["xb"] + o + n]

        def XKc(c, o=0):
            g = cur["g"]
            if g == 0 and o >= GT:
                g = "s"
            if g == "s1":
                g = "s" if o < NS else 1
            return ("x", g, c)

        def s3(ap):
            return ap.rearrange("p (b t) -> p b t", t=TS)

        HK = [("h", c) for c in range(KC)]

        P.dma("sp", vec[:], vec_d[:, :], [], ["vec"], "c0")
        P.dma("pool", rot[:], cst[:, 0:128], [], ["rot"], "c_rot")
        P.dma("pool", maskP[:], cst[:, 128:640], [], ["maskP"], "c_mp")
        P.dma("pool", maskF[:], cst[:, 640:1152], [], ["maskF"], "c_mf")
        P.dma("pool", maskS[:], cst[:, 1152:1664], [], ["maskS"], "c_ms")
        P.dma("pool", ident[:], cst[:, 1664:1792], [], ["ident"], "c_id")
        P.op("dve", I("memset", ones[:], 1.0), [], ["ones"])
        P.op("dve", I("memset", epsc[:], EPS), [], ["epsc"])
        P.op("dve", I("memset", onec[:], 1.0), [], ["onec"])
        for g in range(2):
            P.op("dve", I("memset", onesh[g][:], 0.0), [], [("onesh", g)])
            P.op("dve", I("memset", onesh[g][:, g * 64:(g + 1) * 64], 1.0), [], [("onesh", g)])
            for j in range(2):
                P.op("dve", I("memset", kcar[j][g][:], 0.0), [], [("kcar", j, g)])
                P.op("dve", I("memset", vcar[j][g][:], 0.0), [], [("vcar", j, g)])
        P.dma("sp", flg[:], flags_d[:, :], [], ["flg"], "c_flg")
        lam = vec[:, VEC["rlam"]:VEC["rlam"] + 8]
        P.op("act", I("activation", tmpA[0][:, 0:8], lam, AF.Exp, scale=-1.0), ["vec"], [("tA", 0)])
        P.op("act", I("activation", tmpA[0][:, 8:16], tmpA[0][:, 0:8], AF.Ln, bias=onec[:, 0:1], scale=1.0), [("tA", 0), "onec"], [("tA", 0)])
        P.op("dve", I("tensor_scalar", cvec[:, 0:8], tmpA[0][:, 8:16], -8.0, None, ALU.mult), [("tA", 0)], ["cvec"])
        P.op("dve", I("tensor_scalar", cvec[:, 8:16], tmpA[0][:, 8:16], -16.0, None, ALU.mult), [("tA", 0)], ["cvec"])
        P.op("dve", I("tensor_scalar", cvec[:, 16:24], tmpA[0][:, 8:16], -4.0, None, ALU.mult), [("tA", 0)], ["cvec"])
        P.op("dve", I("tensor_scalar", cvec[:, 24:32], vec[:, VEC["rba"]:VEC["rba"] + 8], 0.5, None, ALU.mult), ["vec"], ["cvec"])
        P.op("dve", I("tensor_scalar", cvec[:, 32:40], vec[:, VEC["rbx"]:VEC["rbx"] + 8], 0.5, None, ALU.mult), ["vec"], ["cvec"])
        for j in range(2):
            sk = vec[:, VEC["sink%d" % j]:VEC["sink%d" % j] + 8]
            P.op("act", I("activation", sk, sk, AF.Exp), ["vec", "cvec"], ["vec"])

        wn = [0]

        def wslot():
            i = wn[0] % NWS
            wn[0] += 1
            return i

        def wload(dram_ap, kc, ncols):
            i = wslot()
            P.dma("pool", ws[i][:, 0:kc, 0:ncols], dram_ap.rearrange("(kc p) n -> p kc n", p=128), [], [("ws", i)], ("ws", i))
            return i

        def proj_chunk(slot, kc, col0, rhs_t, o, n, b, extra_reads, kbase=0):
            P.op("pe", MM([(ps[b][:, 0:n], ws[slot][:, kbase + k, col0:col0 + 128], rhs_t[:, k, o:o + n], k == 0, k == kc - 1) for k in range(kc)]),
                 [("ws", slot)] + extra_reads, [("ps", b)])

        def rms_stats(o, n):
            b = bank()
            for c in range(KC):
                t = tmpi()
                P.op("act", I("activation", tmpC[t][:, 0:n], XV(c, o, n), AF.Square), [XKc(c, o)], [("tC", t)], ms=False)
                P.op("pe", I("matmul", ps[b][:, 0:n], ones[:], tmpC[t][:, 0:n], start=(c == 0), stop=(c == KC - 1)), [("tC", t), "ones"], [("ps", b)], ms=False)
            t = tmpi()
            P.op("act", I("activation", tmpA[t][:, 0:n], ps[b][:, 0:n], AF.Sqrt, bias=epsc[:, 0:1], scale=1.0 / D), [("ps", b), "epsc"], [("tA", t)], ms=False)
            P.op("dve", I("reciprocal", tmpB[t][:, 0:n], tmpA[t][:, 0:n]), [("tA", t)], [("tB", t)], ms=False)
            return t

        def rmsnorm(gname, tiles):
            for (o, n) in tiles:
                t = rms_stats(o, n)
                for c in range(KC):
                    P.op("dve", I("scalar_tensor_tensor", hT[:, c, o:o + n], XV(c, o, n), vcol(gname, c), tmpB[t][:, 0:n], ALU.mult, ALU.mult),
                         [XKc(c, o), ("tB", t), "vec"], [("h", c)], ms=False)

        def rmsnorm_final(tiles, outd, tok0):
            for (o, n) in tiles:
                t = rms_stats(o, n)
                for c in range(KC):
                    t2 = tmpi()
                    P.op("dve", I("scalar_tensor_tensor", tmpA[t2][:, 0:n], XV(c, o, n), vcol("nfin", c), tmpB[t][:, 0:n], ALU.mult, ALU.mult),
                         [XKc(c), ("tB", t), "vec"], [("tA", t2)])
                    P.dma("sp", outd[c * 128:(c + 1) * 128, tok0 + o:tok0 + o + n], tmpA[t2][:, 0:n], [("tA", t2)], [], ("out", t2))

        def out_proj_add(w_dram, tiles, bias_name=None):
            for half in range(2):
                s = wload(w_dram[:, half * 512:(half + 1) * 512], KC, 512)
                for (o, n) in tiles:
                    for cc in range(4):
                        c = half * 4 + cc
                        b = bank()
                        proj_chunk(s, KC, cc * 128, mixT, o, n, b, [("mix", k) for k in range(KC)])
                        if bias_name is None:
                            P.op("dve", I("tensor_tensor", XV(c, o, n), XV(c, o, n), ps[b][:, 0:n], ALU.add), [("ps", b), XKc(c, o)], [XKc(c, o)])
                        else:
                            P.op("dve", I("scalar_tensor_tensor", XV(c, o, n), ps[b][:, 0:n], vcol(bias_name, c), XV(c, o, n), ALU.add, ALU.add),
                                 [("ps", b), XKc(c, o), "vec"], [XKc(c, o)])

        def ffn(li, tiles):
            rmsnorm("nf%d" % li, tiles)
            f0 = 0
            while f0 < DFF:
                fw = min(512, DFF - f0)
                nfc = fw // 128
                sg = wload(fwg[li][:, f0:f0 + fw], KC, fw)
                su = wload(fwu[li][:, f0:f0 + fw], KC, fw)
                for (o, n) in tiles:
                    for fc in range(nfc):
                        bg = bank(); bu = bank()
                        proj_chunk(sg, KC, fc * 128, hT, o, n, bg, HK)
                        proj_chunk(su, KC, fc * 128, hT, o, n, bu, HK)
                        t = tmpi()
                        P.op("act", I("activation", tmpA[t][:, 0:n], ps[bg][:, 0:n], AF.Silu), [("ps", bg)], [("tA", t)])
                        P.op("dve", I("tensor_tensor", act_t[:, fc, o:o + n], tmpA[t][:, 0:n], ps[bu][:, 0:n], ALU.mult), [("tA", t), ("ps", bu)], [("act", fc)])
                sd = []
                for half in range(2):
                    i = wslot()
                    P.dma("pool", ws[i][:, 0:nfc, 0:512], fwd[li][f0:f0 + fw, half * 512:(half + 1) * 512].rearrange("(kc p) n -> p kc n", p=128),
                          [], [("ws", i)], ("ws", i))
                    sd.append(i)
                for (o, n) in tiles:
                    for c in range(KC):
                        b = bank()
                        proj_chunk(sd[c // 4], nfc, (c % 4) * 128, act_t, o, n, b, [("act", k) for k in range(nfc)])
                        P.op("dve", I("tensor_tensor", XV(c, o, n), XV(c, o, n), ps[b][:, 0:n], ALU.add), [("ps", b), XKc(c, o)], [XKc(c, o)])
                f0 += fw

        def sconv(li, tiles, sample, last, stiles=()):
            P.barrier()
            rmsnorm("nm%d" % li, tiles)
            SK = [("sst", c) for c in range(KC)]
            has_s = sample or len(stiles) > 0
            has_p = (not sample)
            if has_s:
                P.dma("sp", sst[:, :, :, 0:2], ssc[:, :, :, :], [], SK, "sst")
            for c in range(KC):
                i = wslot()
                P.dma("pool", ws[i][:, :, 0:384], swi[:, c * 384:(c + 1) * 384].rearrange("(kc p) n -> p kc n", p=128), [], [("ws", i)], ("ws", i))
                for (o, n) in tiles:
                    bb = bank(); bc = bank(); bx = bank()
                    proj_chunk(i, KC, 0, hT, o, n, bb, HK)
                    proj_chunk(i, KC, 128, hT, o, n, bc, HK)
                    proj_chunk(i, KC, 256, hT, o, n, bx, HK)
                    t = tmpi()
                    P.op("act", I("activation", tmpA[t][:, 0:n], ps[bc][:, 0:n], AF.Copy), [("ps", bc)], [("tA", t)])
                    y = tmpB[t]
                    smp = sample or (o in stiles)
                    if not smp:
                        P.op("dve", I("tensor_copy", zext[:, 0:2], zcar[:, c, :]), [("zcar", c)], ["zext"])
                        P.op("dve", I("tensor_tensor", zext[:, 2:2 + n], tmpA[t][:, 0:n], ps[bx][:, 0:n], ALU.mult), [("tA", t), ("ps", bx)], ["zext"])
                        P.op("dve", I("tensor_copy", zcar[:, c, :], zext[:, n:n + 2]), ["zext"], [("zcar", c)])
                        e0, e1, e2, yv = zext[:, 0:n], zext[:, 1:1 + n], zext[:, 2:2 + n], y[:, 0:n]
                        ek = "zext"
                    else:
                        P.op("dve", I("tensor_copy", seq_ext[:, :, 0:2], sst[:, c, :, 0:2]), [("sst", c)], ["seq_ext"])
                        P.op("dve", I("tensor_tensor", seq_ext[:, :, 2:2 + TS], s3(tmpA[t][:, 0:NS]), s3(ps[bx][:, 0:NS]), ALU.mult), [("tA", t), ("ps", bx)], ["seq_ext"])
                        P.op("dve", I("tensor_copy", sst[:, c, :, 0:2], seq_ext[:, :, TS:TS + 2]), ["seq_ext"], [("sst", c)])
                        e0, e1, e2, yv = seq_ext[:, :, 0:TS], seq_ext[:, :, 1:1 + TS], seq_ext[:, :, 2:2 + TS], s3(y[:, 0:NS])
                        ek = "seq_ext"
                    P.op("dve", I("tensor_scalar", yv, e0, vcol("scw0", c), None, ALU.mult), [ek, "vec"], [("tB", t)])
                    P.op("dve", I("scalar_tensor_tensor", yv, e1, vcol("scw1", c), yv, ALU.mult, ALU.add), [ek, ("tB", t)], [("tB", t)])
                    P.op("dve", I("scalar_tensor_tensor", yv, e2, vcol("scw2", c), yv, ALU.mult, ALU.add), [ek, ("tB", t)], [("tB", t)])
                    P.op("dve", I("tensor_tensor", mixT[:, c, o:o + n], y[:, 0:n], ps[bb][:, 0:n], ALU.mult), [("tB", t), ("ps", bb)], [("mix", c)])
            if last and has_p:
                P.dma("sp", o_scp[:, :, :], zcar[:], [("zcar", c) for c in range(KC)], [], "o_sc")
            if has_s:
                P.dma("sp", o_scs[:, :, :, :], sst[:, :, :, 0:2], SK, [], "o_scs")
            out_proj_add(swo, tiles)

        mixF = mixT[:].rearrange("p c t -> p (c t)").bitcast(F32)
        P1R = [mixF[:, i * 512:(i + 1) * 512] for i in range(9)]
        p1n = [0]

        PRJ = [(tmpA[i], tmpB[i], tmpC[i], ("tA", i), ("tB", i), ("tC", i)) for i in range(NTMP)]
        PRJ += [(P1R[3 * i], P1R[3 * i + 1], P1R[3 * i + 2].bitcast(BF16)[:, 0:512], ("p1", 3 * i), ("p1", 3 * i + 1), ("p1", 3 * i + 2)) for i in range(3)]
        prn = [0]
        P1K = [("p1", i) for i in range(9)]

        def prj_ring():
            i = prn[0] % len(PRJ)
            prn[0] += 1
            return PRJ[i]

        def p1buf():
            i = p1n[0] % 9
            p1n[0] += 1
            return P1R[i], ("p1", i)

        def rglru(li, tiles, sample, last, phase1=False, scan=True):
            P.barrier()
            rmsnorm("nm%d" % li, tiles)
            SK = [("sst", c) for c in range(KC)]
            if sample:
                P.dma("sp", shs[:], sh[:, :, :], [], [("shs", c) for c in range(KC)], "shs_in")
                P.dma("sp", sst[:], src[:, :, :, :], [], SK, "sst")
            else:
                for c in range(KC):
                    P.op("dve", I("tensor_copy", uext[:, c, 0:3], ucar[:, c, :]), [("ucar", c)], [("uext", c)])
            for (o, n) in tiles:
                for half in range(2):
                    s = wload(rwi[:, half * 512:(half + 1) * 512], KC, 512)
                    for cc in range(4):
                        c = half * 4 + cc
                        b = bank()
                        proj_chunk(s, KC, cc * 128, hT, o, n, b, HK)
                        if not sample:
                            P.op("act", I("activation", uext[:, c, 3:3 + n], ps[b][:, 0:n], AF.Copy), [("ps", b)], [("uext", c)])
                            ex = [uext[:, c, j:j + n] for j in range(4)]
                            uo = uT[:, c, 0:n]
                            rk = [("uext", c), "vec"]
                        else:
                            P.op("dve", I("tensor_copy", seq_ext[:, :, 0:3], sst[:, c, :, :]), [("sst", c)], ["seq_ext"])
                            P.op("act", I("activation", seq_ext[:, :, 3:3 + TS], s3(ps[b][:, 0:NS]), AF.Copy), [("ps", b)], ["seq_ext"])
                            P.op("dve", I("tensor_copy", sst[:, c, :, :], seq_ext[:, :, TS:TS + 3]), ["seq_ext"], [("sst", c)])
                            ex = [seq_ext[:, :, j:j + TS] for j in range(4)]
                            uo = s3(uT[:, c, 0:NS])
                            rk = ["seq_ext", "vec"]
                        P.op("dve", I("tensor_scalar", uo, ex[0], vcol("rcw0", c), vcol("rcb", c), ALU.mult, ALU.add), rk, [("u", c)])
                        for j in range(1, 4):
                            P.op("dve", I("scalar_tensor_tensor", uo, ex[j], vcol("rcw%d" % j, c), uo, ALU.mult, ALU.add), rk + [("u", c)], [("u", c)])
                        P.op("dve", I("tensor_copy", ubT[:, c, 0:n], uT[:, c, 0:n]), [("u", c)], [("ub", c)])
                        if not sample:
                            P.op("dve", I("tensor_copy", uext[:, c, 0:3], uext[:, c, n:n + 3]), [("uext", c), ("u", c)], [("uext", c)])
                for nb in range(4):
                    ia = wslot()
                    P.dma("pool", ws[ia][:, 0:2, 0:512], rwax[nb].rearrange("(kc p) n -> p kc n", p=128), [], [("ws", ia)], ("ws", ia))
                    sgt = None if phase1 else wload(rwg[:, nb * 256:(nb + 1) * 256], KC, 256)
                    if phase1:
                        ti = cur["g"] * 2 + o // 512
                        ur = [("ub", nb * 2), ("ub", nb * 2 + 1), ("ws", ia)]
                        st = []
                        for sub in range(2):
                            c = nb * 2 + sub
                            ba_ = bank(); bx_ = bank()
                            P.op("pe", MM([(ps[ba_][:, 0:n], ws[ia][:, k, sub * 128:(sub + 1) * 128], ubT[:, nb * 2 + k, 0:n], k == 0, k == 1) for k in range(2)]), ur, [("ps", ba_)])
                            P.op("pe", MM([(ps[bx_][:, 0:n], ws[ia][:, k, 256 + sub * 128:256 + (sub + 1) * 128], ubT[:, nb * 2 + k, 0:n], k == 0, k == 1) for k in range(2)]), ur, [("ps", bx_)])
                            (t1, k1), (t2, k2), (a_, ka), (m_, km) = p1buf(), p1buf(), p1buf(), p1buf()
                            P.op("act", I("activation", t1[:, 0:n], ps[ba_][:, 0:n], AF.Tanh, bias=cvec[:, 24 + c:25 + c], scale=0.5), [("ps", ba_), "cvec"], [k1])
                            P.op("act", I("activation", t2[:, 0:n], ps[bx_][:, 0:n], AF.Tanh, bias=cvec[:, 32 + c:33 + c], scale=0.5), [("ps", bx_), "cvec"], [k2])
                            P.op("act", I("activation", a_[:, 0:n], t1[:, 0:n], AF.Exp, bias=cvec[:, 16 + c:17 + c], scale=cvec[:, 16 + c:17 + c]), [k1, "cvec"], [ka])
                            P.op("act", I("activation", m_[:, 0:n], t1[:, 0:n], AF.Exp, bias=cvec[:, c:c + 1], scale=cvec[:, c:c + 1]), [k1, "cvec"], [km])
                            P.op("dve", I("scalar_tensor_tensor", t2[:, 0:n], t2[:, 0:n], 1.0, uT[:, c, 0:n], ALU.add, ALU.mult), [k2, ("u", c)], [k2])
                            st.append((c, t1, k1, t2, k2, a_, ka, m_, km))
                        for (c, t1, k1, t2, k2, a_, ka, m_, km) in st:
                            P.op("act", I("activation", m_[:, 0:n], m_[:, 0:n], AF.Sqrt, bias=onec[:, 0:1], scale=-1.0), [km, "onec"], [km])
                        for (c, t1, k1, t2, k2, a_, ka, m_, km) in st:
                            P.op("dve", I("scalar_tensor_tensor", t2[:, 0:n], t2[:, 0:n], 0.5, m_[:, 0:n], ALU.mult, ALU.mult), [k2, km], [k2])
                            P.dma("sp", abv(0, ti)[:, c, 0:n], a_[:, 0:n], [ka], [("abd", ti, c, 0)], ("st", ka), ms=True)
                            P.dma("sp", abv(1, ti)[:, c, 0:n], t2[:, 0:n], [k2], [("abd", ti, c, 1)], ("st", k2), ms=True)
                            if scan:
                                P.op("dve", I("tensor_tensor_scan", t1[:, 0:n], a_[:, 0:n], t2[:, 0:n], hcar[:, c:c + 1], ALU.mult, ALU.add), [ka, k2, ("hcar", c), k1], [k1])
                                P.op("dve", I("tensor_copy", hcar[:, c:c + 1], t1[:, n - 1:n]), [k1], [("hcar", c)])
                        continue
                    for sub in range(2):
                        c = nb * 2 + sub
                        ba_ = bank(); bx_ = bank(); bg_ = bank()
                        ur = [("ub", nb * 2), ("ub", nb * 2 + 1), ("ws", ia)]
                        P.op("pe", MM([(ps[ba_][:, 0:n], ws[ia][:, k, sub * 128:(sub + 1) * 128], ubT[:, nb * 2 + k, 0:n], k == 0, k == 1) for k in range(2)]), ur, [("ps", ba_)])
                        P.op("pe", MM([(ps[bx_][:, 0:n], ws[ia][:, k, 256 + sub * 128:256 + (sub + 1) * 128], ubT[:, nb * 2 + k, 0:n], k == 0, k == 1) for k in range(2)]), ur, [("ps", bx_)])
                        if not phase1:
                            proj_chunk(sgt, KC, sub * 128, hT, o, n, bg_, HK)
                        if phase1:
                            (r_, kr), (ig_, ki), (a_, ka), (m_, km) = p1buf(), p1buf(), p1buf(), p1buf()
                        else:
                            t = tmpi(); t2 = tmpi()
                            r_ = tmpA[t]; ig_ = tmpB[t]; a_ = tmpA[t2]; m_ = tmpB[t2]
                            kr, ki, ka, km = ("tA", t), ("tB", t), ("tA", t2), ("tB", t2)
                        P.op("act", I("activation", r_[:, 0:n], ps[ba_][:, 0:n], AF.Sigmoid, bias=vcol("rba", c), scale=1.0), [("ps", ba_), "vec"], [kr])
                        P.op("act", I("activation", ig_[:, 0:n], ps[bx_][:, 0:n], AF.Sigmoid, bias=vcol("rbx", c), scale=1.0), [("ps", bx_), "vec"], [ki])
                        P.op("act", I("activation", a_[:, 0:n], r_[:, 0:n], AF.Exp, scale=cvec[:, c:c + 1]), [kr, "cvec"], [ka])
                        P.op("act", I("activation", m_[:, 0:n], r_[:, 0:n], AF.Exp, scale=cvec[:, 8 + c:9 + c]), [kr, "cvec"], [km])
                        P.op("act", I("activation", m_[:, 0:n], m_[:, 0:n], AF.Sqrt, bias=onec[:, 0:1], scale=-1.0), [km, "onec"], [km])
                        P.op("dve", I("tensor_tensor", ig_[:, 0:n], ig_[:, 0:n], m_[:, 0:n], ALU.mult), [ki, km], [ki])
                        P.op("dve", I("tensor_tensor", ig_[:, 0:n], ig_[:, 0:n], uT[:, c, 0:n], ALU.mult), [ki, ("u", c)], [ki])
                        hs_ = r_
                        if phase1:
                            ti = cur["g"] * 2 + o // 512
                            P.dma("sp", abv(0, ti)[:, c, 0:n], a_[:, 0:n], [ka], [("abd", ti, c, 0)], ("st", ka), ms=True)
                            P.dma("sp", abv(1, ti)[:, c, 0:n], ig_[:, 0:n], [ki], [("abd", ti, c, 1)], ("st", ki), ms=True)
                        if not sample and scan:
                            P.op("dve", I("tensor_tensor_scan", hs_[:, 0:n], a_[:, 0:n], ig_[:, 0:n], hcar[:, c:c + 1], ALU.mult, ALU.add),
                                 [ka, ki, ("hcar", c), kr], [kr])
                            P.op("dve", I("tensor_copy", hcar[:, c:c + 1], hs_[:, n - 1:n]), [kr], [("hcar", c)])
                        elif sample:
                            a3 = s3(a_[:, 0:NS]); b3 = s3(ig_[:, 0:NS])
                            P.op("dve", I("tensor_tensor", seq_ext[:, :, 0:1], a3[:, :, 0:1], shs[:, c, :].unsqueeze(2), ALU.mult), [ka, ("shs", c)], ["seq_ext"])
                            P.op("dve", I("tensor_tensor", b3[:, :, 0:1], b3[:, :, 0:1], seq_ext[:, :, 0:1], ALU.add), ["seq_ext", ki], [ki])
                            P.op("dve", I("memset", a3[:, :, 0:1], 0.0), [ka], [ka])
                            P.op("dve", I("tensor_tensor_scan", hs_[:, 0:NS], a_[:, 0:NS], ig_[:, 0:NS], 0.0, ALU.mult, ALU.add),
                                 [ka, ki, kr], [kr])
                            P.op("dve", I("tensor_copy", shs[:, c, :].unsqueeze(2), s3(hs_[:, 0:NS])[:, :, TS - 1:TS]), [kr], [("shs", c)])
                        if not phase1:
                            P.op("act", I("activation", a_[:, 0:n], ps[bg_][:, 0:n], AF.Gelu), [("ps", bg_), ka], [ka])
                            P.op("dve", I("tensor_tensor", mixT[:, c, o:o + n], hs_[:, 0:n], a_[:, 0:n], ALU.mult), [kr, ka], [("mix", c)])
            if not sample:
                for c in range(KC):
                    P.op("dve", I("tensor_copy", ucar[:, c, :], uext[:, c, 0:3]), [("uext", c)], [("ucar", c)])
            if phase1:
                return
            if last:
                if not sample:
                    P.dma("sp", o_hp[:, :], hcar[:], [("hcar", c) for c in range(KC)], [], "o_h")
                    P.dma("sp", o_rcp[:, :, :], ucar[:], [("ucar", c) for c in range(KC)], [], "o_rcp")
                else:
                    P.dma("sp", o_hs[:, :, :], shs[:], [("shs", c) for c in range(KC)], [], "o_hs")
                    P.dma("sp", o_rcs[:, :, :, :], sst[:], SK, [], "o_rcs")
            out_proj_add(rwo, tiles)

        ab_scr = nc.dram_tensor("ab_scr", [2, 2 * NG, 128, KC * 512], F32)

        def abv(which, ti):
            return ab_scr[which, ti].rearrange("p (c t) -> p c t", c=KC)
        NPB = 8
        pa = [msv(i * 512, 512) for i in range(NPB)]
        pb = [msv((NPB + i) * 512, 512) for i in range(NPB)]
        pq = [0]

        def rglru_p2(li, tiles, last):
            P.barrier()
            rmsnorm("nm%d" % li, tiles)
            for (o, n) in tiles:
                ti = cur["g"] * 2 + o // 512
                for nb in range(4):
                    sgt = wload(rwg[:, nb * 256:(nb + 1) * 256], KC, 256)
                    for sub in range(2):
                        c = nb * 2 + sub
                        bg_ = bank()
                        proj_chunk(sgt, KC, sub * 128, hT, o, n, bg_, HK)
                        i = pq[0] % NPB
                        pq[0] += 1
                        P.dma("sp", pa[i][:, 0:n], abv(0, ti)[:, c, 0:n], [("abd", ti, c, 0)], [("pa", i)], ("ld_a", i), ms=True)
                        P.dma("sp", pb[i][:, 0:n], abv(1, ti)[:, c, 0:n], [("abd", ti, c, 1)], [("pb", i)], ("ld_b", i), ms=True)
                        t = tmpi()
                        hs_ = tmpA[t]; g_ = tmpB[t]
                        P.op("act", I("activation", g_[:, 0:n], ps[bg_][:, 0:n], AF.Gelu), [("ps", bg_)], [("tB", t)])
                        P.op("dve", I("tensor_tensor_scan", hs_[:, 0:n], pa[i][:, 0:n], pb[i][:, 0:n], hcar[:, c:c + 1], ALU.mult, ALU.add),
                             [("pa", i), ("pb", i), ("hcar", c)], [("tA", t)])
                        P.op("dve", I("tensor_copy", hcar[:, c:c + 1], hs_[:, n - 1:n]), [("tA", t)], [("hcar", c)])
                        P.op("dve", I("tensor_tensor", mixT[:, c, o:o + n], hs_[:, 0:n], g_[:, 0:n], ALU.mult), [("tA", t), ("tB", t)], [("mix", c)])
            if last:
                P.dma("sp", o_hp[:, :], hcar[:], [("hcar", c) for c in range(KC)], [], "o_h")
            out_proj_add(rwo, tiles)

        def attn(li, j, tiles, sample, first, last, tok0):
            P.barrier()
            cosT, sinT, kf, kpad, vpad = AV_S if sample else AV_P
            rmsnorm("nm%d" % li, tiles)
            ntok = sum(n for _, n in tiles)
            if not sample:
                P.dma("sp", cosT[:, 0:ntok], cosP[:, tok0:tok0 + ntok], [], ["cos"], "cs_cos", ms=True)
                P.dma("sp", sinT[:, 0:ntok], sinP[:, tok0:tok0 + ntok], [], ["sin"], "cs_sin", ms=True)
            else:
                P.dma("sp", cosT[:, 0:ntok], cosS[:, :], [], ["cos"], "cs_cos", ms=True)
                P.dma("sp", sinT[:, 0:ntok], sinS[:, :], [], ["sin"], "cs_sin", ms=True)
            P.dma("sp", bvt[:], bvb[j], [], ["bvt"], "cs_bvt")
            for g in range(2):
                P.op("dve", I("memset", kpad[g][:], 0.0), [], ["kbuf"])
                P.op("dve", I("memset", vpad[g][:].rearrange("p a b -> p (a b)"), 0.0), [], [("vpad", g)])
                if sample:
                    P.op("dve", I("memset", kcp[g][:].rearrange("p a b -> p (a b)"), 0.0), [], ["kcT"])
                    P.op("dve", I("memset", vcp[g][:].rearrange("p a b -> p (a b)"), 0.0), [], [("vcp", g)])
            if not sample:
                for g in range(2):
                    P.op("dve", I("tensor_copy", kpad[g][:, 0:128], kcar[j][g][:]), [("kcar", j, g)], ["kbuf"])
                    P.op("dve", I("tensor_copy", vpad[g][:, 0, :], vcar[j][g][:]), [("vcar", j, g)], [("vpad", g)])
            for (c0, ncol) in [(0, 512), (512, 512), (1024, 128)]:
                s = wload(wqkv[j][:, c0:c0 + ncol], KC, ncol)
                for (o, n) in tiles:
                    for cc in range(ncol // 128):
                        c = c0 // 128 + cc
                        b = bank()
                        proj_chunk(s, KC, cc * 128, hT, o, n, b, HK)
                        A_, B_, C_, kA, kB, kC = prj_ring()
                        bias = vcol("bq%d" % j, c) if c < 8 else vcol("bk%d" % j, 0)
                        P.op("act", I("activation", A_[:, 0:n], ps[b][:, 0:n], AF.Identity, bias=bias, scale=1.0), [("ps", b), "vec"], [kA])
                        P.op("act", I("activation", C_[:, 0:n], A_[:, 0:n], AF.Copy), [kA], [kC])
                        b2 = bank()
                        P.op("pe", I("matmul", ps[b2][:, 0:n], rot[:], C_[:, 0:n], start=True, stop=True), [kC, "rot"], [("ps", b2)])
                        P.op("dve", I("tensor_tensor", A_[:, 0:n], A_[:, 0:n], cosT[:, o:o + n], ALU.mult), [kA, "cos"], [kA])
                        P.op("dve", I("tensor_tensor", B_[:, 0:n], ps[b2][:, 0:n], sinT[:, o:o + n], ALU.mult), [("ps", b2), "sin"], [kB])
                        if c < 8:
                            P.op("dve", I("tensor_tensor", qT[:, c, o:o + n], A_[:, 0:n], B_[:, 0:n], ALU.add), [kA, kB], [("q", c)])
                        else:
                            P.op("dve", I("tensor_tensor", kf[:, o:o + n], A_[:, 0:n], B_[:, 0:n], ALU.add), [kA, kB], ["kf"])
                            for g in range(2):
                                P.op("act", I("activation", kpad[g][g * 64:(g + 1) * 64, 128 + o:128 + o + n], kf[g * 64:(g + 1) * 64, o:o + n], AF.Copy), ["kf"], ["kbuf"])
            sv = wload(wqkv[j][:, 1152:1280], KC, 128)
            nblk = ntok // 128
            for bi in range(nblk):
                b = bank()
                P.op("pe", MM([(ps[b][:, 0:128], hT[:, k, bi * 128:(bi + 1) * 128], ws[sv][:, k, 0:128], k == 0, k == KC - 1) for k in range(KC)]),
                     [("ws", sv)] + HK, [("ps", b)])
                P.op("dve", I("tensor_tensor", vf[:], ps[b][:, 0:128], bvt[:], ALU.add), [("ps", b), "bvt"], ["vf"])
                for g in range(2):
                    P.op("act", I("activation", vpad[g][:, 1 + bi, g * 64:(g + 1) * 64], vf[:, g * 64:(g + 1) * 64], AF.Copy), ["vf"], [("vpad", g)])
                if last and bi == nblk - 1:
                    P.dma("sp", (o_vs if sample else o_vp)[j], vf[:], ["vf"], [], ("o_v", sample, j))
            if last:
                P.dma("sp", (o_ks if sample else o_kp)[j], kf[:, ntok - 128:ntok], ["kf"], [], ("o_k", sample, j), ms=True)
            if sample:
                for g in range(2):
                    P.dma("pool", kcp[g][g * 64:(g + 1) * 64, :, :], ckT[j][g * 64:(g + 1) * 64, :, :], [], ["kcT"], "kc", ms=True)
                for g in range(2):
                    P.dma("pool", vcp[g][:, :, g * 64:(g + 1) * 64], cvN[j].rearrange("b k f -> k b f")[:, :, g * 64:(g + 1) * 64], [], [("vcp", g)], ("vc", g), ms=True)
                P.dma("sp", o_ksh[j], ckN[j][:, 8:128, :], [], [], "o_shk")
                P.dma("sp", o_vsh[j], cvN[j][:, 8:128, :], [], [], "o_sh")
            sinkn = "sink%d" % j
            if not sample:
                def v4(ap):
                    return ap.rearrange("p (a b) -> p a b", a=4)
                for bi in range(nblk):
                    msk, mr = (maskF, "maskF") if (first and bi == 0) else (maskP, "maskP")
                    for hh in range(2):
                        cs = [4 * hh + i for i in range(4)]
                        qv = qT[:, 4 * hh:4 * hh + 4, bi * 128:(bi + 1) * 128]
                        pts = []
                        for kb in range(2):
                            for g in range(2):
                                i4 = kb * 2 + g
                                bs = bank()
                                mb = msk[:, i4 * 128:(i4 + 1) * 128].unsqueeze(1).broadcast_to([128, 4, 128])
                                P.op("pe", MM([(v4(ps[bs][:, :]), kpad[g][:, (bi + kb) * 128:(bi + kb + 1) * 128], qv, True, False),
                                               (v4(ps[bs][:, :]), ident[:], mb, False, True)]),
                                     [("q", c) for c in cs] + ["kbuf", mr, "ident"], [("ps", bs)])
                                pi = pn[0] % 4
                                pn[0] += 1
                                P.op("act", I("activation", pTb[pi][:, :], ps[bs][:, :], AF.Exp, scale=0.125), [("ps", bs)], [("pT", pi)])
                                pts.append((pi, kb, g))
                        bo = bank(); bd = bank()
                        lo = []; ld = []
                        for n_, (pi, kb, g) in enumerate(pts):
                            lo.append((ps[bo][:, :], vpad[g][:, bi + kb, :], pTb[pi][:, :], n_ == 0, n_ == 3))
                            ld.append((ps[bd][:, :], onesh[g][:], pTb[pi][:, :], n_ == 0, n_ == 3))
                        rk = [("pT", pi) for (pi, _, _) in pts]
                        P.op("pe", MM(lo), rk + [("vpad", 0), ("vpad", 1)], [("ps", bo)])
                        P.op("pe", MM(ld), rk + [("onesh", 0), ("onesh", 1)], [("ps", bd)])
                        t2 = tmpi()
                        dn = tmpA[t2]
                        sk = vec[:, VEC[sinkn] + 4 * hh:VEC[sinkn] + 4 * hh + 4].unsqueeze(2).broadcast_to([128, 4, 128])
                        P.op("dve", I("tensor_tensor", v4(dn[:, :]), v4(ps[bd][:, :]), sk, ALU.add), [("ps", bd), "vec"], [("tA", t2)])
                        P.op("dve", I("reciprocal", dn[:, :], dn[:, :]), [("tA", t2)], [("tA", t2)])
                        P.op("dve", I("tensor_tensor", mixT[:, 4 * hh:4 * hh + 4, bi * 128:(bi + 1) * 128], v4(ps[bo][:, :]), v4(dn[:, :]), ALU.mult),
                             [("tA", t2), ("ps", bo)], [("mix", c) for c in cs] + P1K)
            for bi in range(nblk if sample else 0):
                for c in range(KC):
                    bs = bank()
                    qr = [("q", c), "kbuf"]
                    lst = []
                    for g in range(2):
                        lst.append((ps[bs][:, g * 128:(g + 1) * 128], kpad[g][:, 128:256], qT[:, c, 0:128], True, True))
                    for g in range(2):
                        for sq in range(NSEQ):
                            c0_ = 256 + g * 128 + sq * TS
                            lst.append((ps[bs][:, c0_:c0_ + TS], kcp[g][:, sq, :], qT[:, c, sq * TS:(sq + 1) * TS], True, True))
                    P.op("pe", MM(lst), qr + ["kcT"], [("ps", bs)])
                    msk, mr = maskS, "maskS"
                    t = tmpi()
                    pT = tmpC[t]
                    P.op("act", I("activation", pT[:, :], ps[bs][:, :], AF.Exp, scale=0.125), [("ps", bs)], [("tC", t)])
                    P.op("dve", I("tensor_tensor", pT[:, :], pT[:, :], msk[:, :], ALU.mult), [("tC", t), mr], [("tC", t)])
                    bo = bank(); bd = bank()
                    lo = []; ld = []
                    for g in range(2):
                        lo.append((ps[bo][:, 0:128], vpad[g][:, 1, :], pT[:, g * 128:(g + 1) * 128], g == 0, g == 1))
                        ld.append((ps[bd][:, 0:128], onesh[g][:], pT[:, g * 128:(g + 1) * 128], g == 0, g == 1))
                    for sq in range(NSEQ):
                        for g in range(2):
                            c0_ = 256 + g * 128 + sq * TS
                            lo.append((ps[bo][:, 128 + sq * TS:128 + (sq + 1) * TS], vcp[g][:, sq, :], pT[:, c0_:c0_ + TS], g == 0, g == 1))
                            ld.append((ps[bd][:, 128 + sq * TS:128 + (sq + 1) * TS], onesh[g][:], pT[:, c0_:c0_ + TS], g == 0, g == 1))
                    P.op("pe", MM(lo), [("tC", t), ("vpad", 0), ("vpad", 1), ("vcp", 0), ("vcp", 1)], [("ps", bo)])
                    P.op("pe", MM(ld), [("tC", t), ("onesh", 0), ("onesh", 1)], [("ps", bd)])
                    t2 = tmpi()
                    dn = tmpA[t2]; on = tmpB[t2]
                    P.op("dve", I("tensor_scalar", dn[:, 0:128], ps[bd][:, 0:128], vcol(sinkn, c), None, ALU.add), [("ps", bd), "vec"], [("tA", t2)])
                    P.op("dve", I("tensor_tensor", dn[:, 0:128], dn[:, 0:128], ps[bd][:, 128:256], ALU.add), [("ps", bd), ("tA", t2)], [("tA", t2)])
                    P.op("act", I("activation", on[:, 0:128], ps[bo][:, 0:128], AF.Copy), [("ps", bo)], [("tB", t2)])
                    P.op("dve", I("tensor_tensor", on[:, 0:128], on[:, 0:128], ps[bo][:, 128:256], ALU.add), [("ps", bo), ("tB", t2)], [("tB", t2)])
                    P.op("dve", I("reciprocal", dn[:, 0:128], dn[:, 0:128]), [("tA", t2)], [("tA", t2)])
                    P.op("dve", I("tensor_tensor", mixT[:, c, 0:128], on[:, 0:128], dn[:, 0:128], ALU.mult), [("tA", t2), ("tB", t2)], [("mix", c)] + P1K)
            if not sample:
                for g in range(2):
                    P.op("dve", I("tensor_copy", kcar[j][g][:], kpad[g][:, 8 * 128:9 * 128]), ["kbuf"], [("kcar", j, g)])
                    P.op("dve", I("tensor_copy", vcar[j][g][:], vpad[g][:, 8, :]), [("vpad", g)], [("vcar", j, g)])
            out_proj_add(wo[j], tiles, "bo%d" % j)

        PAIRS = [[0, 1], [2, 3], [4, 5], [6, 7]]
        xch = {}

        def exchange(idx, F):
            cin = nc.dram_tensor("cc_in%d" % idx, [128, F], F32)
            cout = nc.dram_tensor("cc_out%d" % idx, [128, F], F32)
            P.op("dve", I("tensor_scalar", xsend[:, 0:F], xsend[:, 0:F], flg[:, 0:1], None, ALU.mult), ["xsend", "flg"], ["xsend"])
            P.dma("sp", cin[:, :], xsend[:, 0:F], ["xsend"], [("cin", idx)], ("cin", idx))
            P.cc(lambda e: e.collective_compute("AllReduce", ALU.add, replica_groups=PAIRS, ins=[cin.ap().opt()], outs=[cout.ap().opt()]),
                 [("cin", idx)], [("cout", idx)], ("cc", idx))
            xch[idx] = (cout, F)

        def receive(idx):
            cout, F = xch[idx]
            P.dma("sp", xrecv[:, 0:F], cout[:, :], [("cout", idx)], ["xrecv"], ("rcv", idx))

        def kv_block(li, j, o):
            P.barrier()
            rmsnorm("nm%d" % li, [(o, 128)])
            n = 128
            s_ = wload(wqkv[j][:, 1024:1152], KC, 128)
            b = bank()
            proj_chunk(s_, KC, 0, hT, o, n, b, HK)
            t = tmpi()
            P.op("act", I("activation", tmpA[t][:, 0:n], ps[b][:, 0:n], AF.Identity, bias=vcol("bk%d" % j, 0), scale=1.0), [("ps", b), "vec"], [("tA", t)])
            P.op("act", I("activation", tmpC[t][:, 0:n], tmpA[t][:, 0:n], AF.Copy), [("tA", t)], [("tC", t)])
            b2 = bank()
            P.op("pe", I("matmul", ps[b2][:, 0:n], rot[:], tmpC[t][:, 0:n], start=True, stop=True), [("tC", t), "rot"], [("ps", b2)])
            P.op("dve", I("tensor_tensor", tmpA[t][:, 0:n], tmpA[t][:, 0:n], cosT[:, o:o + n], ALU.mult), [("tA", t), "cos"], [("tA", t)])
            P.op("dve", I("tensor_tensor", tmpB[t][:, 0:n], ps[b2][:, 0:n], sinT[:, o:o + n], ALU.mult), [("ps", b2), "sin"], [("tB", t)])
            P.op("dve", I("tensor_tensor", xsend[:, 0:128], tmpA[t][:, 0:n], tmpB[t][:, 0:n], ALU.add), [("tA", t), ("tB", t)], ["xsend"])
            sv = wload(wqkv[j][:, 1152:1280], KC, 128)
            b = bank()
            P.op("pe", MM([(ps[b][:, 0:128], hT[:, k, o:o + 128], ws[sv][:, k, 0:128], k == 0, k == KC - 1) for k in range(KC)]), [("ws", sv)] + HK, [("ps", b)])
            P.op("dve", I("tensor_tensor", xsend[:, 128:256], ps[b][:, 0:128], bvt[:], ALU.add), [("ps", b), "bvt"], ["xsend"])

        def set_carry(j, src, skey):
            for g in range(2):
                P.op("act", I("activation", kcar[j][g][g * 64:(g + 1) * 64, :], src[g * 64:(g + 1) * 64, 0:128], AF.Copy), [skey], [("kcar", j, g)])
                P.op("act", I("activation", vcar[j][g][:, g * 64:(g + 1) * 64], src[:, 128 + g * 64:128 + (g + 1) * 64], AF.Copy), [skey], [("vcar", j, g)])

        def pre_sconv(li, o, dest, dkeys):
            P.barrier()
            rmsnorm("nm%d" % li, [(o, 128)])
            n = 128
            for c in range(KC):
                i = wslot()
                P.dma("pool", ws[i][:, :, 0:256], swi[:, c * 384 + 128:(c + 1) * 384].rearrange("(kc p) n -> p kc n", p=128), [], [("ws", i)], ("ws", i))
                bc = bank(); bx = bank()
                proj_chunk(i, KC, 0, hT, o, n, bc, HK)
                proj_chunk(i, KC, 128, hT, o, n, bx, HK)
                t = tmpi()
                P.op("act", I("activation", tmpA[t][:, 0:n], ps[bc][:, 0:n], AF.Copy), [("ps", bc)], [("tA", t)])
                P.op("dve", I("tensor_tensor", dest[:, 2 * c:2 * c + 2], tmpA[t][:, n - 2:n], ps[bx][:, n - 2:n], ALU.mult), [("tA", t), ("ps", bx)], dkeys(c))

        PT = [(0, 512), (512, 512)]
        ST = [(0, NS)]
        PST = PT + [(GT, NS)]
        def setg(g):
            cur["g"] = g
            cur["xb"] = {0: 0, "s": GT, "s1": GT, 1: GT + NS, "h": 0}[g]

        def load_x(src, tok0, tiles):
            o0 = tiles[0][0]
            n_all = sum(n for _, n in tiles)
            if cur["g"] == "h":
                dst = xhv[:, :, o0:o0 + n_all]
            else:
                dst = xT[:, :, cur["xb"] + o0:cur["xb"] + o0 + n_all]
            P.dma("sp", dst, src[:, tok0 + o0:tok0 + o0 + n_all].rearrange("(c p) t -> p c t", p=128), [], [XKc(c) for c in range(KC)], ("xin", cur["g"]), ms=(cur["g"] == "h"))

        setg("h"); load_x(xhT, 0, [(0, 128)])
        setg("s"); load_x(xsT, 0, ST)
        for g in range(NG):
            setg(g); load_x(xpT, g * GT, PT)
        CK = lambda nm: [(nm, c) for c in range(KC)]
        P.op("dve", I("memset", zcar[:].rearrange("p a b -> p (a b)"), 0.0), [], CK("zcar"))
        P.op("dve", I("memset", hcar[:], 0.0), [], CK("hcar"))
        P.op("dve", I("memset", ucar[:].rearrange("p a b -> p (a b)"), 0.0), [], CK("ucar"))
        LG = NG - 1
        setg("h")
        P.barrier()
        P.dma("sp", cosT[:, 0:128], cosH[:, :], [], ["cos"], "cs_cos", ms=True)
        P.dma("sp", sinT[:, 0:128], sinH[:, :], [], ["sin"], "cs_sin", ms=True)
        P.dma("sp", bvt[:], bvb[0], [], ["bvt"], "cs_bvt")
        kv_block(0, 0, 0)
        set_carry(0, xsend, "xsend")
        setg("s"); attn(0, 0, ST, True, False, True, 0)
        setg(0); attn(0, 0, PT, False, True, False, 0)
        ffn(0, PST)
        setg(1); attn(0, 0, PT, False, False, True, GT); ffn(0, PT)
        for g in range(NG):
            setg(g); rglru(1, PT, False, False, phase1=True)
        P.dma("sp", o_rcp[:, :, :], ucar[:], CK("ucar"), [], "o_rcp")
        P.op("dve", I("tensor_copy", xsend[:, 0:8], hcar[:]), CK("hcar"), ["xsend"])
        P.op("dve", I("tensor_copy", xsend[:, 8:32], ucar[:].rearrange("p a b -> p (a b)")), CK("ucar"), ["xsend"])
        exchange(0, 32)
        setg("s"); rglru(1, ST, True, True)
        receive(0)
        P.op("dve", I("tensor_scalar", hcar[:], xrecv[:, 0:8], flg[:, 1:2], None, ALU.mult), ["xrecv", "flg"], CK("hcar"))
        P.op("dve", I("tensor_scalar", ucar[:].rearrange("p a b -> p (a b)"), xrecv[:, 8:32], flg[:, 1:2], None, ALU.mult), ["xrecv", "flg"], CK("ucar"))
        setg(0); rglru(1, [(0, 128)], False, False, phase1=True, scan=False)
        setg(0); rglru_p2(1, PT, False); ffn(1, PST)
        setg(1); rglru_p2(1, PT, True); ffn(1, PT)
        S1T = [(0, NS), (NS, 512), (NS + 512, 512)]
        setg(1); pre_sconv(2, GT - 128, xsend, lambda c: ["xsend"])
        exchange(1, 16)
        setg(0); pre_sconv(2, GT - 128, zcar[:].rearrange("p a b -> p (a b)"), lambda c: [("zcar", c)])
        setg("s1"); sconv(2, S1T, False, True, stiles=(0,)); ffn(2, S1T)
        setg(LG)
        P.barrier()
        P.dma("sp", cosT[:, GT - 128:GT], cosP[:, TP - 128:TP], [], ["cos"], "cs_cos", ms=True)
        P.dma("sp", sinT[:, GT - 128:GT], sinP[:, TP - 128:TP], [], ["sin"], "cs_sin", ms=True)
        P.dma("sp", bvt[:], bvb[1], [], ["bvt"], "cs_bvt")
        kv_block(3, 1, GT - 128)
        exchange(2, 256)
        receive(1)
        P.op("dve", I("tensor_scalar", zcar[:].rearrange("p a b -> p (a b)"), xrecv[:, 0:16], flg[:, 1:2], None, ALU.mult), ["xrecv", "flg"], CK("zcar"))
        setg(0); sconv(2, PT, False, False); ffn(2, PT)
        setg("s"); attn(3, 1, ST, True, False, True, 0)
        receive(2)
        set_carry(1, xrecv, "xrecv")
        setg(0); attn(3, 1, PT, False, True, False, 0); ffn(3, PST)
        rmsnorm_final(PT, ypT, 0)
        setg("s"); rmsnorm_final(ST, ysT, 0)
        setg(1); attn(3, 1, PT, False, False, True, GT); ffn(3, PT); rmsnorm_final(PT, ypT, GT)
        P.final_wait("sp")
        block = es.enter_context(nc.Block())
        P.emit(block)
    return nc


def _fm(v):
    return np.ascontiguousarray(np.asarray(v, np.float32).reshape(8, 128).T)


def _host_consts(TP, start, half):
    hd = 32
    inv = (np.float32(10000.0) ** (-(np.arange(hd, dtype=np.float32)) / np.float32(hd))).astype(np.float32)

    def tables(pos):
        ang = (pos.astype(np.float32)[:, None] * inv[None, :]).astype(np.float32)
        cos = np.cos(ang.astype(np.float64)).astype(np.float32).T
        sin = np.sin(ang.astype(np.float64)).astype(np.float32).T
        return np.ascontiguousarray(np.tile(cos, (4, 1))), np.ascontiguousarray(np.tile(sin, (4, 1)))
    cosP, sinP = tables(start + np.arange(TP))
    cosH, sinH = tables(np.maximum(start - 128 + np.arange(128), 0))
    cosS, sinS = tables(PAST + (np.arange(NS) % TS))
    rot = np.zeros((128, 128), np.float32)
    for blk in range(2):
        for d in range(64):
            m = blk * 64 + d
            if d < 32:
                rot[blk * 64 + d + 32, m] = -1.0
            else:
                rot[blk * 64 + d - 32, m] = 1.0
    k = np.arange(128)[:, None]
    q = np.arange(128)[None, :]
    prev = (k > q).astype(np.float32)
    cur = (k <= q).astype(np.float32)
    zero = np.zeros_like(prev)
    NEG = np.float32(-30000.0)
    maskP = (np.concatenate([prev, prev, cur, cur], axis=1) - 1.0) * (-NEG)
    maskF = maskP if half else (np.concatenate([zero, zero, cur, cur], axis=1) - 1.0) * (-NEG)
    mnew = ((k // TS == q // TS) & (k % TS <= q % TS)).astype(np.float32)
    mc = (k > (q % TS)).astype(np.float32)
    maskS = np.concatenate([mnew, mnew, mc, mc], axis=1)
    cst = np.ascontiguousarray(np.concatenate([rot, maskP, maskF, maskS, np.eye(128, dtype=np.float32)], axis=1).astype(np.float32))
    flags = np.zeros((128, 2), np.float32)
    flags[:, 0] = 1.0 - half
    flags[:, 1] = float(half)
    return dict(cosP=cosP, sinP=sinP, cosH=cosH, sinH=sinH, cosS=cosS, sinS=sinS, cst=cst, flags=flags)


def _prep_shared(inp):
    f = lambda a: np.ascontiguousarray(np.asarray(a, np.float32))
    vec = np.zeros((128, NV), np.float32)

    def put(name, v):
        vec[:, VEC[name]:VEC[name] + 8] = _fm(v)
    for i in range(4):
        put("nm%d" % i, inp["norm_mixer"][i]); put("nf%d" % i, inp["norm_ffn"][i])
    put("nfin", inp["norm_final"])
    perm = np.concatenate([np.arange((g * 8 + c) * 64, (g * 8 + c) * 64 + 64) for c in range(8) for g in range(2)])
    wq = np.asarray(inp["attn_w_qkv"], np.float32)
    bq = np.asarray(inp["attn_b_qkv"], np.float32)
    wqkv = f(np.concatenate([wq[:, :, perm], wq[:, :, 1024:]], axis=2))
    wo = f(np.asarray(inp["attn_w_o"], np.float32)[:, perm, :])
    bvb = np.zeros((2, 128, 128), np.float32)
    for j in range(2):
        put("bq%d" % j, bq[j][perm])
        vec[:, VEC["bk%d" % j]] = bq[j][1024:1152]
        put("bo%d" % j, inp["attn_b_o"][j])
        sk = np.asarray(inp["attn_sinks"], np.float32)[j]
        for c in range(8):
            vec[0:64, VEC["sink%d" % j] + c] = sk[c]
            vec[64:128, VEC["sink%d" % j] + c] = sk[c + 8]
        bvb[j] = np.broadcast_to(bq[j][1152:1280][None, :], (128, 128))
    for jj in range(4):
        put("rcw%d" % jj, inp["rglru_conv_w"][0][jj])
    put("rcb", inp["rglru_conv_b"][0]); put("rba", inp["rglru_ba"][0]); put("rbx", inp["rglru_bx"][0]); put("rlam", inp["rglru_lambda"][0])
    for jj in range(3):
        put("scw%d" % jj, inp["sconv_conv_w"][0][jj])
    sh = dict(vec=vec, bvb=bvb, wqkv=wqkv, wo=wo,
              rwg=f(inp["rglru_w_gate"][0]), rwi=f(inp["rglru_w_in"][0]), rwax=f(np.concatenate([np.asarray(inp["rglru_wa"][0], np.float32), np.asarray(inp["rglru_wx"][0], np.float32)], axis=2)),
              rwo=f(inp["rglru_w_out"][0]), swi=f(np.asarray(inp["sconv_w_in"][0], np.float32).reshape(D, 3, 8, 128).transpose(0, 2, 1, 3).reshape(D, 3 * D)), swo=f(inp["sconv_w_out"][0]),
              fwg=f(inp["ffn_w_gate"]), fwu=f(inp["ffn_w_up"]), fwd=f(inp["ffn_w_down"]))
    return sh


def run(inp, SEQ=4096, n_cores=8):
    TP = SEQ // 2
    nc = build(TP)
    shared = _prep_shared(inp)
    f = lambda a: np.ascontiguousarray(np.asarray(a, np.float32))
    xp = np.asarray(inp["x_prompt"], np.float32); xs = np.asarray(inp["x_sample"], np.float32)
    ck = np.asarray(inp["cache_k"], np.float32).reshape(2, 128, 128, 128)
    cv = np.asarray(inp["cache_v"], np.float32).reshape(2, 128, 128, 128)
    srh = np.asarray(inp["state_rglru_h"], np.float32)[0]
    src = np.asarray(inp["state_rglru_conv"], np.float32)[0]
    ssc = np.asarray(inp["state_shortconv"], np.float32)[0]
    in_maps = []
    for c in range(n_cores):
        seq, half = c // 2, c % 2
        start = half * TP
        b0 = c * NSEQ
        m = dict(shared)
        m.update(_host_consts(TP, start, half))
        m["xpT"] = f(xp[seq, start:start + TP].T)
        m["xhT"] = f(xp[seq, start - 128:start].T) if half else np.zeros((D, 128), np.float32)
        m["xsT"] = f(xs[b0:b0 + NSEQ].reshape(NS, D).T)
        m["ckN"] = f(ck[:, b0:b0 + NSEQ]); m["cvN"] = f(cv[:, b0:b0 + NSEQ])
        m["ckT"] = f(ck[:, b0:b0 + NSEQ].transpose(0, 3, 1, 2))
        m["sh"] = f(srh[b0:b0 + NSEQ].reshape(NSEQ, 8, 128).transpose(2, 1, 0))
        m["src"] = f(src[b0:b0 + NSEQ].reshape(NSEQ, 3, 8, 128).transpose(3, 2, 0, 1))
        m["ssc"] = f(ssc[b0:b0 + NSEQ].reshape(NSEQ, 2, 8, 128).transpose(3, 2, 0, 1))
        in_maps.append(m)
    res = run_bass_kernel_spmd(nc, in_maps, core_ids=list(range(n_cores)))
    R = res.results
    NB = n_cores * NSEQ
    nP = n_cores // 2
    y_p = np.stack([np.concatenate([R[2 * s]["ypT"].T, R[2 * s + 1]["ypT"].T], axis=0) for s in range(nP)])
    y_s = np.concatenate([R[c]["ysT"].T.reshape(NSEQ, TS, D) for c in range(n_cores)])
    L = lambda s: R[2 * s + 1]
    kp = np.stack([np.stack([L(s)["o_kp"][j].T.reshape(128, 2, 64) for s in range(nP)]) for j in range(2)])
    vp = np.stack([np.stack([L(s)["o_vp"][j].reshape(128, 2, 64) for s in range(nP)]) for j in range(2)])

    def samp(sh_name, new_name, transpose):
        out = np.zeros((2, NB, 128, 128), np.float32)
        for j in range(2):
            for c in range(n_cores):
                out[j, c * NSEQ:(c + 1) * NSEQ, 0:120] = R[c][sh_name][j]
                nw = R[c][new_name][j]
                nw = nw.T if transpose else nw
                out[j, c * NSEQ:(c + 1) * NSEQ, 120:128] = nw.reshape(NSEQ, TS, 128)
        return out.reshape(2, NB, 128, 2, 64)
    ks = samp("o_ksh", "o_ks", True)
    vs = samp("o_vsh", "o_vs", False)
    hp = np.stack([L(s)["o_hp"].T.reshape(D) for s in range(nP)])[None]
    hs = np.concatenate([R[c]["o_hs"].transpose(2, 1, 0).reshape(NSEQ, D) for c in range(n_cores)])[None]
    rcp = np.stack([L(s)["o_rcp"].transpose(2, 1, 0).reshape(3, D) for s in range(nP)])[None]
    rcs = np.concatenate([R[c]["o_rcs"].transpose(2, 3, 1, 0).reshape(NSEQ, 3, D) for c in range(n_cores)])[None]
    scp = np.stack([L(s)["o_scp"].transpose(2, 1, 0).reshape(2, D) for s in range(nP)])[None]
    scs = np.concatenate([R[c]["o_scs"].transpose(2, 3, 1, 0).reshape(NSEQ, 2, D) for c in range(n_cores)])[None]
    outs = (y_p, y_s, kp, vp, ks, vs, hp, hs, rcp, rcs, scp, scs)
    return tuple(np.ascontiguousarray(o.astype(np.float32)) for o in outs)


def kernel(**inputs):
    return run(inputs, 4096, 8)
```

```python
import numpy as np
from contextlib import ExitStack
import concourse.bass as bass
import concourse.mybir as mybir
from concourse.bass_utils import run_bass_kernel_spmd

F32 = mybir.dt.float32
BF16 = mybir.dt.bfloat16
AF = mybir.ActivationFunctionType
ALU = mybir.AluOpType

D = 1024
KC = 8
DFF = 2816
GT = 1024
NSEQ = 16
TS = 8
NS = NSEQ * TS
EPS = 1e-6
PAST = 8192
import os
ATT_STAGE = int(os.environ.get('ATT_STAGE', '9'))
NWS = 4

VEC = {}
_nv = 0
def _v(name, n=8):
    global _nv
    VEC[name] = _nv
    _nv += n
for _i in range(4):
    _v("nm%d" % _i); _v("nf%d" % _i)
_v("nfin")
for _j in range(2):
    _v("bq%d" % _j); _v("bk%d" % _j, 1); _v("bo%d" % _j); _v("sink%d" % _j)
for _n in ["rcw0", "rcw1", "rcw2", "rcw3", "rcb", "rba", "rbx", "rlam", "scw0", "scw1", "scw2"]:
    _v(_n)
NV = _nv


def I(name, *args, **kw):
    return lambda e: getattr(e, name)(*args, **kw)


def MM(lst):
    lst = list(lst)

    def f(e):
        ins = None
        for (o_, l_, r_, st, sp) in lst:
            ins = e.matmul(o_, l_, r_, start=st, stop=sp)
        return ins
    return f


class Prog:
    ENG = ["pe", "act", "dve", "pool", "sp"]

    def __init__(self, nc, es):
        self.nc = nc
        self.es = es
        self.q = {e: [] for e in self.ENG}
        self.esem = {e: es.enter_context(nc.semaphore("s_" + e)) for e in self.ENG}
        self.ecnt = {e: 0 for e in self.ENG}
        self.seen = {e: {} for e in self.ENG}
        self.lastw = {}
        self.readers = {}
        self.dsem = {}
        self.dcnt = {}
        self.semobj = {}
        self.bar = {}
        self.msdma = {}

    def barrier(self):
        self.bar = {("e_" + e): self.ecnt[e] for e in ("pe", "act", "dve") if self.ecnt[e] > 0}
        self.bar.update(self.msdma)
        for e in ("pe", "act", "dve"):
            self.semobj["e_" + e] = self.esem[e]

    def _deps(self, eng, reads, writes, ms=True):
        toks = []
        if ms:
            toks += list(self.bar.items())
        for k in reads:
            if k in self.lastw:
                toks.append(self.lastw[k])
        for k in writes:
            if k in self.lastw:
                toks.append(self.lastw[k])
            toks += self.readers.get(k, [])
        need = {}
        for (sid, val) in toks:
            if val > need.get(sid, 0):
                need[sid] = val
        waits = []
        for sid, val in need.items():
            if self.seen[eng].get(sid, 0) >= val:
                continue
            self.seen[eng][sid] = val
            waits.append((self.semobj[sid], val))
        return waits

    def _commit(self, tok, reads, writes):
        for k in reads:
            self.readers.setdefault(k, []).append(tok)
        for k in writes:
            self.lastw[k] = tok
            self.readers[k] = []

    def op(self, eng, fn, reads=(), writes=(), ms=True):
        waits = self._deps(eng, reads, writes, ms)
        self.ecnt[eng] += 1
        sid = "e_" + eng
        self.semobj[sid] = self.esem[eng]
        tok = (sid, self.ecnt[eng])
        self.q[eng].append((waits, fn, self.esem[eng], 1))
        self._commit(tok, reads, writes)

    def dma(self, eng, out, in_, reads, writes, skey, ms=False, **kw):
        if skey not in self.dsem:
            self.dsem[skey] = self.es.enter_context(self.nc.semaphore("d_%d" % len(self.dsem)))
            self.dcnt[skey] = 0
        sid = "d_" + str(skey)
        self.semobj[sid] = self.dsem[skey]
        waits = self._deps(eng, reads, writes, ms)
        self.dcnt[skey] += 16
        tok = (sid, self.dcnt[skey])
        if ms:
            self.msdma[sid] = self.dcnt[skey]
        self.q[eng].append((waits, (lambda e: e.dma_start(out=out, in_=in_, **kw)), self.dsem[skey], 16))
        self._commit(tok, reads, writes)

    def cc(self, fn, reads, writes, skey):
        self.dsem[skey] = self.es.enter_context(self.nc.semaphore("d_%d" % len(self.dsem)))
        self.dcnt[skey] = 0
        sid = "d_" + str(skey)
        self.semobj[sid] = self.dsem[skey]
        waits = self._deps("pool", reads, writes, False)
        self.dcnt[skey] += 1
        self.q["pool"].append((waits, fn, self.dsem[skey], None))
        self._commit((sid, self.dcnt[skey]), reads, writes)

    def final_wait(self, eng):
        waits = []
        for skey, sem in self.dsem.items():
            if self.dcnt[skey] > 0:
                waits.append((sem, self.dcnt[skey]))
        self.q[eng].append((waits, None, None, 0))

    def emit(self, block):
        def mk(ename):
            def run(e):
                for (waits, fn, sem, inc) in self.q[ename]:
                    for (s, v) in waits:
                        e.wait_ge(s, v)
                    if fn is not None:
                        ins = fn(e)
                        if inc is None:
                            ins.then_inc(sem)
                        else:
                            ins.then_inc(sem, inc)
            return run
        block.tensor(mk("pe"))
        block.scalar(mk("act"))
        block.vector(mk("dve"))
        block.gpsimd(mk("pool"))
        block.sync(mk("sp"))


def build(TP=2048, layers=4, do_sample=True):
    assert TP % GT == 0
    NG = TP // GT
    assert NG == 2
    XS = GT
    XTOT = TP + NS
    HW_ = GT + NS
    nc = bass.Bass("TRN2", target_bir_lowering=False)

    def din(name, shape):
        return nc.dram_tensor(name, list(shape), F32, kind="ExternalInput").ap()

    def dout(name, shape):
        return nc.dram_tensor(name, list(shape), F32, kind="ExternalOutput").ap()

    xpT = din("xpT", [D, TP]); xsT = din("xsT", [D, NS])
    ckT = din("ckT", [2, 128, NSEQ, 128]); ckN = din("ckN", [2, NSEQ, 128, 128])
    cvN = din("cvN", [2, NSEQ, 128, 128])
    sh = din("sh", [128, KC, NSEQ]); src = din("src", [128, KC, NSEQ, 3]); ssc = din("ssc", [128, KC, NSEQ, 2])
    vec_d = din("vec", [128, NV])
    cosP = din("cosP", [128, TP]); sinP = din("sinP", [128, TP])
    cosS = din("cosS", [128, NS]); sinS = din("sinS", [128, NS])
    cst = din("cst", [128, 128 + 512 * 3 + 128])
    xhT = din("xhT", [D, 128]); cosH = din("cosH", [128, 128]); sinH = din("sinH", [128, 128])
    flags_d = din("flags", [128, 2])
    bvb = din("bvb", [2, 128, 128])
    wq_pm = din("wq_pm", [2, 2, 128, 4096]); wk_pm = din("wk_pm", [2, 128, 1024]); wvv_pm = din("wvv_pm", [2, 128, 1024])
    wo_pm = din("wo_pm", [2, 2, 128, 4096])
    rwi_pm = din("rwi_pm", [2, 128, 4096]); rwg_pm = din("rwg_pm", [4, 128, 2048]); rwax_pm = din("rwax_pm", [4, 128, 1024])
    rwo_pm = din("rwo_pm", [2, 128, 4096])
    swi_pm = din("swi_pm", [8, 128, 3072]); swo_pm = din("swo_pm", [2, 128, 4096])
    fwg_a = din("fwg_a", [4, 5, 128, 4096]); fwg_b = din("fwg_b", [4, 128, 2048])
    fwu_a = din("fwu_a", [4, 5, 128, 4096]); fwu_b = din("fwu_b", [4, 128, 2048])
    fwd_a = din("fwd_a", [4, 5, 2, 128, 2048]); fwd_b = din("fwd_b", [4, 2, 128, 1024])

    ypT = dout("ypT", [D, TP]); ysT = dout("ysT", [D, NS])
    o_kp = dout("o_kp", [2, 128, 128]); o_vp = dout("o_vp", [2, 128, 128])
    o_ks = dout("o_ks", [2, 128, 128]); o_vs = dout("o_vs", [2, 128, 128])
    o_ksh = dout("o_ksh", [2, NSEQ, 120, 128]); o_vsh = dout("o_vsh", [2, NSEQ, 120, 128])
    o_hp = dout("o_hp", [128, KC]); o_hs = dout("o_hs", [128, KC, NSEQ])
    o_rcp = dout("o_rcp", [128, KC, 3]); o_rcs = dout("o_rcs", [128, KC, NSEQ, 3])
    o_scp = dout("o_scp", [128, KC, 2]); o_scs = dout("o_scs", [128, KC, NSEQ, 2])

    es = ExitStack()
    with es:
        P = Prog(nc, es)

        def sb(name, shape, dt=F32):
            return es.enter_context(nc.sbuf_tensor("sb_" + name, list(shape), dt))

        xT = sb("xT", [128, KC, XTOT])
        hT = sb("hT", [128, KC, HW_], BF16)
        mixT = sb("mixT", [128, KC, HW_], BF16)
        ws = [sb("ws%d" % i, [128, KC, 512], BF16) for i in range(NWS)]
        vec = sb("vec", [128, NV])
        cvec = sb("cvec", [128, 40])
        flg = sb("flg", [128, 2])
        rot = sb("rot", [128, 128], BF16)
        maskP = sb("maskP", [128, 512], BF16); maskF = sb("maskF", [128, 512], BF16); maskS = sb("maskS", [128, 512], BF16)
        ones = sb("ones", [128, 128], BF16)
        ident = sb("ident", [128, 128], BF16)
        onesh = [sb("onesh%d" % g, [128, 128], BF16) for g in range(2)]
        bvt = sb("bvt", [128, 128])
        vf = sb("vf", [128, 128])
        kcar = [[sb("kcar%d_%d" % (j, g), [128, 128], BF16) for g in range(2)] for j in range(2)]
        vcar = [[sb("vcar%d_%d" % (j, g), [128, 128], BF16) for g in range(2)] for j in range(2)]
        NTMP = 3
        tmpA = [sb("tmpA%d" % i, [128, 512]) for i in range(NTMP)]
        tmpB = [sb("tmpB%d" % i, [128, 512]) for i in range(NTMP)]
        tmpC = [sb("tmpC%d" % i, [128, 512], BF16) for i in range(NTMP)]
        zcar = sb("zcar", [128, KC, 2])
        hcar = sb("hcar", [128, KC])
        ucar = sb("ucar", [128, KC, 3])
        sst = sb("sst", [128, KC, NSEQ, 3])
        shs = sb("shs", [128, KC, NSEQ])
        seq_ext = sb("seq_ext", [128, NSEQ, 3 + TS])
        xsend = sb("xsend", [128, 256]); xrecv = sb("xrecv", [128, 256])
        MSN = 10880
        MS = sb("MS", [128, MSN])

        def msv(lo, n, dt=F32):
            v = MS[:, lo:lo + n]
            return v.bitcast(dt) if dt != F32 else v
        qT = msv(0, 4096, BF16).rearrange("p (c t) -> p c t", c=KC)
        cosT = msv(4096, 1024); sinT = msv(5120, 1024)
        kf = msv(6144, 1024)
        kpad = [msv(7168 + g * 576, 576, BF16) for g in range(2)]
        vpad = [msv(8320 + g * 576, 576, BF16).rearrange("p (a b) -> p a b", a=9) for g in range(2)]
        AV_P = (cosT, sinT, kf, kpad, vpad)
        AV_S = (msv(4096, 128), msv(4224, 128), msv(4352, 128),
                [msv(4480 + g * 576, 576, BF16) for g in range(2)],
                [msv(5632 + g * 576, 576, BF16).rearrange("p (a b) -> p a b", a=9) for g in range(2)])
        pTb = [msv(9472 + i * 256, 256, BF16) for i in range(4)]
        dnb = [msv(10496 + i * 128, 128) for i in range(3)]
        kcp = [msv(6784 + g * 1024, 1024, BF16).rearrange("p (a b) -> p a b", a=NSEQ) for g in range(2)]
        vcp = [msv(8832 + g * 1024, 1024, BF16).rearrange("p (a b) -> p a b", a=NSEQ) for g in range(2)]
        uT = msv(0, 4096).rearrange("p (c t) -> p c t", c=KC)
        uext = msv(4096, 4120).rearrange("p (c t) -> p c t", c=KC)
        ubT = msv(8216, 2048, BF16).rearrange("p (c t) -> p c t", c=KC)
        zext = msv(0, 514)
        act_t = msv(0, 2 * HW_, BF16).rearrange("p (c t) -> p c t", c=4)
        xhv = msv(6144, 1024).rearrange("p (c t) -> p c t", c=KC)
        cur = {"xb": 0, "g": 0}
        epsc = sb("epsc", [128, 1]); onec = sb("onec", [128, 1])
        ps = [es.enter_context(nc.psum_tensor("ps%d" % i, [128, 512], F32)) for i in range(8)]
        psn = [0]

        def bank():
            i = psn[0] % 8
            psn[0] += 1
            return i

        tn = [0]
        pn = [0]
        dq = [0]

        def tmpi():
            i = tn[0] % NTMP
            tn[0] += 1
            return i

        def vcol(name, c=0):
            o = VEC[name] + c
            return vec[:, o:o + 1]

        def XV(c, o, n):
            if cur["g"] == "h":
                return xhv[:, c, o:o + n]
            return xT[:, c, cur["xb"] + o:cur["xb"] + o + n]

        def XKc(c, o=0):
            g = cur["g"]
            if g == 0 and o >= GT:
                g = "s"
            if g == "s1":
                g = "s" if o < NS else 1
            return ("x", g, c)

        def s3(ap):
            return ap.rearrange("p (b t) -> p b t", t=TS)

        HK = [("h", c) for c in range(KC)]

        P.dma("sp", vec[:], vec_d[:, :], [], ["vec"], "c0")
        P.dma("pool", rot[:], cst[:, 0:128], [], ["rot"], "c_rot")
        P.dma("pool", maskP[:], cst[:, 128:640], [], ["maskP"], "c_mp")
        P.dma("pool", maskF[:], cst[:, 640:1152], [], ["maskF"], "c_mf")
        P.dma("pool", maskS[:], cst[:, 1152:1664], [], ["maskS"], "c_ms")
        P.dma("pool", ident[:], cst[:, 1664:1792], [], ["ident"], "c_id")
        P.op("dve", I("memset", ones[:], 1.0), [], ["ones"])
        P.op("dve", I("memset", epsc[:], EPS), [], ["epsc"])
        P.op("dve", I("memset", onec[:], 1.0), [], ["onec"])
        for g in range(2):
            P.op("dve", I("memset", onesh[g][:], 0.0), [], [("onesh", g)])
            P.op("dve", I("memset", onesh[g][:, g * 64:(g + 1) * 64], 1.0), [], [("onesh", g)])
            for j in range(2):
                P.op("dve", I("memset", kcar[j][g][:], 0.0), [], [("kcar", j, g)])
                P.op("dve", I("memset", vcar[j][g][:], 0.0), [], [("vcar", j, g)])
        P.dma("sp", flg[:], flags_d[:, :], [], ["flg"], "c_flg")
        lam = vec[:, VEC["rlam"]:VEC["rlam"] + 8]
        P.op("act", I("activation", tmpA[0][:, 0:8], lam, AF.Exp, scale=-1.0), ["vec"], [("tA", 0)])
        P.op("act", I("activation", tmpA[0][:, 8:16], tmpA[0][:, 0:8], AF.Ln, bias=onec[:, 0:1], scale=1.0), [("tA", 0), "onec"], [("tA", 0)])
        P.op("dve", I("tensor_scalar", cvec[:, 0:8], tmpA[0][:, 8:16], -8.0, None, ALU.mult), [("tA", 0)], ["cvec"])
        P.op("dve", I("tensor_scalar", cvec[:, 8:16], tmpA[0][:, 8:16], -16.0, None, ALU.mult), [("tA", 0)], ["cvec"])
        P.op("dve", I("tensor_scalar", cvec[:, 16:24], tmpA[0][:, 8:16], -4.0, None, ALU.mult), [("tA", 0)], ["cvec"])
        P.op("dve", I("tensor_scalar", cvec[:, 24:32], vec[:, VEC["rba"]:VEC["rba"] + 8], 0.5, None, ALU.mult), ["vec"], ["cvec"])
        P.op("dve", I("tensor_scalar", cvec[:, 32:40], vec[:, VEC["rbx"]:VEC["rbx"] + 8], 0.5, None, ALU.mult), ["vec"], ["cvec"])
        for j in range(2):
            sk = vec[:, VEC["sink%d" % j]:VEC["sink%d" % j] + 8]
            P.op("act", I("activation", sk, sk, AF.Exp), ["vec", "cvec"], ["vec"])

        wn = [0]

        def wslot():
            i = wn[0] % NWS
            wn[0] += 1
            return i

        wflat = [ws[i][:].rearrange("p k n -> p (k n)") for i in range(NWS)]
        wview = {}

        def wload(dram2d, kc, ncols):
            i = wslot()
            P.dma("pool", wflat[i][:, 0:kc * ncols], dram2d, [], [("ws", i)], ("ws", i))
            wview[i] = wflat[i][:, 0:kc * ncols].rearrange("p (k n) -> p k n", k=kc)
            return i

        def WV(i):
            return wview[i]

        def proj_chunk(slot, kc, col0, rhs_t, o, n, b, extra_reads, kbase=0):
            P.op("pe", MM([(ps[b][:, 0:n], WV(slot)[:, kbase + k, col0:col0 + 128], rhs_t[:, k, o:o + n], k == 0, k == kc - 1) for k in range(kc)]),
                 [("ws", slot)] + extra_reads, [("ps", b)])

        def rms_stats(o, n):
            b = bank()
            for c in range(KC):
                t = tmpi()
                P.op("act", I("activation", tmpC[t][:, 0:n], XV(c, o, n), AF.Square), [XKc(c, o)], [("tC", t)], ms=False)
                P.op("pe", I("matmul", ps[b][:, 0:n], ones[:], tmpC[t][:, 0:n], start=(c == 0), stop=(c == KC - 1)), [("tC", t), "ones"], [("ps", b)], ms=False)
            t = tmpi()
            P.op("act", I("activation", tmpA[t][:, 0:n], ps[b][:, 0:n], AF.Sqrt, bias=epsc[:, 0:1], scale=1.0 / D), [("ps", b), "epsc"], [("tA", t)], ms=False)
            P.op("dve", I("reciprocal", tmpB[t][:, 0:n], tmpA[t][:, 0:n]), [("tA", t)], [("tB", t)], ms=False)
            return t

        def rmsnorm(gname, tiles):
            for (o, n) in tiles:
                t = rms_stats(o, n)
                for c in range(KC):
                    P.op("dve", I("scalar_tensor_tensor", hT[:, c, o:o + n], XV(c, o, n), vcol(gname, c), tmpB[t][:, 0:n], ALU.mult, ALU.mult),
                         [XKc(c, o), ("tB", t), "vec"], [("h", c)], ms=False)

        def rmsnorm_final(tiles, outd, tok0):
            for (o, n) in tiles:
                t = rms_stats(o, n)
                for c in range(KC):
                    t2 = tmpi()
                    P.op("dve", I("scalar_tensor_tensor", tmpA[t2][:, 0:n], XV(c, o, n), vcol("nfin", c), tmpB[t][:, 0:n], ALU.mult, ALU.mult),
                         [XKc(c), ("tB", t), "vec"], [("tA", t2)])
                    P.dma("sp", outd[c * 128:(c + 1) * 128, tok0 + o:tok0 + o + n], tmpA[t2][:, 0:n], [("tA", t2)], [], ("out", t2))

        def out_proj_add(w_dram, tiles, bias_name=None):
            for half in range(2):
                s = wload(w_dram[half], KC, 512)
                for (o, n) in tiles:
                    for cc in range(4):
                        c = half * 4 + cc
                        b = bank()
                        proj_chunk(s, KC, cc * 128, mixT, o, n, b, [("mix", k) for k in range(KC)])
                        if bias_name is None:
                            P.op("dve", I("tensor_tensor", XV(c, o, n), XV(c, o, n), ps[b][:, 0:n], ALU.add), [("ps", b), XKc(c, o)], [XKc(c, o)])
                        else:
                            P.op("dve", I("scalar_tensor_tensor", XV(c, o, n), ps[b][:, 0:n], vcol(bias_name, c), XV(c, o, n), ALU.add, ALU.add),
                                 [("ps", b), XKc(c, o), "vec"], [XKc(c, o)])

        def ffn(li, tiles):
            rmsnorm("nf%d" % li, tiles)
            f0 = 0
            while f0 < DFF:
                fw = min(512, DFF - f0)
                nfc = fw // 128
                fgi = f0 // 512
                sg = wload(fwg_a[li, fgi] if fw == 512 else fwg_b[li], KC, fw)
                su = wload(fwu_a[li, fgi] if fw == 512 else fwu_b[li], KC, fw)
                for (o, n) in tiles:
                    for fc in range(nfc):
                        bg = bank(); bu = bank()
                        proj_chunk(sg, KC, fc * 128, hT, o, n, bg, HK)
                        proj_chunk(su, KC, fc * 128, hT, o, n, bu, HK)
                        t = tmpi()
                        P.op("act", I("activation", tmpA[t][:, 0:n], ps[bg][:, 0:n], AF.Silu), [("ps", bg)], [("tA", t)])
                        P.op("dve", I("tensor_tensor", act_t[:, fc, o:o + n], tmpA[t][:, 0:n], ps[bu][:, 0:n], ALU.mult), [("tA", t), ("ps", bu)], [("act", fc)])
                sd = []
                for half in range(2):
                    i = wload(fwd_a[li, fgi, half] if fw == 512 else fwd_b[li, half], nfc, 512)
                    sd.append(i)
                for (o, n) in tiles:
                    for c in range(KC):
                        b = bank()
                        proj_chunk(sd[c // 4], nfc, (c % 4) * 128, act_t, o, n, b, [("act", k) for k in range(nfc)])
                        P.op("dve", I("tensor_tensor", XV(c, o, n), XV(c, o, n), ps[b][:, 0:n], ALU.add), [("ps", b), XKc(c, o)], [XKc(c, o)])
                f0 += fw

        def sconv(li, tiles, sample, last, stiles=()):
            P.barrier()
            rmsnorm("nm%d" % li, tiles)
            SK = [("sst", c) for c in range(KC)]
            has_s = sample or len(stiles) > 0
            has_p = (not sample)
            if has_s:
                P.dma("sp", sst[:, :, :, 0:2], ssc[:, :, :, :], [], SK, "sst")
            for c in range(KC):
                i = wload(swi_pm[c], KC, 384)
                for (o, n) in tiles:
                    bb = bank(); bc = bank(); bx = bank()
                    proj_chunk(i, KC, 0, hT, o, n, bb, HK)
                    proj_chunk(i, KC, 128, hT, o, n, bc, HK)
                    proj_chunk(i, KC, 256, hT, o, n, bx, HK)
                    t = tmpi()
                    P.op("act", I("activation", tmpA[t][:, 0:n], ps[bc][:, 0:n], AF.Copy), [("ps", bc)], [("tA", t)])
                    y = tmpB[t]
                    smp = sample or (o in stiles)
                    if not smp:
                        P.op("dve", I("tensor_copy", zext[:, 0:2], zcar[:, c, :]), [("zcar", c)], ["zext"])
                        P.op("dve", I("tensor_tensor", zext[:, 2:2 + n], tmpA[t][:, 0:n], ps[bx][:, 0:n], ALU.mult), [("tA", t), ("ps", bx)], ["zext"])
                        P.op("dve", I("tensor_copy", zcar[:, c, :], zext[:, n:n + 2]), ["zext"], [("zcar", c)])
                        e0, e1, e2, yv = zext[:, 0:n], zext[:, 1:1 + n], zext[:, 2:2 + n], y[:, 0:n]
                        ek = "zext"
                    else:
                        P.op("dve", I("tensor_copy", seq_ext[:, :, 0:2], sst[:, c, :, 0:2]), [("sst", c)], ["seq_ext"])
                        P.op("dve", I("tensor_tensor", seq_ext[:, :, 2:2 + TS], s3(tmpA[t][:, 0:NS]), s3(ps[bx][:, 0:NS]), ALU.mult), [("tA", t), ("ps", bx)], ["seq_ext"])
                        P.op("dve", I("tensor_copy", sst[:, c, :, 0:2], seq_ext[:, :, TS:TS + 2]), ["seq_ext"], [("sst", c)])
                        e0, e1, e2, yv = seq_ext[:, :, 0:TS], seq_ext[:, :, 1:1 + TS], seq_ext[:, :, 2:2 + TS], s3(y[:, 0:NS])
                        ek = "seq_ext"
                    P.op("dve", I("tensor_scalar", yv, e0, vcol("scw0", c), None, ALU.mult), [ek, "vec"], [("tB", t)])
                    P.op("dve", I("scalar_tensor_tensor", yv, e1, vcol("scw1", c), yv, ALU.mult, ALU.add), [ek, ("tB", t)], [("tB", t)])
                    P.op("dve", I("scalar_tensor_tensor", yv, e2, vcol("scw2", c), yv, ALU.mult, ALU.add), [ek, ("tB", t)], [("tB", t)])
                    P.op("dve", I("tensor_tensor", mixT[:, c, o:o + n], y[:, 0:n], ps[bb][:, 0:n], ALU.mult), [("tB", t), ("ps", bb)], [("mix", c)])
            if last and has_p:
                P.dma("sp", o_scp[:, :, :], zcar[:], [("zcar", c) for c in range(KC)], [], "o_sc")
            if has_s:
                P.dma("sp", o_scs[:, :, :, :], sst[:, :, :, 0:2], SK, [], "o_scs")
            out_proj_add(swo_pm, tiles)

        mixF = mixT[:].rearrange("p c t -> p (c t)").bitcast(F32)
        P1R = [mixF[:, i * 512:(i + 1) * 512] for i in range(9)]
        p1n = [0]

        PRJ = [(tmpA[i], tmpB[i], tmpC[i], ("tA", i), ("tB", i), ("tC", i)) for i in range(NTMP)]
        PRJ += [(P1R[3 * i], P1R[3 * i + 1], P1R[3 * i + 2].bitcast(BF16)[:, 0:512], ("p1", 3 * i), ("p1", 3 * i + 1), ("p1", 3 * i + 2)) for i in range(3)]
        prn = [0]
        P1K = [("p1", i) for i in range(9)]

        def prj_ring():
            i = prn[0] % len(PRJ)
            prn[0] += 1
            return PRJ[i]

        def p1buf():
            i = p1n[0] % 9
            p1n[0] += 1
            return P1R[i], ("p1", i)

        def rglru(li, tiles, sample, last, phase1=False, scan=True):
            P.barrier()
            rmsnorm("nm%d" % li, tiles)
            SK = [("sst", c) for c in range(KC)]
            if sample:
                P.dma("sp", shs[:], sh[:, :, :], [], [("shs", c) for c in range(KC)], "shs_in")
                P.dma("sp", sst[:], src[:, :, :, :], [], SK, "sst")
            else:
                for c in range(KC):
                    P.op("dve", I("tensor_copy", uext[:, c, 0:3], ucar[:, c, :]), [("ucar", c)], [("uext", c)])
            for (o, n) in tiles:
                for half in range(2):
                    s = wload(rwi_pm[half], KC, 512)
                    for cc in range(4):
                        c = half * 4 + cc
                        b = bank()
                        proj_chunk(s, KC, cc * 128, hT, o, n, b, HK)
                        if not sample:
                            P.op("act", I("activation", uext[:, c, 3:3 + n], ps[b][:, 0:n], AF.Copy), [("ps", b)], [("uext", c)])
                            ex = [uext[:, c, j:j + n] for j in range(4)]
                            uo = uT[:, c, 0:n]
                            rk = [("uext", c), "vec"]
                        else:
                            P.op("dve", I("tensor_copy", seq_ext[:, :, 0:3], sst[:, c, :, :]), [("sst", c)], ["seq_ext"])
                            P.op("act", I("activation", seq_ext[:, :, 3:3 + TS], s3(ps[b][:, 0:NS]), AF.Copy), [("ps", b)], ["seq_ext"])
                            P.op("dve", I("tensor_copy", sst[:, c, :, :], seq_ext[:, :, TS:TS + 3]), ["seq_ext"], [("sst", c)])
                            ex = [seq_ext[:, :, j:j + TS] for j in range(4)]
                            uo = s3(uT[:, c, 0:NS])
                            rk = ["seq_ext", "vec"]
                        P.op("dve", I("tensor_scalar", uo, ex[0], vcol("rcw0", c), vcol("rcb", c), ALU.mult, ALU.add), rk, [("u", c)])
                        for j in range(1, 4):
                            P.op("dve", I("scalar_tensor_tensor", uo, ex[j], vcol("rcw%d" % j, c), uo, ALU.mult, ALU.add), rk + [("u", c)], [("u", c)])
                        P.op("dve", I("tensor_copy", ubT[:, c, 0:n], uT[:, c, 0:n]), [("u", c)], [("ub", c)])
                        if not sample:
                            P.op("dve", I("tensor_copy", uext[:, c, 0:3], uext[:, c, n:n + 3]), [("uext", c), ("u", c)], [("uext", c)])
                for nb in range(4):
                    ia = wload(rwax_pm[nb], 2, 512)
                    sgt = None if phase1 else wload(rwg_pm[nb], KC, 256)
                    if phase1:
                        ti = cur["g"] * 2 + o // 512
                        ur = [("ub", nb * 2), ("ub", nb * 2 + 1), ("ws", ia)]
                        st = []
                        for sub in range(2):
                            c = nb * 2 + sub
                            ba_ = bank(); bx_ = bank()
                            P.op("pe", MM([(ps[ba_][:, 0:n], WV(ia)[:, k, sub * 128:(sub + 1) * 128], ubT[:, nb * 2 + k, 0:n], k == 0, k == 1) for k in range(2)]), ur, [("ps", ba_)])
                            P.op("pe", MM([(ps[bx_][:, 0:n], WV(ia)[:, k, 256 + sub * 128:256 + (sub + 1) * 128], ubT[:, nb * 2 + k, 0:n], k == 0, k == 1) for k in range(2)]), ur, [("ps", bx_)])
                            (t1, k1), (t2, k2), (a_, ka), (m_, km) = p1buf(), p1buf(), p1buf(), p1buf()
                            P.op("act", I("activation", t1[:, 0:n], ps[ba_][:, 0:n], AF.Tanh, bias=cvec[:, 24 + c:25 + c], scale=0.5), [("ps", ba_), "cvec"], [k1])
                            P.op("act", I("activation", t2[:, 0:n], ps[bx_][:, 0:n], AF.Tanh, bias=cvec[:, 32 + c:33 + c], scale=0.5), [("ps", bx_), "cvec"], [k2])
                            P.op("act", I("activation", a_[:, 0:n], t1[:, 0:n], AF.Exp, bias=cvec[:, 16 + c:17 + c], scale=cvec[:, 16 + c:17 + c]), [k1, "cvec"], [ka])
                            P.op("act", I("activation", m_[:, 0:n], t1[:, 0:n], AF.Exp, bias=cvec[:, c:c + 1], scale=cvec[:, c:c + 1]), [k1, "cvec"], [km])
                            P.op("dve", I("scalar_tensor_tensor", t2[:, 0:n], t2[:, 0:n], 1.0, uT[:, c, 0:n], ALU.add, ALU.mult), [k2, ("u", c)], [k2])
                            st.append((c, t1, k1, t2, k2, a_, ka, m_, km))
                        for (c, t1, k1, t2, k2, a_, ka, m_, km) in st:
                            P.op("act", I("activation", m_[:, 0:n], m_[:, 0:n], AF.Sqrt, bias=onec[:, 0:1], scale=-1.0), [km, "onec"], [km])
                        for (c, t1, k1, t2, k2, a_, ka, m_, km) in st:
                            P.op("dve", I("scalar_tensor_tensor", t2[:, 0:n], t2[:, 0:n], 0.5, m_[:, 0:n], ALU.mult, ALU.mult), [k2, km], [k2])
                            P.dma("sp", abv(0, ti)[:, c, 0:n], a_[:, 0:n], [ka], [("abd", ti, c, 0)], ("st", ka), ms=True)
                            P.dma("sp", abv(1, ti)[:, c, 0:n], t2[:, 0:n], [k2], [("abd", ti, c, 1)], ("st", k2), ms=True)
                            if scan:
                                P.op("dve", I("tensor_tensor_scan", t1[:, 0:n], a_[:, 0:n], t2[:, 0:n], hcar[:, c:c + 1], ALU.mult, ALU.add), [ka, k2, ("hcar", c), k1], [k1])
                                P.op("dve", I("tensor_copy", hcar[:, c:c + 1], t1[:, n - 1:n]), [k1], [("hcar", c)])
                        continue
                    for sub in range(2):
                        c = nb * 2 + sub
                        ba_ = bank(); bx_ = bank(); bg_ = bank()
                        ur = [("ub", nb * 2), ("ub", nb * 2 + 1), ("ws", ia)]
                        P.op("pe", MM([(ps[ba_][:, 0:n], WV(ia)[:, k, sub * 128:(sub + 1) * 128], ubT[:, nb * 2 + k, 0:n], k == 0, k == 1) for k in range(2)]), ur, [("ps", ba_)])
                        P.op("pe", MM([(ps[bx_][:, 0:n], WV(ia)[:, k, 256 + sub * 128:256 + (sub + 1) * 128], ubT[:, nb * 2 + k, 0:n], k == 0, k == 1) for k in range(2)]), ur, [("ps", bx_)])
                        if not phase1:
                            proj_chunk(sgt, KC, sub * 128, hT, o, n, bg_, HK)
                        if phase1:
                            (r_, kr), (ig_, ki), (a_, ka), (m_, km) = p1buf(), p1buf(), p1buf(), p1buf()
                        else:
                            t = tmpi(); t2 = tmpi()
                            r_ = tmpA[t]; ig_ = tmpB[t]; a_ = tmpA[t2]; m_ = tmpB[t2]
                            kr, ki, ka, km = ("tA", t), ("tB", t), ("tA", t2), ("tB", t2)
                        P.op("act", I("activation", r_[:, 0:n], ps[ba_][:, 0:n], AF.Sigmoid, bias=vcol("rba", c), scale=1.0), [("ps", ba_), "vec"], [kr])
                        P.op("act", I("activation", ig_[:, 0:n], ps[bx_][:, 0:n], AF.Sigmoid, bias=vcol("rbx", c), scale=1.0), [("ps", bx_), "vec"], [ki])
                        P.op("act", I("activation", a_[:, 0:n], r_[:, 0:n], AF.Exp, scale=cvec[:, c:c + 1]), [kr, "cvec"], [ka])
                        P.op("act", I("activation", m_[:, 0:n], r_[:, 0:n], AF.Exp, scale=cvec[:, 8 + c:9 + c]), [kr, "cvec"], [km])
                        P.op("act", I("activation", m_[:, 0:n], m_[:, 0:n], AF.Sqrt, bias=onec[:, 0:1], scale=-1.0), [km, "onec"], [km])
                        P.op("dve", I("tensor_tensor", ig_[:, 0:n], ig_[:, 0:n], m_[:, 0:n], ALU.mult), [ki, km], [ki])
                        P.op("dve", I("tensor_tensor", ig_[:, 0:n], ig_[:, 0:n], uT[:, c, 0:n], ALU.mult), [ki, ("u", c)], [ki])
                        hs_ = r_
                        if phase1:
                            ti = cur["g"] * 2 + o // 512
                            P.dma("sp", abv(0, ti)[:, c, 0:n], a_[:, 0:n], [ka], [("abd", ti, c, 0)], ("st", ka), ms=True)
                            P.dma("sp", abv(1, ti)[:, c, 0:n], ig_[:, 0:n], [ki], [("abd", ti, c, 1)], ("st", ki), ms=True)
                        if not sample and scan:
                            P.op("dve", I("tensor_tensor_scan", hs_[:, 0:n], a_[:, 0:n], ig_[:, 0:n], hcar[:, c:c + 1], ALU.mult, ALU.add),
                                 [ka, ki, ("hcar", c), kr], [kr])
                            P.op("dve", I("tensor_copy", hcar[:, c:c + 1], hs_[:, n - 1:n]), [kr], [("hcar", c)])
                        elif sample:
                            a3 = s3(a_[:, 0:NS]); b3 = s3(ig_[:, 0:NS])
                            P.op("dve", I("tensor_tensor", seq_ext[:, :, 0:1], a3[:, :, 0:1], shs[:, c, :].unsqueeze(2), ALU.mult), [ka, ("shs", c)], ["seq_ext"])
                            P.op("dve", I("tensor_tensor", b3[:, :, 0:1], b3[:, :, 0:1], seq_ext[:, :, 0:1], ALU.add), ["seq_ext", ki], [ki])
                            P.op("dve", I("memset", a3[:, :, 0:1], 0.0), [ka], [ka])
                            P.op("dve", I("tensor_tensor_scan", hs_[:, 0:NS], a_[:, 0:NS], ig_[:, 0:NS], 0.0, ALU.mult, ALU.add),
                                 [ka, ki, kr], [kr])
                            P.op("dve", I("tensor_copy", shs[:, c, :].unsqueeze(2), s3(hs_[:, 0:NS])[:, :, TS - 1:TS]), [kr], [("shs", c)])
                        if not phase1:
                            P.op("act", I("activation", a_[:, 0:n], ps[bg_][:, 0:n], AF.Gelu), [("ps", bg_), ka], [ka])
                            P.op("dve", I("tensor_tensor", mixT[:, c, o:o + n], hs_[:, 0:n], a_[:, 0:n], ALU.mult), [kr, ka], [("mix", c)])
            if not sample:
                for c in range(KC):
                    P.op("dve", I("tensor_copy", ucar[:, c, :], uext[:, c, 0:3]), [("uext", c)], [("ucar", c)])
            if phase1:
                return
            if last:
                if not sample:
                    P.dma("sp", o_hp[:, :], hcar[:], [("hcar", c) for c in range(KC)], [], "o_h")
                    P.dma("sp", o_rcp[:, :, :], ucar[:], [("ucar", c) for c in range(KC)], [], "o_rcp")
                else:
                    P.dma("sp", o_hs[:, :, :], shs[:], [("shs", c) for c in range(KC)], [], "o_hs")
                    P.dma("sp", o_rcs[:, :, :, :], sst[:], SK, [], "o_rcs")
            out_proj_add(rwo_pm, tiles)

        ab_scr = nc.dram_tensor("ab_scr", [2, 2 * NG, 128, KC * 512], F32)

        def abv(which, ti):
            return ab_scr[which, ti].rearrange("p (c t) -> p c t", c=KC)
        NPB = 8
        pa = [msv(i * 512, 512) for i in range(NPB)]
        pb = [msv((NPB + i) * 512, 512) for i in range(NPB)]
        pq = [0]

        def rglru_p2(li, tiles, last):
            P.barrier()
            rmsnorm("nm%d" % li, tiles)
            for (o, n) in tiles:
                ti = cur["g"] * 2 + o // 512
                for nb in range(4):
                    sgt = wload(rwg_pm[nb], KC, 256)
                    for sub in range(2):
                        c = nb * 2 + sub
                        bg_ = bank()
                        proj_chunk(sgt, KC, sub * 128, hT, o, n, bg_, HK)
                        i = pq[0] % NPB
                        pq[0] += 1
                        P.dma("sp", pa[i][:, 0:n], abv(0, ti)[:, c, 0:n], [("abd", ti, c, 0)], [("pa", i)], ("ld_a", i), ms=True)
                        P.dma("sp", pb[i][:, 0:n], abv(1, ti)[:, c, 0:n], [("abd", ti, c, 1)], [("pb", i)], ("ld_b", i), ms=True)
                        t = tmpi()
                        hs_ = tmpA[t]; g_ = tmpB[t]
                        P.op("act", I("activation", g_[:, 0:n], ps[bg_][:, 0:n], AF.Gelu), [("ps", bg_)], [("tB", t)])
                        P.op("dve", I("tensor_tensor_scan", hs_[:, 0:n], pa[i][:, 0:n], pb[i][:, 0:n], hcar[:, c:c + 1], ALU.mult, ALU.add),
                             [("pa", i), ("pb", i), ("hcar", c)], [("tA", t)])
                        P.op("dve", I("tensor_copy", hcar[:, c:c + 1], hs_[:, n - 1:n]), [("tA", t)], [("hcar", c)])
                        P.op("dve", I("tensor_tensor", mixT[:, c, o:o + n], hs_[:, 0:n], g_[:, 0:n], ALU.mult), [("tA", t), ("tB", t)], [("mix", c)])
            if last:
                P.dma("sp", o_hp[:, :], hcar[:], [("hcar", c) for c in range(KC)], [], "o_h")
            out_proj_add(rwo_pm, tiles)

        def attn(li, j, tiles, sample, first, last, tok0):
            P.barrier()
            cosT, sinT, kf, kpad, vpad = AV_S if sample else AV_P
            rmsnorm("nm%d" % li, tiles)
            ntok = sum(n for _, n in tiles)
            if not sample:
                P.dma("sp", cosT[:, 0:ntok], cosP[:, tok0:tok0 + ntok], [], ["cos"], "cs_cos", ms=True)
                P.dma("sp", sinT[:, 0:ntok], sinP[:, tok0:tok0 + ntok], [], ["sin"], "cs_sin", ms=True)
            else:
                P.dma("sp", cosT[:, 0:ntok], cosS[:, :], [], ["cos"], "cs_cos", ms=True)
                P.dma("sp", sinT[:, 0:ntok], sinS[:, :], [], ["sin"], "cs_sin", ms=True)
            P.dma("sp", bvt[:], bvb[j], [], ["bvt"], "cs_bvt")
            for g in range(2):
                P.op("dve", I("memset", kpad[g][:], 0.0), [], ["kbuf"])
                P.op("dve", I("memset", vpad[g][:].rearrange("p a b -> p (a b)"), 0.0), [], [("vpad", g)])
                if sample:
                    P.op("dve", I("memset", kcp[g][:].rearrange("p a b -> p (a b)"), 0.0), [], ["kcT"])
                    P.op("dve", I("memset", vcp[g][:].rearrange("p a b -> p (a b)"), 0.0), [], [("vcp", g)])
            if not sample:
                for g in range(2):
                    P.op("dve", I("tensor_copy", kpad[g][:, 0:128], kcar[j][g][:]), [("kcar", j, g)], ["kbuf"])
                    P.op("dve", I("tensor_copy", vpad[g][:, 0, :], vcar[j][g][:]), [("vcar", j, g)], [("vpad", g)])
            for (c0, ncol) in [(0, 512), (512, 512), (1024, 128)]:
                s = wload(wq_pm[j, c0 // 512] if c0 < 1024 else wk_pm[j], KC, ncol)
                for (o, n) in tiles:
                    for cc in range(ncol // 128):
                        c = c0 // 128 + cc
                        b = bank()
                        proj_chunk(s, KC, cc * 128, hT, o, n, b, HK)
                        A_, B_, C_, kA, kB, kC = prj_ring()
                        bias = vcol("bq%d" % j, c) if c < 8 else vcol("bk%d" % j, 0)
                        P.op("act", I("activation", A_[:, 0:n], ps[b][:, 0:n], AF.Identity, bias=bias, scale=1.0), [("ps", b), "vec"], [kA])
                        P.op("act", I("activation", C_[:, 0:n], A_[:, 0:n], AF.Copy), [kA], [kC])
                        b2 = bank()
                        P.op("pe", I("matmul", ps[b2][:, 0:n], rot[:], C_[:, 0:n], start=True, stop=True), [kC, "rot"], [("ps", b2)])
                        P.op("dve", I("tensor_tensor", A_[:, 0:n], A_[:, 0:n], cosT[:, o:o + n], ALU.mult), [kA, "cos"], [kA])
                        P.op("dve", I("tensor_tensor", B_[:, 0:n], ps[b2][:, 0:n], sinT[:, o:o + n], ALU.mult), [("ps", b2), "sin"], [kB])
                        if c < 8:
                            P.op("dve", I("tensor_tensor", qT[:, c, o:o + n], A_[:, 0:n], B_[:, 0:n], ALU.add), [kA, kB], [("q", c)])
                        else:
                            P.op("dve", I("tensor_tensor", kf[:, o:o + n], A_[:, 0:n], B_[:, 0:n], ALU.add), [kA, kB], ["kf"])
                            for g in range(2):
                                P.op("act", I("activation", kpad[g][g * 64:(g + 1) * 64, 128 + o:128 + o + n], kf[g * 64:(g + 1) * 64, o:o + n], AF.Copy), ["kf"], ["kbuf"])
            sv = wload(wvv_pm[j], KC, 128)
            nblk = ntok // 128
            for bi in range(nblk):
                b = bank()
                P.op("pe", MM([(ps[b][:, 0:128], hT[:, k, bi * 128:(bi + 1) * 128], WV(sv)[:, k, 0:128], k == 0, k == KC - 1) for k in range(KC)]),
                     [("ws", sv)] + HK, [("ps", b)])
                P.op("dve", I("tensor_tensor", vf[:], ps[b][:, 0:128], bvt[:], ALU.add), [("ps", b), "bvt"], ["vf"])
                for g in range(2):
                    P.op("act", I("activation", vpad[g][:, 1 + bi, g * 64:(g + 1) * 64], vf[:, g * 64:(g + 1) * 64], AF.Copy), ["vf"], [("vpad", g)])
                if last and bi == nblk - 1:
                    P.dma("sp", (o_vs if sample else o_vp)[j], vf[:], ["vf"], [], ("o_v", sample, j))
            if last:
                P.dma("sp", (o_ks if sample else o_kp)[j], kf[:, ntok - 128:ntok], ["kf"], [], ("o_k", sample, j), ms=True)
            if sample:
                for g in range(2):
                    P.dma("pool", kcp[g][g * 64:(g + 1) * 64, :, :], ckT[j][g * 64:(g + 1) * 64, :, :], [], ["kcT"], "kc", ms=True)
                for g in range(2):
                    P.dma("pool", vcp[g][:, :, g * 64:(g + 1) * 64], cvN[j].rearrange("b k f -> k b f")[:, :, g * 64:(g + 1) * 64], [], [("vcp", g)], ("vc", g), ms=True)
                P.dma("sp", o_ksh[j], ckN[j][:, 8:128, :], [], [], "o_shk")
                P.dma("sp", o_vsh[j], cvN[j][:, 8:128, :], [], [], "o_sh")
            sinkn = "sink%d" % j
            if not sample:
                def v4(ap):
                    return ap.rearrange("p (a b) -> p a b", a=4)
                for bi in range(nblk):
                    msk, mr = (maskF, "maskF") if (first and bi == 0) else (maskP, "maskP")
                    for hh in range(2):
                        cs = [4 * hh + i for i in range(4)]
                        qv = qT[:, 4 * hh:4 * hh + 4, bi * 128:(bi + 1) * 128]
                        pts = []
                        for kb in range(2):
                            for g in range(2):
                                i4 = kb * 2 + g
                                bs = bank()
                                mb = msk[:, i4 * 128:(i4 + 1) * 128].unsqueeze(1).broadcast_to([128, 4, 128])
                                P.op("pe", MM([(v4(ps[bs][:, :]), kpad[g][:, (bi + kb) * 128:(bi + kb + 1) * 128], qv, True, False),
                                               (v4(ps[bs][:, :]), ident[:], mb, False, True)]),
                                     [("q", c) for c in cs] + ["kbuf", mr, "ident"], [("ps", bs)])
                                pi = pn[0] % 4
                                pn[0] += 1
                                P.op("act", I("activation", pTb[pi][:, :], ps[bs][:, :], AF.Exp, scale=0.125), [("ps", bs)], [("pT", pi)])
                                pts.append((pi, kb, g))
                        bo = bank(); bd = bank()
                        lo = []; ld = []
                        for n_, (pi, kb, g) in enumerate(pts):
                            lo.append((ps[bo][:, :], vpad[g][:, bi + kb, :], pTb[pi][:, :], n_ == 0, n_ == 3))
                            ld.append((ps[bd][:, :], onesh[g][:], pTb[pi][:, :], n_ == 0, n_ == 3))
                        rk = [("pT", pi) for (pi, _, _) in pts]
                        P.op("pe", MM(lo), rk + [("vpad", 0), ("vpad", 1)], [("ps", bo)])
                        P.op("pe", MM(ld), rk + [("onesh", 0), ("onesh", 1)], [("ps", bd)])
                        t2 = tmpi()
                        dn = tmpA[t2]
                        sk = vec[:, VEC[sinkn] + 4 * hh:VEC[sinkn] + 4 * hh + 4].unsqueeze(2).broadcast_to([128, 4, 128])
                        P.op("dve", I("tensor_tensor", v4(dn[:, :]), v4(ps[bd][:, :]), sk, ALU.add), [("ps", bd), "vec"], [("tA", t2)])
                        P.op("dve", I("reciprocal", dn[:, :], dn[:, :]), [("tA", t2)], [("tA", t2)])
                        P.op("dve", I("tensor_tensor", mixT[:, 4 * hh:4 * hh + 4, bi * 128:(bi + 1) * 128], v4(ps[bo][:, :]), v4(dn[:, :]), ALU.mult),
                             [("tA", t2), ("ps", bo)], [("mix", c) for c in cs] + P1K)
            for bi in range(nblk if sample else 0):
                for c in range(KC):
                    bs = bank()
                    qr = [("q", c), "kbuf"]
                    lst = []
                    for g in range(2):
                        lst.append((ps[bs][:, g * 128:(g + 1) * 128], kpad[g][:, 128:256], qT[:, c, 0:128], True, True))
                    for g in range(2):
                        for sq in range(NSEQ):
                            c0_ = 256 + g * 128 + sq * TS
                            lst.append((ps[bs][:, c0_:c0_ + TS], kcp[g][:, sq, :], qT[:, c, sq * TS:(sq + 1) * TS], True, True))
                    P.op("pe", MM(lst), qr + ["kcT"], [("ps", bs)])
                    msk, mr = maskS, "maskS"
                    t = tmpi()
                    pT = tmpC[t]
                    P.op("act", I("activation", pT[:, :], ps[bs][:, :], AF.Exp, scale=0.125), [("ps", bs)], [("tC", t)])
                    P.op("dve", I("tensor_tensor", pT[:, :], pT[:, :], msk[:, :], ALU.mult), [("tC", t), mr], [("tC", t)])
                    bo = bank(); bd = bank()
                    lo = []; ld = []
                    for g in range(2):
                        lo.append((ps[bo][:, 0:128], vpad[g][:, 1, :], pT[:, g * 128:(g + 1) * 128], g == 0, g == 1))
                        ld.append((ps[bd][:, 0:128], onesh[g][:], pT[:, g * 128:(g + 1) * 128], g == 0, g == 1))
                    for sq in range(NSEQ):
                        for g in range(2):
                            c0_ = 256 + g * 128 + sq * TS
                            lo.append((ps[bo][:, 128 + sq * TS:128 + (sq + 1) * TS], vcp[g][:, sq, :], pT[:, c0_:c0_ + TS], g == 0, g == 1))
                            ld.append((ps[bd][:, 128 + sq * TS:128 + (sq + 1) * TS], onesh[g][:], pT[:, c0_:c0_ + TS], g == 0, g == 1))
                    P.op("pe", MM(lo), [("tC", t), ("vpad", 0), ("vpad", 1), ("vcp", 0), ("vcp", 1)], [("ps", bo)])
                    P.op("pe", MM(ld), [("tC", t), ("onesh", 0), ("onesh", 1)], [("ps", bd)])
                    t2 = tmpi()
                    dn = tmpA[t2]; on = tmpB[t2]
                    P.op("dve", I("tensor_scalar", dn[:, 0:128], ps[bd][:, 0:128], vcol(sinkn, c), None, ALU.add), [("ps", bd), "vec"], [("tA", t2)])
                    P.op("dve", I("tensor_tensor", dn[:, 0:128], dn[:, 0:128], ps[bd][:, 128:256], ALU.add), [("ps", bd), ("tA", t2)], [("tA", t2)])
                    P.op("act", I("activation", on[:, 0:128], ps[bo][:, 0:128], AF.Copy), [("ps", bo)], [("tB", t2)])
                    P.op("dve", I("tensor_tensor", on[:, 0:128], on[:, 0:128], ps[bo][:, 128:256], ALU.add), [("ps", bo), ("tB", t2)], [("tB", t2)])
                    P.op("dve", I("reciprocal", dn[:, 0:128], dn[:, 0:128]), [("tA", t2)], [("tA", t2)])
                    P.op("dve", I("tensor_tensor", mixT[:, c, 0:128], on[:, 0:128], dn[:, 0:128], ALU.mult), [("tA", t2), ("tB", t2)], [("mix", c)] + P1K)
            if not sample:
                for g in range(2):
                    P.op("dve", I("tensor_copy", kcar[j][g][:], kpad[g][:, 8 * 128:9 * 128]), ["kbuf"], [("kcar", j, g)])
                    P.op("dve", I("tensor_copy", vcar[j][g][:], vpad[g][:, 8, :]), [("vpad", g)], [("vcar", j, g)])
            out_proj_add(wo_pm[j], tiles, "bo%d" % j)

        PAIRS = [[0, 1], [2, 3], [4, 5], [6, 7]]
        xch = {}

        def exchange(idx, F):
            cin = nc.dram_tensor("cc_in%d" % idx, [128, F], F32)
            cout = nc.dram_tensor("cc_out%d" % idx, [128, F], F32)
            P.op("dve", I("tensor_scalar", xsend[:, 0:F], xsend[:, 0:F], flg[:, 0:1], None, ALU.mult), ["xsend", "flg"], ["xsend"])
            P.dma("sp", cin[:, :], xsend[:, 0:F], ["xsend"], [("cin", idx)], ("cin", idx))
            P.cc(lambda e: e.collective_compute("AllReduce", ALU.add, replica_groups=PAIRS, ins=[cin.ap().opt()], outs=[cout.ap().opt()]),
                 [("cin", idx)], [("cout", idx)], ("cc", idx))
            xch[idx] = (cout, F)

        def receive(idx):
            cout, F = xch[idx]
            P.dma("sp", xrecv[:, 0:F], cout[:, :], [("cout", idx)], ["xrecv"], ("rcv", idx))

        def kv_block(li, j, o):
            P.barrier()
            rmsnorm("nm%d" % li, [(o, 128)])
            n = 128
            s_ = wload(wk_pm[j], KC, 128)
            b = bank()
            proj_chunk(s_, KC, 0, hT, o, n, b, HK)
            t = tmpi()
            P.op("act", I("activation", tmpA[t][:, 0:n], ps[b][:, 0:n], AF.Identity, bias=vcol("bk%d" % j, 0), scale=1.0), [("ps", b), "vec"], [("tA", t)])
            P.op("act", I("activation", tmpC[t][:, 0:n], tmpA[t][:, 0:n], AF.Copy), [("tA", t)], [("tC", t)])
            b2 = bank()
            P.op("pe", I("matmul", ps[b2][:, 0:n], rot[:], tmpC[t][:, 0:n], start=True, stop=True), [("tC", t), "rot"], [("ps", b2)])
            P.op("dve", I("tensor_tensor", tmpA[t][:, 0:n], tmpA[t][:, 0:n], cosT[:, o:o + n], ALU.mult), [("tA", t), "cos"], [("tA", t)])
            P.op("dve", I("tensor_tensor", tmpB[t][:, 0:n], ps[b2][:, 0:n], sinT[:, o:o + n], ALU.mult), [("ps", b2), "sin"], [("tB", t)])
            P.op("dve", I("tensor_tensor", xsend[:, 0:128], tmpA[t][:, 0:n], tmpB[t][:, 0:n], ALU.add), [("tA", t), ("tB", t)], ["xsend"])
            sv = wload(wvv_pm[j], KC, 128)
            b = bank()
            P.op("pe", MM([(ps[b][:, 0:128], hT[:, k, o:o + 128], WV(sv)[:, k, 0:128], k == 0, k == KC - 1) for k in range(KC)]), [("ws", sv)] + HK, [("ps", b)])
            P.op("dve", I("tensor_tensor", xsend[:, 128:256], ps[b][:, 0:128], bvt[:], ALU.add), [("ps", b), "bvt"], ["xsend"])

        def set_carry(j, src, skey):
            for g in range(2):
                P.op("act", I("activation", kcar[j][g][g * 64:(g + 1) * 64, :], src[g * 64:(g + 1) * 64, 0:128], AF.Copy), [skey], [("kcar", j, g)])
                P.op("act", I("activation", vcar[j][g][:, g * 64:(g + 1) * 64], src[:, 128 + g * 64:128 + (g + 1) * 64], AF.Copy), [skey], [("vcar", j, g)])

        def pre_sconv(li, o, dest, dkeys):
            P.barrier()
            rmsnorm("nm%d" % li, [(o, 128)])
            n = 128
            for c in range(KC):
                i = wload(swi_pm[c], KC, 384)
                bc = bank(); bx = bank()
                proj_chunk(i, KC, 128, hT, o, n, bc, HK)
                proj_chunk(i, KC, 256, hT, o, n, bx, HK)
                t = tmpi()
                P.op("act", I("activation", tmpA[t][:, 0:n], ps[bc][:, 0:n], AF.Copy), [("ps", bc)], [("tA", t)])
                P.op("dve", I("tensor_tensor", dest[:, 2 * c:2 * c + 2], tmpA[t][:, n - 2:n], ps[bx][:, n - 2:n], ALU.mult), [("tA", t), ("ps", bx)], dkeys(c))

        PT = [(0, 512), (512, 512)]
        ST = [(0, NS)]
        PST = PT + [(GT, NS)]
        def setg(g):
            cur["g"] = g
            cur["xb"] = {0: 0, "s": GT, "s1": GT, 1: GT + NS, "h": 0}[g]

        def load_x(src, tok0, tiles):
            o0 = tiles[0][0]
            n_all = sum(n for _, n in tiles)
            if cur["g"] == "h":
                dst = xhv[:, :, o0:o0 + n_all]
            else:
                dst = xT[:, :, cur["xb"] + o0:cur["xb"] + o0 + n_all]
            P.dma("sp", dst, src[:, tok0 + o0:tok0 + o0 + n_all].rearrange("(c p) t -> p c t", p=128), [], [XKc(c) for c in range(KC)], ("xin", cur["g"]), ms=(cur["g"] == "h"))

        setg("h"); load_x(xhT, 0, [(0, 128)])
        setg("s"); load_x(xsT, 0, ST)
        for g in range(NG):
            setg(g); load_x(xpT, g * GT, PT)
        CK = lambda nm: [(nm, c) for c in range(KC)]
        P.op("dve", I("memset", zcar[:].rearrange("p a b -> p (a b)"), 0.0), [], CK("zcar"))
        P.op("dve", I("memset", hcar[:], 0.0), [], CK("hcar"))
        P.op("dve", I("memset", ucar[:].rearrange("p a b -> p (a b)"), 0.0), [], CK("ucar"))
        LG = NG - 1
        setg("h")
        P.barrier()
        P.dma("sp", cosT[:, 0:128], cosH[:, :], [], ["cos"], "cs_cos", ms=True)
        P.dma("sp", sinT[:, 0:128], sinH[:, :], [], ["sin"], "cs_sin", ms=True)
        P.dma("sp", bvt[:], bvb[0], [], ["bvt"], "cs_bvt")
        kv_block(0, 0, 0)
        set_carry(0, xsend, "xsend")
        setg("s"); attn(0, 0, ST, True, False, True, 0)
        setg(0); attn(0, 0, PT, False, True, False, 0)
        ffn(0, PST)
        setg(1); attn(0, 0, PT, False, False, True, GT); ffn(0, PT)
        for g in range(NG):
            setg(g); rglru(1, PT, False, False, phase1=True)
        P.dma("sp", o_rcp[:, :, :], ucar[:], CK("ucar"), [], "o_rcp")
        P.op("dve", I("tensor_copy", xsend[:, 0:8], hcar[:]), CK("hcar"), ["xsend"])
        P.op("dve", I("tensor_copy", xsend[:, 8:32], ucar[:].rearrange("p a b -> p (a b)")), CK("ucar"), ["xsend"])
        exchange(0, 32)
        setg("s"); rglru(1, ST, True, True)
        receive(0)
        P.op("dve", I("tensor_scalar", hcar[:], xrecv[:, 0:8], flg[:, 1:2], None, ALU.mult), ["xrecv", "flg"], CK("hcar"))
        P.op("dve", I("tensor_scalar", ucar[:].rearrange("p a b -> p (a b)"), xrecv[:, 8:32], flg[:, 1:2], None, ALU.mult), ["xrecv", "flg"], CK("ucar"))
        setg(0); rglru(1, [(0, 128)], False, False, phase1=True, scan=False)
        setg(0); rglru_p2(1, PT, False); ffn(1, PST)
        setg(1); rglru_p2(1, PT, True); ffn(1, PT)
        S1T = [(0, NS), (NS, 512), (NS + 512, 512)]
        setg(1); pre_sconv(2, GT - 128, xsend, lambda c: ["xsend"])
        exchange(1, 16)
        setg(0); pre_sconv(2, GT - 128, zcar[:].rearrange("p a b -> p (a b)"), lambda c: [("zcar", c)])
        setg("s1"); sconv(2, S1T, False, True, stiles=(0,)); ffn(2, S1T)
        setg(LG)
        P.barrier()
        P.dma("sp", cosT[:, GT - 128:GT], cosP[:, TP - 128:TP], [], ["cos"], "cs_cos", ms=True)
        P.dma("sp", sinT[:, GT - 128:GT], sinP[:, TP - 128:TP], [], ["sin"], "cs_sin", ms=True)
        P.dma("sp", bvt[:], bvb[1], [], ["bvt"], "cs_bvt")
        kv_block(3, 1, GT - 128)
        exchange(2, 256)
        receive(1)
        P.op("dve", I("tensor_scalar", zcar[:].rearrange("p a b -> p (a b)"), xrecv[:, 0:16], flg[:, 1:2], None, ALU.mult), ["xrecv", "flg"], CK("zcar"))
        setg(0); sconv(2, PT, False, False); ffn(2, PT)
        setg("s"); attn(3, 1, ST, True, False, True, 0)
        receive(2)
        set_carry(1, xrecv, "xrecv")
        setg(0); attn(3, 1, PT, False, True, False, 0); ffn(3, PST)
        rmsnorm_final(PT, ypT, 0)
        setg("s"); rmsnorm_final(ST, ysT, 0)
        setg(1); attn(3, 1, PT, False, False, True, GT); ffn(3, PT); rmsnorm_final(PT, ypT, GT)
        P.final_wait("sp")
        block = es.enter_context(nc.Block())
        P.emit(block)
    return nc


def _fm(v):
    return np.ascontiguousarray(np.asarray(v, np.float32).reshape(8, 128).T)


def _host_consts(TP, start, half):
    hd = 32
    inv = (np.float32(10000.0) ** (-(np.arange(hd, dtype=np.float32)) / np.float32(hd))).astype(np.float32)

    def tables(pos):
        ang = (pos.astype(np.float32)[:, None] * inv[None, :]).astype(np.float32)
        cos = np.cos(ang.astype(np.float64)).astype(np.float32).T
        sin = np.sin(ang.astype(np.float64)).astype(np.float32).T
        return np.ascontiguousarray(np.tile(cos, (4, 1))), np.ascontiguousarray(np.tile(sin, (4, 1)))
    cosP, sinP = tables(start + np.arange(TP))
    cosH, sinH = tables(np.maximum(start - 128 + np.arange(128), 0))
    cosS, sinS = tables(PAST + (np.arange(NS) % TS))
    rot = np.zeros((128, 128), np.float32)
    for blk in range(2):
        for d in range(64):
            m = blk * 64 + d
            if d < 32:
                rot[blk * 64 + d + 32, m] = -1.0
            else:
                rot[blk * 64 + d - 32, m] = 1.0
    k = np.arange(128)[:, None]
    q = np.arange(128)[None, :]
    prev = (k > q).astype(np.float32)
    cur = (k <= q).astype(np.float32)
    zero = np.zeros_like(prev)
    NEG = np.float32(-30000.0)
    maskP = (np.concatenate([prev, prev, cur, cur], axis=1) - 1.0) * (-NEG)
    maskF = maskP if half else (np.concatenate([zero, zero, cur, cur], axis=1) - 1.0) * (-NEG)
    mnew = ((k // TS == q // TS) & (k % TS <= q % TS)).astype(np.float32)
    mc = (k > (q % TS)).astype(np.float32)
    maskS = np.concatenate([mnew, mnew, mc, mc], axis=1)
    cst = np.ascontiguousarray(np.concatenate([rot, maskP, maskF, maskS, np.eye(128, dtype=np.float32)], axis=1).astype(np.float32))
    flags = np.zeros((128, 2), np.float32)
    flags[:, 0] = 1.0 - half
    flags[:, 1] = float(half)
    return dict(cosP=cosP, sinP=sinP, cosH=cosH, sinH=sinH, cosS=cosS, sinS=sinS, cst=cst, flags=flags)


def _prep_shared(inp):
    f = lambda a: np.ascontiguousarray(np.asarray(a, np.float32))
    vec = np.zeros((128, NV), np.float32)

    def put(name, v):
        vec[:, VEC[name]:VEC[name] + 8] = _fm(v)
    for i in range(4):
        put("nm%d" % i, inp["norm_mixer"][i]); put("nf%d" % i, inp["norm_ffn"][i])
    put("nfin", inp["norm_final"])
    perm = np.concatenate([np.arange((g * 8 + c) * 64, (g * 8 + c) * 64 + 64) for c in range(8) for g in range(2)])
    wq = np.asarray(inp["attn_w_qkv"], np.float32)
    bq = np.asarray(inp["attn_b_qkv"], np.float32)
    wqkv = f(np.concatenate([wq[:, :, perm], wq[:, :, 1024:]], axis=2))
    wo = f(np.asarray(inp["attn_w_o"], np.float32)[:, perm, :])
    bvb = np.zeros((2, 128, 128), np.float32)
    for j in range(2):
        put("bq%d" % j, bq[j][perm])
        vec[:, VEC["bk%d" % j]] = bq[j][1024:1152]
        put("bo%d" % j, inp["attn_b_o"][j])
        sk = np.asarray(inp["attn_sinks"], np.float32)[j]
        for c in range(8):
            vec[0:64, VEC["sink%d" % j] + c] = sk[c]
            vec[64:128, VEC["sink%d" % j] + c] = sk[c + 8]
        bvb[j] = np.broadcast_to(bq[j][1152:1280][None, :], (128, 128))
    for jj in range(4):
        put("rcw%d" % jj, inp["rglru_conv_w"][0][jj])
    put("rcb", inp["rglru_conv_b"][0]); put("rba", inp["rglru_ba"][0]); put("rbx", inp["rglru_bx"][0]); put("rlam", inp["rglru_lambda"][0])
    for jj in range(3):
        put("scw%d" % jj, inp["sconv_conv_w"][0][jj])
    def pm(W, kc, ncol):
        W = np.asarray(W, np.float32)
        npc = W.shape[1] // ncol
        return f(W.reshape(kc, 128, npc, ncol).transpose(2, 1, 0, 3).reshape(npc, 128, kc * ncol))
    swi_perm = np.asarray(inp["sconv_w_in"][0], np.float32).reshape(D, 3, 8, 128).transpose(0, 2, 1, 3).reshape(D, 3 * D)
    rwax = np.concatenate([np.asarray(inp["rglru_wa"][0], np.float32), np.asarray(inp["rglru_wx"][0], np.float32)], axis=2)
    fg = np.asarray(inp["ffn_w_gate"], np.float32); fu = np.asarray(inp["ffn_w_up"], np.float32); fd = np.asarray(inp["ffn_w_down"], np.float32)
    sh = dict(vec=vec, bvb=bvb,
              wq_pm=f(np.stack([pm(wqkv[j][:, 0:1024], 8, 512) for j in range(2)])),
              wk_pm=f(np.stack([pm(wqkv[j][:, 1024:1152], 8, 128)[0] for j in range(2)])),
              wvv_pm=f(np.stack([pm(wqkv[j][:, 1152:1280], 8, 128)[0] for j in range(2)])),
              wo_pm=f(np.stack([pm(wo[j], 8, 512) for j in range(2)])),
              rwi_pm=pm(inp["rglru_w_in"][0], 8, 512), rwg_pm=pm(inp["rglru_w_gate"][0], 8, 256),
              rwax_pm=f(np.stack([pm(rwax[nb], 2, 512)[0] for nb in range(4)])),
              rwo_pm=pm(inp["rglru_w_out"][0], 8, 512),
              swi_pm=pm(swi_perm, 8, 384), swo_pm=pm(inp["sconv_w_out"][0], 8, 512),
              fwg_a=f(np.stack([pm(fg[l][:, 0:2560], 8, 512) for l in range(4)])), fwg_b=f(np.stack([pm(fg[l][:, 2560:], 8, 256)[0] for l in range(4)])),
              fwu_a=f(np.stack([pm(fu[l][:, 0:2560], 8, 512) for l in range(4)])), fwu_b=f(np.stack([pm(fu[l][:, 2560:], 8, 256)[0] for l in range(4)])),
              fwd_a=f(np.stack([np.stack([pm(fd[l][g * 512:(g + 1) * 512, :], 4, 512) for g in range(5)]) for l in range(4)])),
              fwd_b=f(np.stack([pm(fd[l][2560:, :], 2, 512) for l in range(4)])))
    return sh


def run(inp, SEQ=4096, n_cores=8):
    TP = SEQ // 2
    nc = build(TP)
    shared = _prep_shared(inp)
    f = lambda a: np.ascontiguousarray(np.asarray(a, np.float32))
    xp = np.asarray(inp["x_prompt"], np.float32); xs = np.asarray(inp["x_sample"], np.float32)
    ck = np.asarray(inp["cache_k"], np.float32).reshape(2, 128, 128, 128)
    cv = np.asarray(inp["cache_v"], np.float32).reshape(2, 128, 128, 128)
    srh = np.asarray(inp["state_rglru_h"], np.float32)[0]
    src = np.asarray(inp["state_rglru_conv"], np.float32)[0]
    ssc = np.asarray(inp["state_shortconv"], np.float32)[0]
    in_maps = []
    for c in range(n_cores):
        seq, half = c // 2, c % 2
        start = half * TP
        b0 = c * NSEQ
        m = dict(shared)
        m.update(_host_consts(TP, start, half))
        m["xpT"] = f(xp[seq, start:start + TP].T)
        m["xhT"] = f(xp[seq, start - 128:start].T) if half else np.zeros((D, 128), np.float32)
        m["xsT"] = f(xs[b0:b0 + NSEQ].reshape(NS, D).T)
        m["ckN"] = f(ck[:, b0:b0 + NSEQ]); m["cvN"] = f(cv[:, b0:b0 + NSEQ])
        m["ckT"] = f(ck[:, b0:b0 + NSEQ].transpose(0, 3, 1, 2))
        m["sh"] = f(srh[b0:b0 + NSEQ].reshape(NSEQ, 8, 128).transpose(2, 1, 0))
        m["src"] = f(src[b0:b0 + NSEQ].reshape(NSEQ, 3, 8, 128).transpose(3, 2, 0, 1))
        m["ssc"] = f(ssc[b0:b0 + NSEQ].reshape(NSEQ, 2, 8, 128).transpose(3, 2, 0, 1))
        in_maps.append(m)
    res = run_bass_kernel_spmd(nc, in_maps, core_ids=list(range(n_cores)))
    R = res.results
    NB = n_cores * NSEQ
    nP = n_cores // 2
    y_p = np.stack([np.concatenate([R[2 * s]["ypT"].T, R[2 * s + 1]["ypT"].T], axis=0) for s in range(nP)])
    y_s = np.concatenate([R[c]["ysT"].T.reshape(NSEQ, TS, D) for c in range(n_cores)])
    L = lambda s: R[2 * s + 1]
    kp = np.stack([np.stack([L(s)["o_kp"][j].T.reshape(128, 2, 64) for s in range(nP)]) for j in range(2)])
    vp = np.stack([np.stack([L(s)["o_vp"][j].reshape(128, 2, 64) for s in range(nP)]) for j in range(2)])

    def samp(sh_name, new_name, transpose):
        out = np.zeros((2, NB, 128, 128), np.float32)
        for j in range(2):
            for c in range(n_cores):
                out[j, c * NSEQ:(c + 1) * NSEQ, 0:120] = R[c][sh_name][j]
                nw = R[c][new_name][j]
                nw = nw.T if transpose else nw
                out[j, c * NSEQ:(c + 1) * NSEQ, 120:128] = nw.reshape(NSEQ, TS, 128)
        return out.reshape(2, NB, 128, 2, 64)
    ks = samp("o_ksh", "o_ks", True)
    vs = samp("o_vsh", "o_vs", False)
    hp = np.stack([L(s)["o_hp"].T.reshape(D) for s in range(nP)])[None]
    hs = np.concatenate([R[c]["o_hs"].transpose(2, 1, 0).reshape(NSEQ, D) for c in range(n_cores)])[None]
    rcp = np.stack([L(s)["o_rcp"].transpose(2, 1, 0).reshape(3, D) for s in range(nP)])[None]
    rcs = np.concatenate([R[c]["o_rcs"].transpose(2, 3, 1, 0).reshape(NSEQ, 3, D) for c in range(n_cores)])[None]
    scp = np.stack([L(s)["o_scp"].transpose(2, 1, 0).reshape(2, D) for s in range(nP)])[None]
    scs = np.concatenate([R[c]["o_scs"].transpose(2, 3, 1, 0).reshape(NSEQ, 2, D) for c in range(n_cores)])[None]
    outs = (y_p, y_s, kp, vp, ks, vs, hp, hs, rcp, rcs, scp, scs)
    return tuple(np.ascontiguousarray(o.astype(np.float32)) for o in outs)


def kernel(**inputs):
    return run(inputs, 4096, 8)
```
